# Optimizing a Trainium2 kernel written in Bass

```python
import math
import jax
import jax.numpy as jnp
from jax import lax
import numpy as np

D_MODEL = 1024
BATCH = 8
SEQ = 2048
DEPTH = 2

CTX_LEN = 256
GRID_W = 64
N_BRANCH = 4
HEAD_DIM = 64
W_A = 512
CONV_K = 31
H_B = 8
W_B = H_B * HEAD_DIM
NA_ROWS = 8
NA_COLS = 16
H_C = 8
KV_C = 2
W_C = H_C * HEAD_DIM
W_C_KV = KV_C * HEAD_DIM
H_D = 4
W_D = H_D * 2 * HEAD_DIM
Q_BLOCK = 128
ROPE_THETA = 10000.0
EPS = 1e-6
NEG_INF = -1e30
IN_SIZES = (2 * W_A, W_A, 3 * W_B, W_B, W_C + 2 * W_C_KV, W_C, 3 * W_D, W_D, N_BRANCH * D_MODEL)
IN_W = 2 * W_A + W_A + 3 * W_B + W_B + W_C + 2 * W_C_KV + W_C + 3 * W_D + W_D + N_BRANCH * D_MODEL

kernel_name = 'hybrid_parallel_gated_dit_block'


def rms_norm(x, g):
    xf = x.astype(jnp.float32)
    y = xf * lax.rsqrt(jnp.mean(xf * xf, axis=-1, keepdims=True) + EPS)
    return (y * g.astype(jnp.float32)).astype(x.dtype)


def layer_norm(x, g, b):
    xf = x.astype(jnp.float32)
    mu = jnp.mean(xf, axis=-1, keepdims=True)
    var = jnp.mean(jnp.square(xf - mu), axis=-1, keepdims=True)
    y = (xf - mu) * lax.rsqrt(var + EPS)
    return (y * g.astype(jnp.float32) + b.astype(jnp.float32)).astype(x.dtype)


def grid_positions(t_len):
    t = jnp.arange(t_len)
    return (t // GRID_W).astype(jnp.float32), (t % GRID_W).astype(jnp.float32)


def rope_1d(x, pos):
    dr = x.shape[-1]
    freqs = ROPE_THETA ** (-jnp.arange(0, dr, 2, dtype=jnp.float32) / dr)
    ang = pos[:, None] * freqs[None, :]
    cos = jnp.cos(ang)[:, None, :].astype(x.dtype)
    sin = jnp.sin(ang)[:, None, :].astype(x.dtype)
    x1, x2 = jnp.split(x, 2, axis=-1)
    return jnp.concatenate([x1 * cos - x2 * sin, x2 * cos + x1 * sin], axis=-1)


def rope_2d(x, rows, cols):
    half = x.shape[-1] // 2
    return jnp.concatenate([rope_1d(x[..., :half], rows), rope_1d(x[..., half:], cols)], axis=-1)


def split_cols(p):
    return jnp.split(p, np.cumsum(IN_SIZES)[:-1].tolist(), axis=-1)


def blockwise(fn, q):
    b_, t = q.shape[:2]
    nb = t // Q_BLOCK
    qb = jnp.moveaxis(q.reshape(b_, nb, Q_BLOCK, *q.shape[2:]), 1, 0)
    out = lax.map(fn, qb)
    return jnp.moveaxis(out, 0, 1).reshape(b_, t, *out.shape[3:])


def grouped_attention(q, k, v):
    s = jnp.einsum('bqngd,bsnd->bngqs', q, k).astype(jnp.float32) * (q.shape[-1] ** -0.5)
    p = jax.nn.softmax(s, axis=-1).astype(v.dtype)
    return jnp.einsum('bngqs,bsnd->bqngd', p, v)


def diff_attention(q, k, v, lam):
    s = jnp.einsum('bqhtd,bkhtd->bhtqk', q, k).astype(jnp.float32) * (q.shape[-1] ** -0.5)
    p = jax.nn.softmax(s, axis=-1)
    a = (p[:, :, 0] - lam * p[:, :, 1]).astype(v.dtype)
    return jnp.einsum('bhqk,bkhe->bqhe', a, v)


def conformer_conv(u, conv_w, conv_b, ln_g, ln_b):
    a, g = jnp.split(u, 2, axis=-1)
    h = a * jax.nn.sigmoid(g)
    h = lax.conv_general_dilated(h, conv_w[:, None, :], window_strides=(1,),
                                 padding=[(CONV_K // 2, CONV_K // 2)],
                                 dimension_numbers=('NWC', 'WIO', 'NWC'),
                                 feature_group_count=W_A) + conv_b
    return jax.nn.silu(layer_norm(h, ln_g, ln_b))


def natten(q, k, v, kc, vc, rpb):
    b_, s_len, h_, hd = q.shape
    rows = s_len // GRID_W
    kr = min(NA_ROWS, rows)
    r = jnp.arange(rows)
    r0 = jnp.clip(r - kr // 2, 0, rows - kr)
    row_idx = r0[:, None] + jnp.arange(kr)[None, :]
    cq = jnp.arange(GRID_W)
    c0 = jnp.clip(cq - NA_COLS // 2, 0, GRID_W - NA_COLS)
    col_ok = (cq[None, :] >= c0[:, None]) & (cq[None, :] < c0[:, None] + NA_COLS)
    mask = jnp.broadcast_to(col_ok[:, None, :], (GRID_W, kr, GRID_W)).reshape(GRID_W, kr * GRID_W)
    dr = row_idx - r[:, None] + (NA_ROWS - 1)
    dc = jnp.clip(cq[None, :] - cq[:, None] + (NA_COLS - 1), 0, 2 * NA_COLS - 2)
    bias = rpb[:, dr[:, None, :, None], dc[None, :, None, :]]
    bias = bias.reshape(h_, rows, GRID_W, kr * GRID_W).astype(jnp.float32)
    qg = q.reshape(b_, rows, GRID_W, h_, hd)
    kband = k.reshape(b_, rows, GRID_W, h_, hd)[:, row_idx].reshape(b_, rows, kr * GRID_W, h_, hd)
    vband = v.reshape(b_, rows, GRID_W, h_, hd)[:, row_idx].reshape(b_, rows, kr * GRID_W, h_, hd)
    scale = hd ** -0.5
    s_lat = jnp.einsum('brqhd,brkhd->bhrqk', qg, kband).astype(jnp.float32) * scale + bias
    s_lat = jnp.where(mask, s_lat, NEG_INF)
    s_ctx = jnp.einsum('brqhd,blhd->bhrql', qg, kc).astype(jnp.float32) * scale
    p = jax.nn.softmax(jnp.concatenate([s_lat, s_ctx], axis=-1), axis=-1).astype(v.dtype)
    nk = kr * GRID_W
    o = (jnp.einsum('bhrqk,brkhd->brqhd', p[..., :nk], vband)
         + jnp.einsum('bhrql,blhd->brqhd', p[..., nk:], vc))
    return o.reshape(b_, s_len, h_ * hd)


def merge_branches(ya, yb, yc, yd, logits, b_merge, w_br_a, w_br_b, w_br_c, w_br_d, w_out):
    ga, gb, gc, gd = jnp.split(jax.nn.sigmoid(logits + b_merge), N_BRANCH, axis=-1)
    m = ga * (ya @ w_br_a) + gb * (yb @ w_br_b) + gc * (yc @ w_br_c) + gd * (yd @ w_br_d)
    return m @ w_out


def hybrid_layer(l, x, xc, c, c_ctx, need_ctx, w_ada, b_ada, norm_g, w_in, b_merge,
                 conv_w, conv_b, conv_ln_g, conv_ln_b, na_qn_g, na_kn_g, na_rpb,
                 gqa_qn_g, gqa_kn_g, diff_qn_g, diff_kn_g, lam_q1, lam_k1, lam_q2, lam_k2,
                 diff_subln_g, w_br_a, w_br_b, w_br_c, w_br_d, w_out):
    b_, s_len, _ = x.shape
    l_len = xc.shape[1]
    rows, cols = grid_positions(s_len)
    shift_x, scale_x, gate_x = jnp.split((jax.nn.silu(c) @ w_ada + b_ada)[:, None, :], 3, axis=-1)
    shift_c, scale_c, gate_c = jnp.split(jax.nn.silu(c_ctx) @ w_ada + b_ada, 3, axis=-1)
    h = rms_norm(x, norm_g) * (1.0 + scale_x) + shift_x
    hc = rms_norm(xc, norm_g) * (1.0 + scale_c) + shift_c
    a_in, a_gate, b_qkv, b_gate, c_qkv, c_gate, d_qkv, d_gate, logits = split_cols(h @ w_in)
    ca_in, ca_gate, cb_qkv, cb_gate, cc_qkv, cc_gate, cd_qkv, cd_gate, clogits = split_cols(hc @ w_in)

    y_a = conformer_conv(a_in, conv_w, conv_b, conv_ln_g, conv_ln_b) * jax.nn.silu(a_gate)

    qkv_b = b_qkv.reshape(b_, s_len, 3, H_B, HEAD_DIM)
    cqkv_b = cb_qkv.reshape(b_, l_len, 3, H_B, HEAD_DIM)
    kc_b = rms_norm(cqkv_b[:, :, 1], na_kn_g)
    vc_b = cqkv_b[:, :, 2]
    y_b = natten(rms_norm(qkv_b[:, :, 0], na_qn_g), rms_norm(qkv_b[:, :, 1], na_kn_g), qkv_b[:, :, 2],
                 kc_b, vc_b, na_rpb) * jax.nn.silu(b_gate)

    q_c = rope_2d(rms_norm(c_qkv[..., :W_C].reshape(b_, s_len, H_C, HEAD_DIM), gqa_qn_g), rows, cols)
    k_c = rope_2d(rms_norm(c_qkv[..., W_C:W_C + W_C_KV].reshape(b_, s_len, KV_C, HEAD_DIM), gqa_kn_g), rows, cols)
    v_c = c_qkv[..., W_C + W_C_KV:].reshape(b_, s_len, KV_C, HEAD_DIM)
    kc_c = rms_norm(cc_qkv[..., W_C:W_C + W_C_KV].reshape(b_, l_len, KV_C, HEAD_DIM), gqa_kn_g)
    vc_c = cc_qkv[..., W_C + W_C_KV:].reshape(b_, l_len, KV_C, HEAD_DIM)
    k_all_c = jnp.concatenate([k_c, kc_c], axis=1)
    v_all_c = jnp.concatenate([v_c, vc_c], axis=1)
    q_c = q_c.reshape(b_, s_len, KV_C, H_C // KV_C, HEAD_DIM)
    y_c = blockwise(lambda qi: grouped_attention(qi, k_all_c, v_all_c), q_c).reshape(b_, s_len, W_C)
    y_c = y_c * jax.nn.silu(c_gate)

    lam_init = 0.8 - 0.6 * math.exp(-0.3 * l)
    lam = (jnp.exp(jnp.sum(lam_q1.astype(jnp.float32) * lam_k1.astype(jnp.float32)))
           - jnp.exp(jnp.sum(lam_q2.astype(jnp.float32) * lam_k2.astype(jnp.float32))) + lam_init)
    qkv_d = d_qkv.reshape(b_, s_len, 3, H_D, 2, HEAD_DIM)
    q_d = rope_2d(rms_norm(qkv_d[:, :, 0], diff_qn_g).reshape(b_, s_len, 2 * H_D, HEAD_DIM), rows, cols)
    k_d = rope_2d(rms_norm(qkv_d[:, :, 1], diff_kn_g).reshape(b_, s_len, 2 * H_D, HEAD_DIM), rows, cols)
    q_d = q_d.reshape(b_, s_len, H_D, 2, HEAD_DIM)
    k_d = k_d.reshape(b_, s_len, H_D, 2, HEAD_DIM)
    v_d = qkv_d[:, :, 2].reshape(b_, s_len, H_D, 2 * HEAD_DIM)
    cqkv_d = cd_qkv.reshape(b_, l_len, 3, H_D, 2, HEAD_DIM)
    kc_d = rms_norm(cqkv_d[:, :, 1], diff_kn_g)
    vc_d = cqkv_d[:, :, 2].reshape(b_, l_len, H_D, 2 * HEAD_DIM)
    k_all_d = jnp.concatenate([k_d, kc_d], axis=1)
    v_all_d = jnp.concatenate([v_d, vc_d], axis=1)
    o_d = blockwise(lambda qi: diff_attention(qi, k_all_d, v_all_d, lam), q_d)
    y_d = (rms_norm(o_d, diff_subln_g) * (1.0 - lam_init)).reshape(b_, s_len, W_D) * jax.nn.silu(d_gate)

    x_out = x + gate_x * merge_branches(y_a, y_b, y_c, y_d, logits, b_merge,
                                        w_br_a, w_br_b, w_br_c, w_br_d, w_out)

    if need_ctx:
        yc_a = conformer_conv(ca_in, conv_w, conv_b, conv_ln_g, conv_ln_b) * jax.nn.silu(ca_gate)
        qc_b = rms_norm(cqkv_b[:, :, 0], na_qn_g)
        yc_b = grouped_attention(qc_b[:, :, :, None, :], kc_b, vc_b).reshape(b_, l_len, W_B)
        yc_b = yc_b * jax.nn.silu(cb_gate)
        qc_c = rms_norm(cc_qkv[..., :W_C].reshape(b_, l_len, H_C, HEAD_DIM), gqa_qn_g)
        qc_c = qc_c.reshape(b_, l_len, KV_C, H_C // KV_C, HEAD_DIM)
        yc_c = grouped_attention(qc_c, kc_c, vc_c).reshape(b_, l_len, W_C) * jax.nn.silu(cc_gate)
        qc_d = rms_norm(cqkv_d[:, :, 0], diff_qn_g)
        oc_d = diff_attention(qc_d, kc_d, vc_d, lam)
        yc_d = (rms_norm(oc_d, diff_subln_g) * (1.0 - lam_init)).reshape(b_, l_len, W_D) * jax.nn.silu(cd_gate)
        xc = xc + gate_c * merge_branches(yc_a, yc_b, yc_c, yc_d, clogits, b_merge,
                                          w_br_a, w_br_b, w_br_c, w_br_d, w_out)
    return x_out, xc


def setup_inputs(seed: int = 0) -> dict:
    key = jax.random.key(seed)
    ks = iter(jax.random.split(key, 40))

    def nrm(shape, s):
        return jax.random.normal(next(ks), shape, jnp.float32) * s

    def gain(shape):
        return 1.0 + nrm(shape, 0.02)

    L = DEPTH
    return {
        'x': nrm((BATCH, SEQ, D_MODEL), 1.0),
        'c': nrm((BATCH, D_MODEL), 1.0),
        'ctx': nrm((BATCH, CTX_LEN, D_MODEL), 1.0),
        'c_ctx': nrm((D_MODEL,), 1.0),
        'w_ada': nrm((L, D_MODEL, 3 * D_MODEL), D_MODEL ** -0.5),
        'b_ada': nrm((L, 3 * D_MODEL), 0.02),
        'norm_g': gain((L, D_MODEL)),
        'w_in': nrm((L, D_MODEL, IN_W), D_MODEL ** -0.5),
        'b_merge': nrm((L, N_BRANCH * D_MODEL), 0.02),
        'conv_w': nrm((L, CONV_K, W_A), CONV_K ** -0.5),
        'conv_b': nrm((L, W_A), 0.02),
        'conv_ln_g': gain((L, W_A)),
        'conv_ln_b': nrm((L, W_A), 0.02),
        'na_qn_g': gain((L, HEAD_DIM)),
        'na_kn_g': gain((L, HEAD_DIM)),
        'na_rpb': nrm((L, H_B, 2 * NA_ROWS - 1, 2 * NA_COLS - 1), 0.1),
        'gqa_qn_g': gain((L, HEAD_DIM)),
        'gqa_kn_g': gain((L, HEAD_DIM)),
        'diff_qn_g': gain((L, HEAD_DIM)),
        'diff_kn_g': gain((L, HEAD_DIM)),
        'lam_q1': nrm((L, HEAD_DIM), 0.1),
        'lam_k1': nrm((L, HEAD_DIM), 0.1),
        'lam_q2': nrm((L, HEAD_DIM), 0.1),
        'lam_k2': nrm((L, HEAD_DIM), 0.1),
        'diff_subln_g': gain((L, 2 * HEAD_DIM)),
        'w_br_a': nrm((L, W_A, D_MODEL), W_A ** -0.5),
        'w_br_b': nrm((L, W_B, D_MODEL), W_B ** -0.5),
        'w_br_c': nrm((L, W_C, D_MODEL), W_C ** -0.5),
        'w_br_d': nrm((L, W_D, D_MODEL), W_D ** -0.5),
        'w_out': nrm((L, D_MODEL, D_MODEL), D_MODEL ** -0.5),
    }


def reference(x, c, ctx, c_ctx, w_ada, b_ada, norm_g, w_in, b_merge, conv_w, conv_b, conv_ln_g,
              conv_ln_b, na_qn_g, na_kn_g, na_rpb, gqa_qn_g, gqa_kn_g, diff_qn_g, diff_kn_g,
              lam_q1, lam_k1, lam_q2, lam_k2, diff_subln_g, w_br_a, w_br_b, w_br_c, w_br_d, w_out):
    xc = ctx
    for l in range(DEPTH):
        x, xc = hybrid_layer(l, x, xc, c, c_ctx, l < DEPTH - 1, w_ada[l], b_ada[l], norm_g[l], w_in[l],
                             b_merge[l], conv_w[l], conv_b[l], conv_ln_g[l], conv_ln_b[l], na_qn_g[l],
                             na_kn_g[l], na_rpb[l], gqa_qn_g[l], gqa_kn_g[l], diff_qn_g[l], diff_kn_g[l],
                             lam_q1[l], lam_k1[l], lam_q2[l], lam_k2[l], diff_subln_g[l], w_br_a[l],
                             w_br_b[l], w_br_c[l], w_br_d[l], w_out[l])
    return x
```

```python
import math
import numpy as np
from contextlib import ExitStack
import concourse.bass as bass
import concourse.mybir as mybir
from concourse.bass_utils import run_bass_kernel_spmd

F32 = mybir.dt.float32
BF16 = mybir.dt.bfloat16
AF = mybir.ActivationFunctionType
ALU = mybir.AluOpType

DM = 1024
S = 2048
LC = 256
T = S + LC
NL = 2
INW = 11008
EPS = 1e-6
NEGM = -30000.0
NVL = 455
NJ = 26
TILES = [(0, 512), (512, 512), (1024, 512), (1536, 512), (2048, 256)]

V_G, V_BSH, V_BSC, V_BM, V_CB, V_LG, V_LB, V_CW = 0, 8, 16, 24, 56, 60, 64, 68
V_NAQ, V_NAK, V_GQ, V_GK, V_DQ, V_DK, V_SUB = 192, 193, 194, 195, 196, 197, 198
V_L = 199


class Buf:
    __slots__ = ("w", "r", "name")

    def __init__(self, name=""):
        self.w = None
        self.r = {}
        self.name = name


class DSem:
    def __init__(self, h):
        self.h = h
        self.count = 0


class Eng:
    def __init__(self, tr, name):
        self.tr = tr
        self.name = name
        self.items = []
        self.seen = {}
        self.sems = []
        self.count = 0
        self.newsem()

    def newsem(self):
        h = self.tr.es.enter_context(self.tr.nc.semaphore(f"s_{self.name}{len(self.sems)}"))
        self.sems.append(h)
        self.count = 0


class Tracer:
    def __init__(self, nc, es):
        self.nc = nc
        self.es = es
        self.pe = Eng(self, "pe")
        self.act = Eng(self, "act")
        self.dve = Eng(self, "dve")
        self.pool = Eng(self, "pool")
        self.sp = Eng(self, "sp")
        self.engs = [self.pe, self.act, self.dve, self.pool, self.sp]
        self.dsems = []

    def dsem(self, name):
        d = DSem(self.es.enter_context(self.nc.semaphore("d_" + name)))
        self.dsems.append(d)
        return d

    def _deps(self, eng, reads, writes):
        need = {}

        def add(tok):
            if tok is None:
                return
            sem, val, src = tok
            if src is eng and eng.name == "pe":
                return
            k = id(sem)
            if k not in need or need[k][1] < val:
                need[k] = (sem, val)

        for b in reads:
            add(b.w)
        for b in writes:
            add(b.w)
            for t in b.r.values():
                add(t)
        for k, (sem, val) in need.items():
            if eng.seen.get(k, 0) < val:
                eng.items.append(("wait", sem, val))
                eng.seen[k] = val

    @staticmethod
    def _commit(tok, reads, writes):
        for b in writes:
            b.w = tok
            b.r = {}
        k = id(tok[0])
        for b in reads:
            b.r[k] = tok

    def op(self, eng, fn, reads=(), writes=()):
        self._deps(eng, reads, writes)
        if eng.count >= 16000:
            eng.newsem()
        eng.count += 1
        tok = (eng.sems[-1], eng.count, eng)
        eng.items.append(("ins", fn, eng.sems[-1], 1))
        self._commit(tok, reads, writes)
        return tok

    def group(self, eng, fns, reads=(), writes=()):
        self._deps(eng, reads, writes)
        if eng.count >= 16000:
            eng.newsem()
        for f in fns[:-1]:
            eng.items.append(("ins", f, None, 0))
        eng.count += 1
        tok = (eng.sems[-1], eng.count, eng)
        eng.items.append(("ins", fns[-1], eng.sems[-1], 1))
        self._commit(tok, reads, writes)
        return tok

    def dma(self, eng, dsem, fns, reads=(), writes=()):
        self._deps(eng, reads, writes)
        for f in fns:
            eng.items.append(("ins", f, dsem.h, 16))
            dsem.count += 16
        tok = (dsem.h, dsem.count, None)
        self._commit(tok, reads, writes)
        return tok

    def barrier(self):
        toks = [(e.sems[-1], e.count, e) for e in self.engs if e.count > 0]
        toks += [(d.h, d.count, None) for d in self.dsems if d.count > 0]
        for e in self.engs:
            for sem, val, src in toks:
                if src is e:
                    continue
                k = id(sem)
                if e.seen.get(k, 0) < val:
                    e.items.append(("wait", sem, val))
                    e.seen[k] = val

    @staticmethod
    def replay(eng, h):
        for it in eng.items:
            if it[0] == "wait":
                h.wait_ge(it[1], it[2])
            else:
                ins = it[1](h)
                if it[2] is not None:
                    ins.then_inc(it[2], it[3])


def MM(out, lhsT, rhs, start=True, stop=True):
    return lambda h: h.matmul(out, lhsT=lhsT, rhs=rhs, start=start, stop=stop)


def TRN(out, in_, ident):
    return lambda h: h.transpose(out, in_, ident)


def ACTF(out, in_, func, **kw):
    return lambda h: h.activation(out=out, in_=in_, func=func, **kw)


def TT(out, in0, in1, op):
    return lambda h: h.tensor_tensor(out=out, in0=in0, in1=in1, op=op)


def TS(out, in0, s1, s2=None, op0=ALU.mult, op1=None):
    if op1 is None:
        return lambda h: h.tensor_scalar(out=out, in0=in0, scalar1=s1, scalar2=None, op0=op0)
    return lambda h: h.tensor_scalar(out=out, in0=in0, scalar1=s1, scalar2=s2, op0=op0, op1=op1)


def STT(out, in0, scalar, in1, op0, op1):
    return lambda h: h.scalar_tensor_tensor(out=out, in0=in0, scalar=scalar, in1=in1, op0=op0, op1=op1)


def CP(out, in_):
    return lambda h: h.tensor_copy(out=out, in_=in_)


def RCP(out, in_):
    return lambda h: h.reciprocal(out=out, in_=in_)


def MSET(ap, v):
    return lambda h: h.memset(ap, v)


def DMA(out, in_):
    return lambda h: h.dma_start(out=out, in_=in_)


class Ring:
    def __init__(self, aps, name):
        self.aps = aps
        self.bufs = [Buf(f"{name}{i}") for i in range(len(aps))]
        self.i = 0

    def next(self):
        k = self.i % len(self.aps)
        self.i += 1
        return self.aps[k], self.bufs[k]


def _host_consts():
    identf = np.eye(128, dtype=np.float32)
    p = np.arange(128)
    hd = p % 64
    half = hd // 32
    fi = (hd % 32) % 16
    freq = (10000.0 ** (-(2.0 * fi) / 32.0)).astype(np.float32)
    t = np.arange(S)
    rows = (t // 64).astype(np.float32)
    cols = (t % 64).astype(np.float32)
    pos = np.where(half[:, None] == 0, rows[None, :], cols[None, :]).astype(np.float32)
    ang = (pos * freq[:, None]).astype(np.float32)
    cos = np.cos(ang).astype(np.float32)
    sgn = np.where((hd % 32) < 16, -1.0, 1.0).astype(np.float32)
    sin = (np.sin(ang).astype(np.float32) * sgn[:, None]).astype(np.float32)
    selR = np.zeros((128, 64), np.float32)

    def delta(hf, jj):
        return (4 - jj) + hf if jj < 10 else (7 - (jj - 10)) + hf

    for hf in range(2):
        for jj in range(NJ):
            dr = delta(hf, jj) + 7
            if 0 <= dr <= 14:
                selR[dr, hf * NJ + jj] = 8.0
    cf = np.concatenate([identf, cos, sin, selR], axis=1)

    ident = identf
    blk = ((p[:, None] // 64) == (p[None, :] // 64)).astype(np.float32)
    partner = np.where((p % 32) < 16, p + 16, p - 16)
    perm = np.zeros((128, 128), np.float32)
    perm[partner, p] = 1.0
    band = np.zeros((128, 128), np.float32)
    for c in range(31):
        band[c, c + 48] = 1.0
    mask = np.zeros((128, 2, NJ, 64), np.float32)
    qc = np.arange(64)
    c0 = np.clip(qc - 8, 0, 48)
    for kc in range(64):
        colok = (kc >= c0) & (kc < c0 + 16)
        for hf in range(2):
            for jj in range(NJ):
                d = delta(hf, jj)
                ok = (-4 <= d <= 3) if jj < 10 else (-7 <= d <= 7)
                mask[kc, hf, jj, :] = np.where(colok & ok, 0.0, NEGM)
    cb = np.concatenate([ident, blk, perm, band, np.ones((128, 128), np.float32),
                         mask.reshape(128, -1)], axis=1)
    return np.ascontiguousarray(cf), np.ascontiguousarray(cb)


CF_ID, CF_COS, CF_SIN, CF_SEL, CF_N = 0, 128, 128 + 2048, 128 + 4096, 128 + 4096 + 64
CB_ID, CB_BLK, CB_PERM, CB_BAND, CB_ONES, CB_MASK, CB_N = 0, 128, 256, 384, 512, 640, 640 + 2 * NJ * 64


def build_program(debug=False):
    nc = bass.Bass("TRN2", target_bir_lowering=False)
    dx = nc.dram_tensor("x", [S, DM], F32, kind="ExternalInput").ap()
    dctx = nc.dram_tensor("ctx", [LC, DM], F32, kind="ExternalInput").ap()
    dcT = nc.dram_tensor("cT", [128, 16], F32, kind="ExternalInput").ap()
    dwada = nc.dram_tensor("w_ada", [NL, DM, 3 * DM], F32, kind="ExternalInput").ap()
    dwin = nc.dram_tensor("w_in", [NL, DM, INW], F32, kind="ExternalInput").ap()
    dwbr = nc.dram_tensor("w_br", [NL, 4, 512, DM], F32, kind="ExternalInput").ap()
    dwout = nc.dram_tensor("w_out", [NL, DM, DM], F32, kind="ExternalInput").ap()
    dvecs = nc.dram_tensor("vecs", [128, NL * NVL], F32, kind="ExternalInput").ap()
    dbg_rows = nc.dram_tensor("bgrow", [1, NL * DM], F32, kind="ExternalInput").ap()
    drpb = nc.dram_tensor("rpb", [NL, 8, 15, 31], F32, kind="ExternalInput").ap()
    dcf = nc.dram_tensor("cf", [128, CF_N], F32, kind="ExternalInput").ap()
    dcb = nc.dram_tensor("cb", [128, CB_N], F32, kind="ExternalInput").ap()
    dout = nc.dram_tensor("out", [S, DM], F32, kind="ExternalOutput").ap()
    dx1 = nc.dram_tensor("x1s", [T, DM], F32, kind="ExternalOutput" if debug else "Internal").ap()
    ddbg = None
    if debug:
        ddbg = nc.dram_tensor("dbg", [8, 4, 128, T], F32, kind="ExternalOutput").ap()
        ddbgm = nc.dram_tensor("dbgm", [8, 128, T], F32, kind="ExternalOutput").ap()
        ddbgg = nc.dram_tensor("dbgg", [128, 2048], F32, kind="ExternalOutput").ap()

    es = ExitStack()
    with es:
        tr = Tracer(nc, es)
        pe, act, dve, pool, sp = tr.pe, tr.act, tr.dve, tr.pool, tr.sp

        def sb(name, shape, dt):
            return es.enter_context(nc.sbuf_tensor("sb_" + name, shape, dt))

        hT = sb("hT", [128, 8, T], BF16)
        mT = sb("mT", [128, 8, T], BF16)
        yT = sb("yT", [128, 4, T], BF16)
        cfs = sb("cfs", [128, CF_N], F32)
        cbs = sb("cbs", [128, CB_N], BF16)
        vecs = sb("vecs", [128, NL * NVL], F32)
        bgrow = sb("bgrow", [1, NL * DM], BF16)
        modv = sb("modv", [128, 4, 8], F32)
        s2 = sb("s2", [128, 8, 2], BF16)
        small = sb("small", [128, 64], F32)
        wring_t = [sb(f"wr{i}", [128, 4096], BF16) for i in range(3)]
        ARENA_N = 29000
        arena = sb("arena", [128, ARENA_N], BF16)
        psum = [es.enter_context(nc.psum_tensor(f"ps{i}", [128, 512], F32)) for i in range(8)]
        pb = [Buf(f"ps{i}") for i in range(8)]

        hTB = [Buf(f"hT{t}") for t in range(5)]
        mTB = [[Buf(f"mT{c}_{t}") for t in range(5)] for c in range(8)]
        yTB = [[Buf(f"yT{c}_{t}") for t in range(5)] for c in range(4)]
        constB = Buf("const")
        vecB = Buf("vecs")
        modB = Buf("modv")
        s2B = Buf("s2")
        smallB = Buf("small")
        x1B = [Buf(f"x1_{i}") for i in range(18)]
        outB = Buf("out")

        identf = cfs[:, CF_ID:CF_ID + 128]
        COS = cfs[:, CF_COS:CF_COS + S]
        SIN = cfs[:, CF_SIN:CF_SIN + S]
        selR = cfs[:, CF_SEL:CF_SEL + 64]
        ident = cbs[:, CB_ID:CB_ID + 128]
        blk = cbs[:, CB_BLK:CB_BLK + 128]
        perm = cbs[:, CB_PERM:CB_PERM + 128]
        band = cbs[:, CB_BAND:CB_BAND + 128]
        ones = cbs[:, CB_ONES:CB_ONES + 128]
        maskT = cbs[:, CB_MASK:CB_MASK + 2 * NJ * 64].rearrange("p (t j q) -> p t j q", t=2, j=NJ)

        class Arena:
            def __init__(self):
                self.off = 0

            def reset(self):
                tr.barrier()
                self.off = 0

            def bf(self, n):
                ap = arena[:, self.off:self.off + n]
                self.off += n
                assert self.off <= ARENA_N, self.off
                return ap

            def f32(self, n):
                ap = arena[:, self.off:self.off + 2 * n].bitcast(F32)
                self.off += 2 * n
                assert self.off <= ARENA_N, self.off
                return ap

        ar = Arena()

        wsl = [Buf(f"w{i}") for i in range(3)]
        wds = [tr.dsem(f"w{i}") for i in range(3)]
        pieces = []

        def w3(slot, kc, n):
            return slot[:, 0:kc * n].rearrange("p (k n) -> p k n", k=kc)

        def piece_cols(tag, src2d, cols):
            specs = []
            for (do, c0, n) in cols:
                specs.append((lambda sl, do=do, n=n: w3(sl, 8, 512)[:, :, do:do + n],
                              src2d[:, c0:c0 + n].rearrange("(k p) n -> p k n", p=128)))
            pieces.append((tag, specs))

        def layer_pieces(l):
            win = dwin[l]
            for pi in range(4):
                piece_cols(f"ada{l}_{pi}", dwada[l], [(0, pi * 512, 512)])
            for j in range(4):
                piece_cols(f"A{l}_{j}", win, [(0, j * 128, 128), (128, 512 + j * 128, 128), (256, 1024 + j * 128, 128)])
            merge_pieces(l, 0)
            for hp in range(4):
                piece_cols(f"B{l}_{hp}", win, [(0, 1536 + hp * 128, 128), (128, 2048 + hp * 128, 128),
                                               (256, 2560 + hp * 128, 128), (384, 3072 + hp * 128, 128)])
            merge_pieces(l, 1)
            for cp in range(4):
                n = cp // 2
                piece_cols(f"C{l}_{cp}", win, [(0, 3584 + cp * 128, 128), (128, 4096 + n * 64, 64), (192, 4096 + n * 64, 64),
                                               (256, 4224 + n * 64, 64), (384, 4352 + cp * 128, 128)])
            merge_pieces(l, 2)
            for hd in range(4):
                piece_cols(f"D{l}_{hd}", win, [(0, 4864 + hd * 128, 128), (128, 5376 + hd * 128, 128),
                                               (256, 5888 + hd * 128, 128), (384, 6400 + hd * 128, 128)])
            merge_pieces(l, 3)
            for pi in range(2):
                piece_cols(f"adag{l}_{pi}", dwada[l], [(0, 2048 + pi * 512, 512)])
            for ph in range(2):
                piece_cols(f"wo{l}_{ph}", dwout[l], [(0, ph * 512, 512)])

        def merge_pieces(l, i):
            for hf in range(2):
                pieces.append((f"br{l}_{i}_{hf}", [(lambda sl: w3(sl, 4, 1024),
                                                    dwbr[l, i].rearrange("(k p) n -> p k n", p=128))]))
                piece_cols(f"lg{l}_{i}_{hf}", dwin[l], [(0, 6912 + i * 1024 + hf * 512, 512)])

        for l in range(NL):
            layer_pieces(l)
        wstate = {"issued": 0, "next": 0}

        def w_issue(upto):
            while wstate["issued"] < min(upto, len(pieces)):
                i = wstate["issued"]
                tag, specs = pieces[i]
                sl = wring_t[i % 3]
                tr.dma(pool, wds[i % 3], [DMA(f(sl), src) for (f, src) in specs], writes=[wsl[i % 3]])
                wstate["issued"] += 1

        def wget(tag):
            i = wstate["next"]
            assert pieces[i][0] == tag, (pieces[i][0], tag)
            w_issue(i + 2)
            wstate["next"] += 1
            return wring_t[i % 3], wsl[i % 3]

        d_init = tr.dsem("init")
        tr.dma(sp, d_init, [DMA(cfs[:, :], dcf), DMA(vecs[:, :], dvecs), DMA(small[:, 0:16], dcT)],
               writes=[constB, vecB, smallB])
        d_init2 = tr.dsem("init2")
        tr.dma(pool, d_init2, [DMA(cbs[:, :], dcb), DMA(bgrow[:, :], dbg_rows)], writes=[constB])
        w_issue(2)

        def vcol(l, off, n=1):
            return vecs[:, l * NVL + off: l * NVL + off + n]

        xld = [tr.dsem(f"xld{i}") for i in range(2)]
        std = [tr.dsem(f"st{i}") for i in range(2)]

        def run_layer(l, need_ctx):
            lam_init = 0.8 - 0.6 * math.exp(-0.3 * l)
            qtiles = TILES if need_ctx else TILES[:4]

            ar.reset()
            tr.op(act, ACTF(s2[:, :, 0], small[:, 0:8], AF.Silu), reads=[smallB], writes=[s2B])
            tr.op(act, ACTF(s2[:, :, 1], small[:, 8:16], AF.Silu), reads=[smallB], writes=[s2B])
            pm = psum[7]
            for pi in range(4):
                wsl_ap, wb = wget(f"ada{l}_{pi}")
                w = w3(wsl_ap, 8, 512)
                for fc in range(4):
                    g = pi * 4 + fc
                    tr.group(pe, [MM(pm[:, g * 2:g * 2 + 2], w[:, kc, fc * 128:(fc + 1) * 128], s2[:, kc, :],
                                     start=(kc == 0), stop=(kc == 7)) for kc in range(8)],
                             reads=[wb, s2B], writes=[pb[7]])
            pmv = pm[:, 0:32].rearrange("p (f w) -> p f w", w=2)
            tmp8 = small[:, 16:24]
            for which in range(2):
                tr.op(dve, TT(modv[:, 2 * which + 1, :], pmv[:, 0:8, which], vcol(l, V_BSH, 8), ALU.add),
                      reads=[pb[7], vecB], writes=[modB])
                tr.op(dve, TT(tmp8, pmv[:, 8:16, which], vcol(l, V_BSC, 8), ALU.add),
                      reads=[pb[7], vecB], writes=[smallB])
                tr.op(dve, STT(modv[:, 2 * which, :], tmp8, 1.0, vcol(l, V_G, 8), ALU.add, ALU.mult),
                      reads=[smallB, vecB], writes=[modB])
            lt = small[:, 24:28]
            prod = ar.f32(64)
            prodB = Buf("prod")
            for k in range(2):
                tr.op(dve, TT(prod, vcol(l, V_L + 128 * k, 64), vcol(l, V_L + 128 * k + 64, 64), ALU.mult),
                      reads=[vecB], writes=[prodB])
                tr.op(dve, MSET(lt[:, k:k + 1], 0.0), writes=[smallB])
                tr.op(act, ACTF(prod, prod, AF.Identity, accum_out=lt[:, k:k + 1]), reads=[prodB, smallB], writes=[prodB, smallB])
                tr.op(act, ACTF(lt[:, k:k + 1], lt[:, k:k + 1], AF.Exp), reads=[smallB], writes=[smallB])
            neglam = small[:, 28:29]
            gsub = small[:, 29:30]
            tr.op(dve, TT(lt[:, 2:3], lt[:, 0:1], lt[:, 1:2], ALU.subtract), reads=[smallB], writes=[smallB])
            tr.op(dve, TS(neglam, lt[:, 2:3], lam_init, -1.0, ALU.add, ALU.mult), reads=[smallB], writes=[smallB])
            tr.op(dve, TS(gsub, vcol(l, V_SUB), 1.0 - lam_init), reads=[vecB], writes=[smallB])

            ar.reset()
            xt_r = Ring([ar.f32(1024) for _ in range(2)], "xt")
            xn_r = Ring([ar.f32(1024) for _ in range(2)], "xn")
            junk = ar.bf(1024)
            junkB = Buf("junk")
            st_r = Ring([small[:, 32 + 4 * i: 36 + 4 * i] for i in range(2)], "st")
            for i in range(18):
                xt, xtB = xt_r.next()
                xn, xnB = xn_r.next()
                stt_, stB = st_r.next()
                if l == 0:
                    src = dx[i * 128:(i + 1) * 128, :] if i < 16 else dctx[(i - 16) * 128:(i - 15) * 128, :]
                    rd = []
                else:
                    src = dx1[i * 128:(i + 1) * 128, :]
                    rd = [x1B[i]]
                tr.dma(sp, xld[i % 2], [DMA(xt, src)], reads=rd, writes=[xtB])
                tr.op(dve, MSET(stt_[:, 0:1], 0.0), writes=[stB])
                tr.op(act, ACTF(junk, xt, AF.Square, accum_out=stt_[:, 0:1]), reads=[xtB, stB], writes=[junkB, stB])
                tr.op(act, ACTF(stt_[:, 1:2], stt_[:, 0:1], AF.Sqrt, scale=1.0 / DM, bias=EPS), reads=[stB], writes=[stB])
                tr.op(dve, RCP(stt_[:, 2:3], stt_[:, 1:2]), reads=[stB], writes=[stB])
                tr.op(dve, TS(xn, xt, stt_[:, 2:3]), reads=[xtB, stB], writes=[xnB])
                which = 0 if i < 16 else 1
                for hb in range(2):
                    bk = (2 * i + hb) % 8
                    tr.group(pe, [TRN(psum[bk][:, k4 * 128:(k4 + 1) * 128], xn[:, (hb * 4 + k4) * 128:(hb * 4 + k4 + 1) * 128], identf)
                                  for k4 in range(4)], reads=[xnB, constB], writes=[pb[bk]])
                    for k4 in range(4):
                        kc = hb * 4 + k4
                        dst = hT[:, kc, i * 128:(i + 1) * 128]
                        srcp = psum[bk][:, k4 * 128:(k4 + 1) * 128]
                        A = modv[:, 2 * which, kc:kc + 1]
                        Bc = modv[:, 2 * which + 1, kc:kc + 1]
                        if k4 % 2 == 0:
                            tr.op(dve, TS(dst, srcp, A, Bc, ALU.mult, ALU.add), reads=[pb[bk], modB], writes=[hTB[i // 4]])
                        else:
                            tr.op(act, ACTF(dst, srcp, AF.Identity, scale=A, bias=Bc), reads=[pb[bk], modB], writes=[hTB[i // 4]])

            def proj_fm(ps_i, w, col0, t0, n, wb, t5):
                tr.group(pe, [MM(psum[ps_i][:, 0:n], w[:, kc, col0:col0 + 128], hT[:, kc, t0:t0 + n],
                                 start=(kc == 0), stop=(kc == 7)) for kc in range(8)],
                         reads=[wb, hTB[t5]], writes=[pb[ps_i]])

            def merge_branch(i):
                ar.reset()
                g_r = Ring([ar.f32(512) for _ in range(2)], "G")
                t_r = Ring([ar.f32(512) for _ in range(2)], "mt")
                cnt = 0
                for hf in range(2):
                    wbr_ap, wbrB = wget(f"br{l}_{i}_{hf}")
                    wbr = w3(wbr_ap, 4, 1024)
                    wl_ap, wlB = wget(f"lg{l}_{i}_{hf}")
                    wl = w3(wl_ap, 8, 512)
                    for fcl in range(4):
                        fc = hf * 4 + fcl
                        for t5, (t0, n) in enumerate(qtiles):
                            pz = (2 * cnt) % 8
                            pl = (2 * cnt + 1) % 8
                            cnt += 1
                            tr.group(pe, [MM(psum[pz][:, 0:n], wbr[:, kc, fc * 128:(fc + 1) * 128], yT[:, kc, t0:t0 + n],
                                             start=(kc == 0), stop=(kc == 3)) for kc in range(4)],
                                     reads=[wbrB] + [yTB[kc][t5] for kc in range(4)], writes=[pb[pz]])
                            proj_fm(pl, wl, fcl * 128, t0, n, wlB, t5)
                            G, GB = g_r.next()
                            tr.op(act, ACTF(G[:, 0:n], psum[pl][:, 0:n], AF.Sigmoid, bias=vcol(l, V_BM + i * 8 + fc)),
                                  reads=[pb[pl], vecB], writes=[GB])
                            if i == 0:
                                tr.op(dve, TT(mT[:, fc, t0:t0 + n], psum[pz][:, 0:n], G[:, 0:n], ALU.mult),
                                      reads=[pb[pz], GB], writes=[mTB[fc][t5]])
                            else:
                                tm, tmB = t_r.next()
                                tr.op(dve, TT(tm[:, 0:n], psum[pz][:, 0:n], G[:, 0:n], ALU.mult), reads=[pb[pz], GB], writes=[tmB])
                                tr.op(dve, TT(mT[:, fc, t0:t0 + n], mT[:, fc, t0:t0 + n], tm[:, 0:n], ALU.add),
                                      reads=[tmB], writes=[mTB[fc][t5]])

            def dump(i):
                if debug:
                    dd = tr.dsem(f"dbg{l}_{i}")
                    tr.dma(pool, dd, [DMA(ddbg[l * 4 + i].rearrange("c p t -> p c t"), yT[:, :, :])],
                           reads=[yTB[c][t] for c in range(4) for t in range(5)], writes=[outB])

            ar.reset()
            HG = 2364
            hglu_r = Ring([ar.bf(HG) for _ in range(2)], "hglu")
            diag_ap = ar.bf(31 * 128).rearrange("p (k n) -> p k n", k=31)
            diagB = Buf("diag")
            sg_r = Ring([ar.f32(512) for _ in range(2)], "sg")
            cbuf = mT[:, :, :].rearrange("p c t -> p (c t)").bitcast(F32).rearrange("p (c t) -> p c t", c=4)

            def cB(j):
                return [mTB[2 * j][t] for t in range(5)] + [mTB[2 * j + 1][t] for t in range(5)]
            segs = [(0, 0, 512), (512, 512, 512), (1024, 1024, 512), (1536, 1536, 512), (2078, 2048, 256)]
            segs = segs if need_ctx else segs[:4]
            for j in range(4):
                w_ap, wb = wget(f"A{l}_{j}")
                w = w3(w_ap, 8, 512)
                for k in range(31):
                    tr.op(pool, TS(diag_ap[:, k, :], ident, vcol(l, V_CW + j * 31 + k)), reads=[constB, vecB], writes=[diagB])
                hg, hgB = hglu_r.next()
                for (a, b) in ((0, 15), (2063, 2093), (2349, 2364)):
                    tr.op(pool, MSET(hg[:, a:b], 0.0), writes=[hgB])
                for t5, (bb, t0, n) in enumerate(segs):
                    proj_fm(0 + (t5 % 2) * 2, w, 0, t0, n, wb, t5)
                    proj_fm(1 + (t5 % 2) * 2, w, 128, t0, n, wb, t5)
                    pa, pg = (t5 % 2) * 2, 1 + (t5 % 2) * 2
                    sg, sgB = sg_r.next()
                    tr.op(act, ACTF(sg[:, 0:n], psum[pg][:, 0:n], AF.Sigmoid), reads=[pb[pg]], writes=[sgB])
                    tr.op(dve, TT(hg[:, bb + 15:bb + 15 + n], psum[pa][:, 0:n], sg[:, 0:n], ALU.mult),
                          reads=[pb[pa], sgB], writes=[hgB])
                for t5, (bb, t0, n) in enumerate(segs):
                    pc = 4 + (t5 % 2)
                    tr.group(pe, [MM(psum[pc][:, 0:n], diag_ap[:, k, :], hg[:, bb + k:bb + k + n], start=(k == 0), stop=(k == 30))
                                  for k in range(31)], reads=[diagB, hgB], writes=[pb[pc]])
                    tr.op(act, ACTF(cbuf[:, j, t0:t0 + n], psum[pc][:, 0:n], AF.Identity, bias=vcol(l, V_CB + j)),
                          reads=[pb[pc], vecB], writes=cB(j))
                    pgt = 6 + (t5 % 2)
                    proj_fm(pgt, w, 256, t0, n, wb, t5)
                    tr.op(act, ACTF(yT[:, j, t0:t0 + n], psum[pgt][:, 0:n], AF.Silu), reads=[pb[pgt]], writes=[yTB[j][t5]])
            ar.reset()
            onesf = ar.f32(128)
            onesfB = Buf("onesf")
            tr.op(dve, MSET(onesf, 1.0), writes=[onesfB])
            sq_r = Ring([ar.f32(512) for _ in range(2)], "csq")
            mean = ar.f32(512)
            rstd = ar.f32(512)
            msq = ar.f32(512)
            stB2 = Buf("lnstat")
            d_r = Ring([ar.f32(512) for _ in range(2)], "lnd")
            allc = [b for j in range(4) for b in cB(j)]
            for t5, (t0, n) in enumerate(qtiles):
                tr.group(pe, [MM(psum[0][:, 0:n], onesf, cbuf[:, j, t0:t0 + n], start=(j == 0), stop=(j == 3)) for j in range(4)],
                         reads=allc + [onesfB], writes=[pb[0]])
                sqs = []
                for j in range(4):
                    sq, sqB = sq_r.next()
                    tr.op(act, ACTF(sq[:, 0:n], cbuf[:, j, t0:t0 + n], AF.Square), reads=allc, writes=[sqB])
                    tr.group(pe, [MM(psum[1][:, 0:n], onesf, sq[:, 0:n], start=(j == 0), stop=(j == 3))],
                             reads=[sqB, onesfB], writes=[pb[1]])
                tr.op(dve, TS(mean[:, 0:n], psum[0][:, 0:n], 1.0 / 512), reads=[pb[0]], writes=[stB2])
                tr.op(dve, TT(msq[:, 0:n], mean[:, 0:n], mean[:, 0:n], ALU.mult), reads=[stB2], writes=[stB2])
                tr.op(dve, STT(msq[:, 0:n], psum[1][:, 0:n], 1.0 / 512, msq[:, 0:n], ALU.mult, ALU.subtract),
                      reads=[pb[1], stB2], writes=[stB2])
                tr.op(act, ACTF(msq[:, 0:n], msq[:, 0:n], AF.Sqrt, bias=EPS, scale=1.0), reads=[stB2], writes=[stB2])
                tr.op(dve, RCP(rstd[:, 0:n], msq[:, 0:n]), reads=[stB2], writes=[stB2])
                for j in range(4):
                    d, dB = d_r.next()
                    tr.op(dve, TT(d[:, 0:n], cbuf[:, j, t0:t0 + n], mean[:, 0:n], ALU.subtract), reads=allc + [stB2], writes=[dB])
                    tr.op(dve, TT(d[:, 0:n], d[:, 0:n], rstd[:, 0:n], ALU.mult), reads=[stB2], writes=[dB])
                    tr.op(act, ACTF(d[:, 0:n], d[:, 0:n], AF.Silu, scale=vcol(l, V_LG + j), bias=vcol(l, V_LB + j)),
                          reads=[vecB], writes=[dB])
                    tr.op(dve, TT(yT[:, j, t0:t0 + n], yT[:, j, t0:t0 + n], d[:, 0:n], ALU.mult), reads=[dB], writes=[yTB[j][t5]])
            dump(0)
            merge_branch(0)

            def attn_branch(kind):
                ar.reset()
                QT = ar.bf(T)
                KT = ar.bf(T)
                GT = ar.bf(T)
                Vp = ar.bf(18 * 256).rearrange("p (k n) -> p k n", k=18)
                QTB = [Buf(f"QT{t}") for t in range(5)]
                KTB = [Buf(f"KT{t}") for t in range(5)]
                GTB = [Buf(f"GT{t}") for t in range(5)]
                VB = [Buf(f"V{g}") for g in range(5)]
                sq_r = Ring([ar.bf(512) for _ in range(2)], "sq")
                rs_r = Ring([ar.f32(512) for _ in range(2)], "rs")
                xn_r2 = Ring([ar.bf(512) for _ in range(2)], "xnb")
                t1_r = Ring([ar.f32(512) for _ in range(1)], "t1")
                t2_r = Ring([ar.f32(512) for _ in range(1)], "t2")
                P_r = Ring([ar.bf(512) for _ in range(4)], "P")
                ya_r = Ring([ar.f32(512) for _ in range(1)], "ya")
                yb_r = Ring([ar.f32(512) for _ in range(1)], "yb")
                if kind == "B":
                    tab = ar.bf(2 * NJ * 64).rearrange("p (h n) -> p h n", h=2)
                    tab64 = ar.bf(2 * NJ * 64).rearrange("p (t j q) -> p t j q", t=2, j=NJ)
                    rt2 = ar.bf(64)
                    rp = ar.f32(32)
                    tabB = [Buf("tab0"), Buf("tab1")]
                    tab64B, rt2B, rpB = Buf("tab64"), Buf("rt2"), Buf("rp")
                    rpd = tr.dsem(f"rp{l}")
                if kind != "D":
                    tr.op(pool, MSET(Vp[:, :, 64:128], 1.0), writes=VB)
                    tr.op(pool, MSET(Vp[:, :, 192:256], 1.0), writes=VB)
                rope = kind != "B"
                qg = {"B": V_NAQ, "C": V_GQ, "D": V_DQ}[kind]
                kg = {"B": V_NAK, "C": V_GK, "D": V_DK}[kind]

                def normed(psi, dst, dstB, gcol, t0, n, do_rope):
                    sq, sqB = sq_r.next()
                    tr.op(act, ACTF(sq[:, 0:n], psum[psi][:, 0:n], AF.Square), reads=[pb[psi]], writes=[sqB])
                    tr.group(pe, [MM(psum[5][:, 0:n], blk, sq[:, 0:n])], reads=[sqB, constB], writes=[pb[5]])
                    rs, rsB = rs_r.next()
                    tr.op(act, ACTF(rs[:, 0:n], psum[5][:, 0:n], AF.Sqrt, scale=1.0 / 64, bias=EPS), reads=[pb[5]], writes=[rsB])
                    tr.op(dve, RCP(rs[:, 0:n], rs[:, 0:n]), reads=[rsB], writes=[rsB])
                    if not do_rope:
                        tr.op(dve, STT(dst, psum[psi][:, 0:n], vcol(l, gcol), rs[:, 0:n], ALU.mult, ALU.mult),
                              reads=[pb[psi], rsB, vecB], writes=[dstB])
                        return
                    xb, xbB = xn_r2.next()
                    tr.op(dve, STT(xb[:, 0:n], psum[psi][:, 0:n], vcol(l, gcol), rs[:, 0:n], ALU.mult, ALU.mult),
                          reads=[pb[psi], rsB, vecB], writes=[xbB])
                    tr.group(pe, [MM(psum[6][:, 0:n], perm, xb[:, 0:n])], reads=[xbB, constB], writes=[pb[6]])
                    t1, t1B = t1_r.next()
                    t2, t2B = t2_r.next()
                    tr.op(dve, TT(t1[:, 0:n], xb[:, 0:n], COS[:, t0:t0 + n], ALU.mult), reads=[xbB, constB], writes=[t1B])
                    tr.op(dve, TT(t2[:, 0:n], psum[6][:, 0:n], SIN[:, t0:t0 + n], ALU.mult), reads=[pb[6], constB], writes=[t2B])
                    tr.op(dve, TT(dst, t1[:, 0:n], t2[:, 0:n], ALU.add), reads=[t1B, t2B], writes=[dstB])

                for pr in range(4):
                    w_ap, wb = wget(f"{kind}{l}_{pr}")
                    w = w3(w_ap, 8, 512)
                    if kind == "B":
                        for h2 in range(2):
                            h = 2 * pr + h2
                            tr.dma(sp, rpd, [DMA(rp[0:15, 0:31], drpb[l, h])], writes=[rpB])
                            tr.group(pe, [MM(psum[7][0:31, 0:2 * NJ], rp[0:15, 0:31], selR[0:15, 0:2 * NJ])],
                                     reads=[rpB, constB], writes=[pb[7]])
                            tr.op(dve, CP(rt2[0:31, 0:2 * NJ], psum[7][0:31, 0:2 * NJ]), reads=[pb[7]], writes=[rt2B])
                            for g in range(8):
                                bk = g % 4
                                tr.group(pe, [MM(psum[bk][0:64, q8 * 2 * NJ:(q8 + 1) * 2 * NJ],
                                                 band[0:31, 63 - (8 * g + q8):127 - (8 * g + q8)], rt2[0:31, 0:2 * NJ])
                                              for q8 in range(8)], reads=[rt2B, constB], writes=[pb[bk]])
                                pv = psum[bk][0:64, 0:8 * 2 * NJ].rearrange("p (q t j) -> p t j q", q=8, t=2)
                                for hf in range(2):
                                    tr.op(dve, TT(tab64[0:64, hf, :, 8 * g:8 * g + 8], pv[:, hf], maskT[0:64, hf, :, 8 * g:8 * g + 8], ALU.add),
                                          reads=[pb[bk], constB], writes=[tab64B])
                            tr.op(dve, CP(tab[0:64, h2, :], tab64[0:64, 0].rearrange("p j q -> p (j q)")), reads=[tab64B], writes=[tabB[h2]])
                            tr.op(dve, CP(tab[64:128, h2, :], tab64[0:64, 1].rearrange("p j q -> p (j q)")), reads=[tab64B], writes=[tabB[h2]])
                    cnt = 0
                    for t5, (t0, n) in enumerate(qtiles):
                        psi = cnt % 4
                        cnt += 1
                        proj_fm(psi, w, 0, t0, n, wb, t5)
                        normed(psi, QT[:, t0:t0 + n], QTB[t5], qg, t0, n, rope and t5 < 4)
                    for t5, (t0, n) in enumerate(TILES):
                        psi = cnt % 4
                        cnt += 1
                        proj_fm(psi, w, 128, t0, n, wb, t5)
                        normed(psi, KT[:, t0:t0 + n], KTB[t5], kg, t0, n, rope and t5 < 4)
                    for t5, (t0, n) in enumerate(qtiles):
                        psi = cnt % 4
                        cnt += 1
                        proj_fm(psi, w, 384, t0, n, wb, t5)
                        tr.op(act, ACTF(GT[:, t0:t0 + n], psum[psi][:, 0:n], AF.Silu), reads=[pb[psi]], writes=[GTB[t5]])
                    nv = 64 if kind == "C" else 128
                    for g5 in range(5):
                        kts = list(range(4 * g5, min(4 * g5 + 4, 18)))
                        psi = cnt % 4
                        cnt += 1
                        fns = []
                        for ii, kt in enumerate(kts):
                            for kc in range(8):
                                fns.append(MM(psum[psi][:, ii * 128:ii * 128 + nv], hT[:, kc, kt * 128:(kt + 1) * 128],
                                              w[:, kc, 256:256 + nv], start=(kc == 0), stop=(kc == 7)))
                        tr.group(pe, fns, reads=[wb, hTB[g5]], writes=[pb[psi]])
                        nk = len(kts)
                        pvv = psum[psi][:, 0:nk * 128].rearrange("p (k n) -> p k n", k=nk)
                        k0 = kts[0]
                        if kind == "B":
                            tr.op(act, ACTF(Vp[:, k0:k0 + nk, 0:64], pvv[:, :, 0:64], AF.Copy), reads=[pb[psi]], writes=[VB[g5]])
                            tr.op(dve, CP(Vp[:, k0:k0 + nk, 128:192], pvv[:, :, 64:128]), reads=[pb[psi]], writes=[VB[g5]])
                        elif kind == "C":
                            tr.op(act, ACTF(Vp[:, k0:k0 + nk, 0:64], pvv[:, :, 0:64], AF.Copy), reads=[pb[psi]], writes=[VB[g5]])
                        else:
                            tr.op(act, ACTF(Vp[:, k0:k0 + nk, 0:128], pvv[:, :, 0:128], AF.Copy), reads=[pb[psi]], writes=[VB[g5]])

                    def vlhs(kt, hf):
                        if kind == "B":
                            return Vp[:, kt, hf * 128:(hf + 1) * 128]
                        if kind == "C":
                            return Vp[:, kt, 0:128]
                        return Vp[:, kt, 0:128]

                    qts = []
                    if kind == "B":
                        def add_tile(r_lo, nr, mode):
                            ch = [(16, 0, nr * 64, None), (17, 0, nr * 64, None)]
                            if mode == "edge":
                                a0 = 0 if r_lo == 0 else 12
                                for a in range(a0, a0 + 4):
                                    j0 = 7 - 2 * a + r_lo
                                    ch.append((a, 0, nr * 64, (10 + j0) * 64))
                            else:
                                for a in range(16):
                                    rs_ = max(r_lo, 2 * a - 3)
                                    re_ = min(r_lo + nr - 1, 2 * a + 5)
                                    if rs_ <= re_:
                                        j0 = 4 - 2 * a + rs_
                                        ch.append((a, (rs_ - r_lo) * 64, (re_ - r_lo + 1) * 64, j0 * 64))
                            qts.append((r_lo * 64, nr * 64, ch))
                        add_tile(0, 4, "edge")
                        add_tile(4, 8, "mid")
                        add_tile(12, 8, "mid")
                        add_tile(20, 8, "mid")
                        add_tile(28, 4, "edge")
                    else:
                        for (t0, n) in TILES[:4]:
                            qts.append((t0, n, [(kt, 0, n, None) for kt in range(18)]))
                    if need_ctx:
                        qts.append((2048, 256, [(16, 0, 256, None), (17, 0, 256, None)]))

                    def t5_of(q0, qn):
                        return sorted(set([q0 // 512, (q0 + qn - 1) // 512])) if q0 < 2048 else [4]

                    acc_i = [0]
                    steps = []
                    for (q0, qn, ch) in qts:
                        t5s = t5_of(q0, qn)
                        if kind == "D":
                            accs = [[0, 1], [2, 3]]
                        else:
                            base = (acc_i[0] % 2) * 2
                            acc_i[0] += 1
                            accs = [[base], [base + 1]]
                        for hf in range(2):
                            rows = slice(hf * 64, (hf + 1) * 64)
                            nch = len(ch)
                            for ci, (kt, qa, qb, bcol) in enumerate(ch):
                                first, last = ci == 0, ci == nch - 1
                                steps.append(dict(q0=q0, qa=qa, qb=qb, kt=kt, hf=hf, bcol=bcol, first=first, last=last,
                                                  accs=accs[hf], rows=rows, t5s=t5s))
                        steps.append(dict(epi=True, q0=q0, qn=qn, accs=accs, t5s=t5s))

                    sring = [4, 5, 6]
                    scount = [0]

                    def do_qk(st):
                        si = sring[scount[0] % 3]
                        scount[0] += 1
                        st["si"] = si
                        q0, qa, qb, kt, rows = st["q0"], st["qa"], st["qb"], st["kt"], st["rows"]
                        fns = [MM(psum[si][:, qa:qb], KT[rows, kt * 128:(kt + 1) * 128], QT[rows, q0 + qa:q0 + qb],
                                  start=True, stop=(st["bcol"] is None))]
                        rd = [KTB[kt // 4]] + [QTB[t] for t in st["t5s"]]
                        if st["bcol"] is not None:
                            fns.append(MM(psum[si][:, qa:qb], ident, tab[:, st["hf"], st["bcol"]:st["bcol"] + (qb - qa)],
                                          start=False, stop=True))
                            rd += [tabB[st["hf"]], constB]
                        tr.group(pe, fns, reads=rd, writes=[pb[si]])
                        P, PB = P_r.next()
                        st["P"], st["PB"] = P, PB
                        tr.op(act, ACTF(P[:, qa:qb], psum[si][:, qa:qb], AF.Exp, scale=0.125), reads=[pb[si]], writes=[PB])

                    def do_pv(st):
                        qa, qb, kt, hf = st["qa"], st["qb"], st["kt"], st["hf"]
                        P, PB = st["P"], st["PB"]
                        o = st["accs"][0]
                        tr.group(pe, [MM(psum[o][:, qa:qb], vlhs(kt, hf), P[:, qa:qb], start=st["first"], stop=st["last"])],
                                 reads=[PB, VB[kt // 4]], writes=[pb[o]])
                        if kind == "D":
                            dn = st["accs"][1]
                            tr.group(pe, [MM(psum[dn][:, qa:qb], ones, P[:, qa:qb], start=st["first"], stop=st["last"])],
                                     reads=[PB, constB], writes=[pb[dn]])

                    def do_epi(st):
                        q0, qn, accs, t5s = st["q0"], st["qn"], st["accs"], st["t5s"]
                        ywr = [yTB[pr][t] for t in t5s]
                        grd = [GTB[t] for t in t5s]
                        if kind != "D":
                            ya, yaB = ya_r.next()
                            yb, ybB = yb_r.next()
                            for hf in range(2):
                                o = accs[hf][0]
                                tr.op(dve, RCP(ya[0:64, 0:qn], psum[o][64:128, 0:qn]), reads=[pb[o]], writes=[yaB])
                                tr.op(dve, TT(yb[hf * 64:(hf + 1) * 64, 0:qn], psum[o][0:64, 0:qn], ya[0:64, 0:qn], ALU.mult),
                                      reads=[pb[o], yaB], writes=[ybB])
                            tr.op(dve, TT(yT[:, pr, q0:q0 + qn], yb[:, 0:qn], GT[:, q0:q0 + qn], ALU.mult),
                                  reads=[ybB] + grd, writes=ywr)
                        else:
                            ya, yaB = ya_r.next()
                            yb, ybB = yb_r.next()
                            (o1, d1), (o2, d2) = accs
                            tr.op(dve, RCP(ya[:, 0:qn], psum[d1][:, 0:qn]), reads=[pb[d1]], writes=[yaB])
                            tr.op(dve, TT(ya[:, 0:qn], psum[o1][:, 0:qn], ya[:, 0:qn], ALU.mult), reads=[pb[o1]], writes=[yaB])
                            tr.op(dve, RCP(yb[:, 0:qn], psum[d2][:, 0:qn]), reads=[pb[d2]], writes=[ybB])
                            tr.op(dve, TT(yb[:, 0:qn], psum[o2][:, 0:qn], yb[:, 0:qn], ALU.mult), reads=[pb[o2]], writes=[ybB])
                            tr.op(dve, STT(ya[:, 0:qn], yb[:, 0:qn], neglam, ya[:, 0:qn], ALU.mult, ALU.add),
                                  reads=[ybB, smallB], writes=[yaB])
                            sq, sqB = sq_r.next()
                            tr.op(act, ACTF(sq[:, 0:qn], ya[:, 0:qn], AF.Square), reads=[yaB], writes=[sqB])
                            tr.group(pe, [MM(psum[7][:, 0:qn], ones, sq[:, 0:qn])], reads=[sqB, constB], writes=[pb[7]])
                            tr.op(act, ACTF(yb[:, 0:qn], psum[7][:, 0:qn], AF.Sqrt, scale=1.0 / 128, bias=EPS), reads=[pb[7]], writes=[ybB])
                            tr.op(dve, RCP(yb[:, 0:qn], yb[:, 0:qn]), reads=[], writes=[ybB])
                            tr.op(dve, TT(ya[:, 0:qn], ya[:, 0:qn], yb[:, 0:qn], ALU.mult), reads=[ybB], writes=[yaB])
                            tr.op(dve, STT(yT[:, pr, q0:q0 + qn], ya[:, 0:qn], gsub, GT[:, q0:q0 + qn], ALU.mult, ALU.mult),
                                  reads=[yaB, smallB] + grd, writes=ywr)

                    LOOK = 2
                    qk_list = [s_ for s_ in steps if "epi" not in s_]
                    qi = 0
                    done = 0
                    for s_ in steps:
                        if "epi" in s_:
                            do_epi(s_)
                            continue
                        while qi < len(qk_list) and qi <= done + LOOK:
                            do_qk(qk_list[qi])
                            qi += 1
                        do_pv(s_)
                        done += 1

            attn_branch("B")
            dump(1)
            merge_branch(1)
            attn_branch("C")
            dump(2)
            merge_branch(2)
            attn_branch("D")
            dump(3)
            merge_branch(3)

            ar.reset()
            if debug and l == 0:
                ddm = tr.dsem("dbgm")
                tr.dma(pool, ddm, [DMA(ddbgm.rearrange("c p t -> p c t"), mT[:, :, :])],
                       reads=[mTB[c][t] for c in range(8) for t in range(5)], writes=[outB])
            screp = ar.bf(8 * 128).rearrange("p (k n) -> p k n", k=8)
            sccrep = ar.bf(8 * 128).rearrange("p (k n) -> p k n", k=8)
            repB = Buf("rep")
            for kc in range(8):
                tr.op(dve, CP(screp[:, kc, :], s2[:, kc, 0:1].to_broadcast([128, 128])), reads=[s2B], writes=[repB])
                tr.op(dve, CP(sccrep[:, kc, :], s2[:, kc, 1:2].to_broadcast([128, 128])), reads=[s2B], writes=[repB])
            gx = ar.f32(1024)
            gc = ar.f32(1024)
            gB = Buf("gates")
            for pi in range(2):
                w_ap, wb = wget(f"adag{l}_{pi}")
                w = w3(w_ap, 8, 512)
                for which, (rep, dst) in enumerate(((screp, gx), (sccrep, gc))):
                    if which == 1 and not need_ctx:
                        continue
                    psi = pi * 2 + which
                    fns = [MM(psum[psi][:, :], rep[:, kc, :], w[:, kc, :], start=(kc == 0), stop=False) for kc in range(8)]
                    fns.append(MM(psum[psi][:, :], ones[0:1, :], bgrow[0:1, l * DM + pi * 512:l * DM + (pi + 1) * 512], start=False, stop=True))
                    tr.group(pe, fns, reads=[wb, repB, constB], writes=[pb[psi]])
                    tr.op(act, ACTF(dst[:, pi * 512:(pi + 1) * 512], psum[psi][:, :], AF.Copy), reads=[pb[psi]], writes=[gB])
            if debug and l == 0:
                ddg = tr.dsem("dbgg")
                tr.dma(sp, ddg, [DMA(ddbgg[:, 0:1024], gx), DMA(ddbgg[:, 1024:2048], gc)], reads=[gB], writes=[outB])
            xo_r = Ring([ar.f32(512) for _ in range(2)], "xo")
            res_r = Ring([ar.f32(512) for _ in range(2)], "res")
            tm_r = Ring([ar.f32(512) for _ in range(2)], "otm")
            ntile = 18 if need_ctx else 16
            cnt = 0
            for ph in range(2):
                w_ap, wb = wget(f"wo{l}_{ph}")
                w = w3(w_ap, 8, 512)
                for i in range(ntile):
                    psi = 4 + cnt % 4
                    t5 = min(i // 4, 4)
                    tr.group(pe, [MM(psum[psi][:, :], mT[:, kc, i * 128:(i + 1) * 128], w[:, kc, :], start=(kc == 0), stop=(kc == 7))
                                  for kc in range(8)], reads=[wb] + [mTB[kc][t5] for kc in range(8)], writes=[pb[psi]])
                    xo, xoB = xo_r.next()
                    if l == 0:
                        src = dx[i * 128:(i + 1) * 128, ph * 512:(ph + 1) * 512] if i < 16 else \
                            dctx[(i - 16) * 128:(i - 15) * 128, ph * 512:(ph + 1) * 512]
                        rd = []
                    else:
                        src = dx1[i * 128:(i + 1) * 128, ph * 512:(ph + 1) * 512]
                        rd = [x1B[i]]
                    tr.dma(sp, xld[cnt % 2], [DMA(xo, src)], reads=rd, writes=[xoB])
                    gate = gx if i < 16 else gc
                    tm, tmB = tm_r.next()
                    rs_, rsB_ = res_r.next()
                    tr.op(dve, TT(tm, psum[psi][:, :], gate[:, ph * 512:(ph + 1) * 512], ALU.mult), reads=[pb[psi], gB], writes=[tmB])
                    tr.op(dve, TT(rs_, tm, xo, ALU.add), reads=[tmB, xoB], writes=[rsB_])
                    if l == NL - 1:
                        tr.dma(sp, std[cnt % 2], [DMA(dout[i * 128:(i + 1) * 128, ph * 512:(ph + 1) * 512], rs_)],
                               reads=[rsB_], writes=[outB])
                    else:
                        tr.dma(sp, std[cnt % 2], [DMA(dx1[i * 128:(i + 1) * 128, ph * 512:(ph + 1) * 512], rs_)],
                               reads=[rsB_], writes=[x1B[i]])
                    cnt += 1

        for l in range(NL):
            run_layer(l, l < NL - 1)
        tr.barrier()

        block = es.enter_context(nc.Block())

        @block.tensor
        def _(h):
            Tracer.replay(pe, h)

        @block.scalar
        def _(h):
            Tracer.replay(act, h)

        @block.vector
        def _(h):
            Tracer.replay(dve, h)

        @block.gpsimd
        def _(h):
            Tracer.replay(pool, h)

        @block.sync
        def _(h):
            Tracer.replay(sp, h)
    return nc


_CACHE = {}


def _prep_inputs(inputs):
    f = lambda a: np.ascontiguousarray(np.asarray(a, dtype=np.float32))
    x, c, ctx, c_ctx = f(inputs["x"]), f(inputs["c"]), f(inputs["ctx"]), f(inputs["c_ctx"])
    cf, cb = _host_consts()
    w_br = np.ascontiguousarray(np.stack([f(inputs["w_br_a"]), f(inputs["w_br_b"]), f(inputs["w_br_c"]), f(inputs["w_br_d"])], axis=1))
    vecs = np.zeros((128, NL * NVL), np.float32)

    def fm(v, n):
        return np.ascontiguousarray(v.reshape(n, 128).T)

    for l in range(NL):
        o = l * NVL
        vecs[:, o + V_G:o + V_G + 8] = fm(f(inputs["norm_g"])[l], 8)
        b_ada = f(inputs["b_ada"])[l]
        vecs[:, o + V_BSH:o + V_BSH + 8] = fm(b_ada[0:1024], 8)
        vecs[:, o + V_BSC:o + V_BSC + 8] = fm(b_ada[1024:2048], 8)
        vecs[:, o + V_BM:o + V_BM + 32] = fm(f(inputs["b_merge"])[l], 32)
        vecs[:, o + V_CB:o + V_CB + 4] = fm(f(inputs["conv_b"])[l], 4)
        vecs[:, o + V_LG:o + V_LG + 4] = fm(f(inputs["conv_ln_g"])[l], 4)
        vecs[:, o + V_LB:o + V_LB + 4] = fm(f(inputs["conv_ln_b"])[l], 4)
        cw = f(inputs["conv_w"])[l]
        vecs[:, o + V_CW:o + V_CW + 124] = cw.T.reshape(4, 128, 31).transpose(1, 0, 2).reshape(128, 124)
        for nm, off in (("na_qn_g", V_NAQ), ("na_kn_g", V_NAK), ("gqa_qn_g", V_GQ), ("gqa_kn_g", V_GK),
                        ("diff_qn_g", V_DQ), ("diff_kn_g", V_DK)):
            vecs[:, o + off] = np.tile(f(inputs[nm])[l], 2)
        vecs[:, o + V_SUB] = f(inputs["diff_subln_g"])[l]
        for k, nm in enumerate(("lam_q1", "lam_k1", "lam_q2", "lam_k2")):
            vecs[:, o + V_L + 64 * k:o + V_L + 64 * (k + 1)] = f(inputs[nm])[l][None, :]
    bgrow = np.ascontiguousarray(f(inputs["b_ada"])[:, 2048:3072].reshape(1, NL * DM))
    shared = {"w_ada": f(inputs["w_ada"]), "w_in": f(inputs["w_in"]), "w_br": w_br, "w_out": f(inputs["w_out"]),
              "vecs": vecs, "bgrow": bgrow, "rpb": f(inputs["na_rpb"]), "cf": cf, "cb": cb}
    maps = []
    for b in range(8):
        cT = np.concatenate([fm(c[b], 8), fm(c_ctx, 8)], axis=1)
        m = dict(shared)
        m.update({"x": x[b], "ctx": ctx[b], "cT": np.ascontiguousarray(cT)})
        maps.append(m)
    return maps


def kernel(**inputs):
    if "nc" not in _CACHE:
        _CACHE["nc"] = build_program(False)
    maps = _prep_inputs(inputs)
    res = run_bass_kernel_spmd(_CACHE["nc"], maps, core_ids=list(range(8)))
    return np.stack([np.asarray(r["out"], dtype=np.float32) for r in res.results], axis=0)
```

```python
import math
import numpy as np
from contextlib import ExitStack
import concourse.bass as bass
import concourse.mybir as mybir
from concourse.bass_utils import run_bass_kernel_spmd

F32 = mybir.dt.float32
BF16 = mybir.dt.bfloat16
AF = mybir.ActivationFunctionType
ALU = mybir.AluOpType

DM = 1024
S = 2048
LC = 256
T = S + LC
NL = 2
INW = 11008
EPS = 1e-6
NEGM = -30000.0
NVL = 455
NJ = 26
TILES = [(0, 512), (512, 512), (1024, 512), (1536, 512), (2048, 256)]

V_G, V_BSH, V_BSC, V_BM, V_CB, V_LG, V_LB, V_CW = 0, 8, 16, 24, 56, 60, 64, 68
V_NAQ, V_NAK, V_GQ, V_GK, V_DQ, V_DK, V_SUB = 192, 193, 194, 195, 196, 197, 198
V_L = 199


class Buf:
    __slots__ = ("w", "r", "name")

    def __init__(self, name=""):
        self.w = None
        self.r = {}
        self.name = name


class DSem:
    def __init__(self, h):
        self.h = h
        self.count = 0


class Eng:
    def __init__(self, tr, name):
        self.tr = tr
        self.name = name
        self.items = []
        self.seen = {}
        self.sems = []
        self.count = 0
        self.newsem()

    def newsem(self):
        h = self.tr.es.enter_context(self.tr.nc.semaphore(f"s_{self.name}{len(self.sems)}"))
        self.sems.append(h)
        self.count = 0


class Tracer:
    def __init__(self, nc, es):
        self.nc = nc
        self.es = es
        self.pe = Eng(self, "pe")
        self.act = Eng(self, "act")
        self.dve = Eng(self, "dve")
        self.pool = Eng(self, "pool")
        self.sp = Eng(self, "sp")
        self.engs = [self.pe, self.act, self.dve, self.pool, self.sp]
        self.dsems = []

    def dsem(self, name):
        d = DSem(self.es.enter_context(self.nc.semaphore("d_" + name)))
        self.dsems.append(d)
        return d

    def _deps(self, eng, reads, writes):
        need = {}

        def add(tok):
            if tok is None:
                return
            sem, val, src = tok
            if src is eng and eng.name == "pe":
                return
            k = id(sem)
            if k not in need or need[k][1] < val:
                need[k] = (sem, val)

        for b in reads:
            add(b.w)
        for b in writes:
            add(b.w)
            for t in b.r.values():
                add(t)
        for k, (sem, val) in need.items():
            if eng.seen.get(k, 0) < val:
                eng.items.append(("wait", sem, val))
                eng.seen[k] = val

    @staticmethod
    def _commit(tok, reads, writes):
        for b in writes:
            b.w = tok
            b.r = {}
        k = id(tok[0])
        for b in reads:
            b.r[k] = tok

    def op(self, eng, fn, reads=(), writes=()):
        self._deps(eng, reads, writes)
        if eng.count >= 16000:
            eng.newsem()
        eng.count += 1
        tok = (eng.sems[-1], eng.count, eng)
        eng.items.append(("ins", fn, eng.sems[-1], 1))
        self._commit(tok, reads, writes)
        return tok

    def group(self, eng, fns, reads=(), writes=()):
        self._deps(eng, reads, writes)
        if eng.count >= 16000:
            eng.newsem()
        for f in fns[:-1]:
            eng.items.append(("ins", f, None, 0))
        eng.count += 1
        tok = (eng.sems[-1], eng.count, eng)
        eng.items.append(("ins", fns[-1], eng.sems[-1], 1))
        self._commit(tok, reads, writes)
        return tok

    def dma(self, eng, dsem, fns, reads=(), writes=()):
        self._deps(eng, reads, writes)
        for f in fns:
            eng.items.append(("ins", f, dsem.h, 16))
            dsem.count += 16
        tok = (dsem.h, dsem.count, None)
        self._commit(tok, reads, writes)
        return tok

    def barrier(self):
        toks = [(e.sems[-1], e.count, e) for e in self.engs if e.count > 0]
        toks += [(d.h, d.count, None) for d in self.dsems if d.count > 0]
        for e in self.engs:
            for sem, val, src in toks:
                if src is e:
                    continue
                k = id(sem)
                if e.seen.get(k, 0) < val:
                    e.items.append(("wait", sem, val))
                    e.seen[k] = val

    @staticmethod
    def replay(eng, h):
        for it in eng.items:
            if it[0] == "wait":
                h.wait_ge(it[1], it[2])
            else:
                ins = it[1](h)
                if it[2] is not None:
                    ins.then_inc(it[2], it[3])


def MM(out, lhsT, rhs, start=True, stop=True):
    return lambda h: h.matmul(out, lhsT=lhsT, rhs=rhs, start=start, stop=stop)


def TRN(out, in_, ident):
    return lambda h: h.transpose(out, in_, ident)


def ACTF(out, in_, func, **kw):
    return lambda h: h.activation(out=out, in_=in_, func=func, **kw)


def TT(out, in0, in1, op):
    return lambda h: h.tensor_tensor(out=out, in0=in0, in1=in1, op=op)


def TS(out, in0, s1, s2=None, op0=ALU.mult, op1=None):
    if op1 is None:
        return lambda h: h.tensor_scalar(out=out, in0=in0, scalar1=s1, scalar2=None, op0=op0)
    return lambda h: h.tensor_scalar(out=out, in0=in0, scalar1=s1, scalar2=s2, op0=op0, op1=op1)


def STT(out, in0, scalar, in1, op0, op1):
    return lambda h: h.scalar_tensor_tensor(out=out, in0=in0, scalar=scalar, in1=in1, op0=op0, op1=op1)


def CP(out, in_):
    return lambda h: h.tensor_copy(out=out, in_=in_)


def RCP(out, in_):
    return lambda h: h.reciprocal(out=out, in_=in_)


def MSET(ap, v):
    return lambda h: h.memset(ap, v)


def DMA(out, in_):
    return lambda h: h.dma_start(out=out, in_=in_)


class Ring:
    def __init__(self, aps, name):
        self.aps = aps
        self.bufs = [Buf(f"{name}{i}") for i in range(len(aps))]
        self.i = 0

    def next(self):
        k = self.i % len(self.aps)
        self.i += 1
        return self.aps[k], self.bufs[k]


def _host_consts():
    identf = np.eye(128, dtype=np.float32)
    p = np.arange(128)
    hd = p % 64
    half = hd // 32
    fi = (hd % 32) % 16
    freq = (10000.0 ** (-(2.0 * fi) / 32.0)).astype(np.float32)
    t = np.arange(S)
    rows = (t // 64).astype(np.float32)
    cols = (t % 64).astype(np.float32)
    pos = np.where(half[:, None] == 0, rows[None, :], cols[None, :]).astype(np.float32)
    ang = (pos * freq[:, None]).astype(np.float32)
    cos = np.cos(ang).astype(np.float32)
    sgn = np.where((hd % 32) < 16, -1.0, 1.0).astype(np.float32)
    sin = (np.sin(ang).astype(np.float32) * sgn[:, None]).astype(np.float32)
    selR = np.zeros((128, 64), np.float32)

    def delta(hf, jj):
        return (4 - jj) + hf if jj < 10 else (7 - (jj - 10)) + hf

    for hf in range(2):
        for jj in range(NJ):
            dr = delta(hf, jj) + 7
            if 0 <= dr <= 14:
                selR[dr, hf * NJ + jj] = 8.0
    cf = np.concatenate([identf, cos, sin, selR], axis=1)

    ident = identf
    blk = ((p[:, None] // 64) == (p[None, :] // 64)).astype(np.float32)
    partner = np.where((p % 32) < 16, p + 16, p - 16)
    perm = np.zeros((128, 128), np.float32)
    perm[partner, p] = 1.0
    band = np.zeros((128, 128), np.float32)
    for c in range(31):
        band[c, c + 48] = 1.0
    mask = np.zeros((128, 2, NJ, 64), np.float32)
    qc = np.arange(64)
    c0 = np.clip(qc - 8, 0, 48)
    for kc in range(64):
        colok = (kc >= c0) & (kc < c0 + 16)
        for hf in range(2):
            for jj in range(NJ):
                d = delta(hf, jj)
                ok = (-4 <= d <= 3) if jj < 10 else (-7 <= d <= 7)
                mask[kc, hf, jj, :] = np.where(colok & ok, 0.0, NEGM)
    cb = np.concatenate([ident, blk, perm, band, np.ones((128, 128), np.float32),
                         mask.reshape(128, -1)], axis=1)
    return np.ascontiguousarray(cf), np.ascontiguousarray(cb)


CF_ID, CF_COS, CF_SIN, CF_SEL, CF_N = 0, 128, 128 + 2048, 128 + 4096, 128 + 4096 + 64
CB_ID, CB_BLK, CB_PERM, CB_BAND, CB_ONES, CB_MASK, CB_N = 0, 128, 256, 384, 512, 640, 640 + 2 * NJ * 64


def build_program(debug=False):
    nc = bass.Bass("TRN2", target_bir_lowering=False)
    dx = nc.dram_tensor("x", [S, DM], F32, kind="ExternalInput").ap()
    dctx = nc.dram_tensor("ctx", [LC, DM], F32, kind="ExternalInput").ap()
    dcT = nc.dram_tensor("cT", [128, 16], F32, kind="ExternalInput").ap()
    dwada = nc.dram_tensor("w_ada", [NL, DM, 3 * DM], F32, kind="ExternalInput").ap()
    dwin = nc.dram_tensor("w_in", [NL, DM, INW], F32, kind="ExternalInput").ap()
    dwbr = nc.dram_tensor("w_br", [NL, 4, 512, DM], F32, kind="ExternalInput").ap()
    dwout = nc.dram_tensor("w_out", [NL, DM, DM], F32, kind="ExternalInput").ap()
    dvecs = nc.dram_tensor("vecs", [128, NL * NVL], F32, kind="ExternalInput").ap()
    dbg_rows = nc.dram_tensor("bgrow", [1, NL * DM], F32, kind="ExternalInput").ap()
    drpb = nc.dram_tensor("rpb", [NL, 8, 15, 31], F32, kind="ExternalInput").ap()
    dcf = nc.dram_tensor("cf", [128, CF_N], F32, kind="ExternalInput").ap()
    dcb = nc.dram_tensor("cb", [128, CB_N], F32, kind="ExternalInput").ap()
    dout = nc.dram_tensor("out", [S, DM], F32, kind="ExternalOutput").ap()
    dx1 = nc.dram_tensor("x1s", [T, DM], F32, kind="ExternalOutput" if debug else "Internal").ap()
    ddbg = None
    if debug:
        ddbg = nc.dram_tensor("dbg", [8, 4, 128, T], F32, kind="ExternalOutput").ap()
        ddbgm = nc.dram_tensor("dbgm", [8, 128, T], F32, kind="ExternalOutput").ap()
        ddbgg = nc.dram_tensor("dbgg", [128, 2048], F32, kind="ExternalOutput").ap()

    es = ExitStack()
    with es:
        tr = Tracer(nc, es)
        pe, act, dve, pool, sp = tr.pe, tr.act, tr.dve, tr.pool, tr.sp

        def sb(name, shape, dt):
            return es.enter_context(nc.sbuf_tensor("sb_" + name, shape, dt))

        hT = sb("hT", [128, 8, T], BF16)
        mT = sb("mT", [128, 8, T], BF16)
        yT = sb("yT", [128, 4, T], BF16)
        cfs = sb("cfs", [128, CF_N], F32)
        cbs = sb("cbs", [128, CB_N], BF16)
        vecs = sb("vecs", [128, NL * NVL], F32)
        bgrow = sb("bgrow", [1, DM], BF16)
        bgB = Buf("bgrow")
        bgd = tr.dsem("bgrow")
        modv = sb("modv", [128, 4, 8], F32)
        s2 = sb("s2", [128, 8, 2], BF16)
        small = sb("small", [128, 64], F32)
        wring_t = [sb(f"wr{i}", [128, 4096], BF16) for i in range(3)]
        ARENA_N = 31780
        arena = sb("arena", [128, ARENA_N], BF16)
        psum = [es.enter_context(nc.psum_tensor(f"ps{i}", [128, 512], F32)) for i in range(8)]
        pb = [Buf(f"ps{i}") for i in range(8)]

        hTB = [Buf(f"hT{t}") for t in range(5)]
        mTB = [[Buf(f"mT{c}_{t}") for t in range(5)] for c in range(8)]
        yTB = [[Buf(f"yT{c}_{t}") for t in range(5)] for c in range(4)]
        constB = Buf("const")
        vecB = Buf("vecs")
        modB = Buf("modv")
        s2B = Buf("s2")
        smallB = Buf("small")
        x1B = [Buf(f"x1_{i}") for i in range(18)]
        outB = Buf("out")

        identf = cfs[:, CF_ID:CF_ID + 128]
        COS = cfs[:, CF_COS:CF_COS + S]
        SIN = cfs[:, CF_SIN:CF_SIN + S]
        selR = cfs[:, CF_SEL:CF_SEL + 64]
        ident = cbs[:, CB_ID:CB_ID + 128]
        blk = cbs[:, CB_BLK:CB_BLK + 128]
        perm = cbs[:, CB_PERM:CB_PERM + 128]
        band = cbs[:, CB_BAND:CB_BAND + 128]
        ones = cbs[:, CB_ONES:CB_ONES + 128]
        maskT = cbs[:, CB_MASK:CB_MASK + 2 * NJ * 64].rearrange("p (t j q) -> p t j q", t=2, j=NJ)

        class Arena:
            def __init__(self):
                self.off = 0

            def reset(self):
                tr.barrier()
                self.off = 0

            def bf(self, n):
                ap = arena[:, self.off:self.off + n]
                self.off += n
                assert self.off <= ARENA_N, self.off
                return ap

            def f32(self, n):
                ap = arena[:, self.off:self.off + 2 * n].bitcast(F32)
                self.off += 2 * n
                assert self.off <= ARENA_N, self.off
                return ap

        ar = Arena()

        wsl = [Buf(f"w{i}") for i in range(3)]
        wds = [tr.dsem(f"w{i}") for i in range(3)]
        pieces = []

        def w3(slot, kc, n):
            return slot[:, 0:kc * n].rearrange("p (k n) -> p k n", k=kc)

        def piece_cols(tag, src2d, cols):
            specs = []
            for (do, c0, n) in cols:
                specs.append((lambda sl, do=do, n=n: w3(sl, 8, 512)[:, :, do:do + n],
                              src2d[:, c0:c0 + n].rearrange("(k p) n -> p k n", p=128)))
            pieces.append((tag, specs))

        def layer_pieces(l):
            win = dwin[l]
            for pi in range(4):
                piece_cols(f"ada{l}_{pi}", dwada[l], [(0, pi * 512, 512)])
            for j in range(4):
                piece_cols(f"A{l}_{j}", win, [(0, j * 128, 128), (128, 512 + j * 128, 128), (256, 1024 + j * 128, 128)])
            merge_pieces(l, 0)
            for hp in range(4):
                piece_cols(f"B{l}_{hp}", win, [(0, 1536 + hp * 128, 128), (128, 2048 + hp * 128, 128),
                                               (256, 2560 + hp * 128, 128), (384, 3072 + hp * 128, 128)])
            merge_pieces(l, 1)
            for cp in range(4):
                n = cp // 2
                piece_cols(f"C{l}_{cp}", win, [(0, 3584 + cp * 128, 128), (128, 4096 + n * 64, 64), (192, 4096 + n * 64, 64),
                                               (256, 4224 + n * 64, 64), (384, 4352 + cp * 128, 128)])
            merge_pieces(l, 2)
            for hd in range(4):
                piece_cols(f"D{l}_{hd}", win, [(0, 4864 + hd * 128, 128), (128, 5376 + hd * 128, 128),
                                               (256, 5888 + hd * 128, 128), (384, 6400 + hd * 128, 128)])
            merge_pieces(l, 3)
            for pi in range(2):
                piece_cols(f"adag{l}_{pi}", dwada[l], [(0, 2048 + pi * 512, 512)])
            for ph in range(2):
                piece_cols(f"wo{l}_{ph}", dwout[l], [(0, ph * 512, 512)])

        def merge_pieces(l, i):
            for hf in range(2):
                pieces.append((f"br{l}_{i}_{hf}", [(lambda sl: w3(sl, 4, 1024),
                                                    dwbr[l, i].rearrange("(k p) n -> p k n", p=128))]))
                piece_cols(f"lg{l}_{i}_{hf}", dwin[l], [(0, 6912 + i * 1024 + hf * 512, 512)])

        for l in range(NL):
            layer_pieces(l)
        wstate = {"issued": 0, "next": 0}

        def w_issue(upto):
            while wstate["issued"] < min(upto, len(pieces)):
                i = wstate["issued"]
                tag, specs = pieces[i]
                sl = wring_t[i % 3]
                tr.dma(pool, wds[i % 3], [DMA(f(sl), src) for (f, src) in specs], writes=[wsl[i % 3]])
                wstate["issued"] += 1

        def wget(tag):
            i = wstate["next"]
            assert pieces[i][0] == tag, (pieces[i][0], tag)
            w_issue(i + 2)
            wstate["next"] += 1
            return wring_t[i % 3], wsl[i % 3]

        d_init = tr.dsem("init")
        tr.dma(sp, d_init, [DMA(cfs[:, :], dcf), DMA(vecs[:, :], dvecs), DMA(small[:, 0:16], dcT)],
               writes=[constB, vecB, smallB])
        d_init2 = tr.dsem("init2")
        tr.dma(pool, d_init2, [DMA(cbs[:, :], dcb)], writes=[constB])
        w_issue(2)

        def vcol(l, off, n=1):
            return vecs[:, l * NVL + off: l * NVL + off + n]

        xld = [tr.dsem(f"xld{i}") for i in range(3)]
        std = [tr.dsem(f"st{i}") for i in range(2)]

        def run_layer(l, need_ctx):
            lam_init = 0.8 - 0.6 * math.exp(-0.3 * l)
            qtiles = TILES if need_ctx else TILES[:4]

            ar.reset()
            tr.op(act, ACTF(s2[:, :, 0], small[:, 0:8], AF.Silu), reads=[smallB], writes=[s2B])
            tr.op(act, ACTF(s2[:, :, 1], small[:, 8:16], AF.Silu), reads=[smallB], writes=[s2B])
            pm = psum[7]
            for pi in range(4):
                wsl_ap, wb = wget(f"ada{l}_{pi}")
                w = w3(wsl_ap, 8, 512)
                for fc in range(4):
                    g = pi * 4 + fc
                    tr.group(pe, [MM(pm[:, g * 2:g * 2 + 2], w[:, kc, fc * 128:(fc + 1) * 128], s2[:, kc, :],
                                     start=(kc == 0), stop=(kc == 7)) for kc in range(8)],
                             reads=[wb, s2B], writes=[pb[7]])
            pmv = pm[:, 0:32].rearrange("p (f w) -> p f w", w=2)
            tmp8 = small[:, 16:24]
            for which in range(2):
                tr.op(dve, TT(modv[:, 2 * which + 1, :], pmv[:, 0:8, which], vcol(l, V_BSH, 8), ALU.add),
                      reads=[pb[7], vecB], writes=[modB])
                tr.op(dve, TT(tmp8, pmv[:, 8:16, which], vcol(l, V_BSC, 8), ALU.add),
                      reads=[pb[7], vecB], writes=[smallB])
                tr.op(dve, STT(modv[:, 2 * which, :], tmp8, 1.0, vcol(l, V_G, 8), ALU.add, ALU.mult),
                      reads=[smallB, vecB], writes=[modB])
            lt = small[:, 24:28]
            prod = ar.f32(64)
            prodB = Buf("prod")
            for k in range(2):
                tr.op(dve, TT(prod, vcol(l, V_L + 128 * k, 64), vcol(l, V_L + 128 * k + 64, 64), ALU.mult),
                      reads=[vecB], writes=[prodB])
                tr.op(dve, MSET(lt[:, k:k + 1], 0.0), writes=[smallB])
                tr.op(act, ACTF(prod, prod, AF.Identity, accum_out=lt[:, k:k + 1]), reads=[prodB, smallB], writes=[prodB, smallB])
                tr.op(act, ACTF(lt[:, k:k + 1], lt[:, k:k + 1], AF.Exp), reads=[smallB], writes=[smallB])
            neglam = small[:, 28:29]
            gsub = small[:, 29:30]
            tr.op(dve, TT(lt[:, 2:3], lt[:, 0:1], lt[:, 1:2], ALU.subtract), reads=[smallB], writes=[smallB])
            tr.op(dve, TS(neglam, lt[:, 2:3], lam_init, -1.0, ALU.add, ALU.mult), reads=[smallB], writes=[smallB])
            tr.op(dve, TS(gsub, vcol(l, V_SUB), 1.0 - lam_init), reads=[vecB], writes=[smallB])

            ar.reset()
            xt_r = Ring([ar.f32(1024) for _ in range(3)], "xt")
            xn_r = Ring([ar.f32(1024) for _ in range(3)], "xn")
            junk = ar.bf(1024)
            junkB = Buf("junk")
            st_r = Ring([small[:, 32 + 4 * i: 36 + 4 * i] for i in range(3)], "st")
            for i in range(18):
                xt, xtB = xt_r.next()
                xn, xnB = xn_r.next()
                stt_, stB = st_r.next()
                if l == 0:
                    src = dx[i * 128:(i + 1) * 128, :] if i < 16 else dctx[(i - 16) * 128:(i - 15) * 128, :]
                    rd = []
                else:
                    src = dx1[i * 128:(i + 1) * 128, :]
                    rd = [x1B[i]]
                tr.dma(sp, xld[i % 3], [DMA(xt, src)], reads=rd, writes=[xtB])
                tr.op(dve, MSET(stt_[:, 0:1], 0.0), writes=[stB])
                tr.op(act, ACTF(junk, xt, AF.Square, accum_out=stt_[:, 0:1]), reads=[xtB, stB], writes=[junkB, stB])
                tr.op(act, ACTF(stt_[:, 1:2], stt_[:, 0:1], AF.Sqrt, scale=1.0 / DM, bias=EPS), reads=[stB], writes=[stB])
                tr.op(dve, RCP(stt_[:, 2:3], stt_[:, 1:2]), reads=[stB], writes=[stB])
                tr.op(dve, TS(xn, xt, stt_[:, 2:3]), reads=[xtB, stB], writes=[xnB])
                which = 0 if i < 16 else 1
                for hb in range(2):
                    bk = (2 * i + hb) % 8
                    tr.group(pe, [TRN(psum[bk][:, k4 * 128:(k4 + 1) * 128], xn[:, (hb * 4 + k4) * 128:(hb * 4 + k4 + 1) * 128], identf)
                                  for k4 in range(4)], reads=[xnB, constB], writes=[pb[bk]])
                    for k4 in range(4):
                        kc = hb * 4 + k4
                        dst = hT[:, kc, i * 128:(i + 1) * 128]
                        srcp = psum[bk][:, k4 * 128:(k4 + 1) * 128]
                        A = modv[:, 2 * which, kc:kc + 1]
                        Bc = modv[:, 2 * which + 1, kc:kc + 1]
                        if k4 % 2 == 0:
                            tr.op(dve, TS(dst, srcp, A, Bc, ALU.mult, ALU.add), reads=[pb[bk], modB], writes=[hTB[i // 4]])
                        else:
                            tr.op(act, ACTF(dst, srcp, AF.Identity, scale=A, bias=Bc), reads=[pb[bk], modB], writes=[hTB[i // 4]])

            def proj_fm(ps_i, w, col0, t0, n, wb, t5):
                tr.group(pe, [MM(psum[ps_i][:, 0:n], w[:, kc, col0:col0 + 128], hT[:, kc, t0:t0 + n],
                                 start=(kc == 0), stop=(kc == 7)) for kc in range(8)],
                         reads=[wb, hTB[t5]], writes=[pb[ps_i]])

            def merge_branch(i):
                ar.reset()
                g_r = Ring([ar.f32(512) for _ in range(4)], "G")
                t_r = Ring([ar.f32(512) for _ in range(4)], "mt")
                cnt = 0
                for hf in range(2):
                    wbr_ap, wbrB = wget(f"br{l}_{i}_{hf}")
                    wbr = w3(wbr_ap, 4, 1024)
                    wl_ap, wlB = wget(f"lg{l}_{i}_{hf}")
                    wl = w3(wl_ap, 8, 512)
                    for fcl in range(4):
                        fc = hf * 4 + fcl
                        for t5, (t0, n) in enumerate(qtiles):
                            pz = (2 * cnt) % 8
                            pl = (2 * cnt + 1) % 8
                            cnt += 1
                            fz = [MM(psum[pz][:, 0:n], wbr[:, kc, fc * 128:(fc + 1) * 128], yT[:, kc, t0:t0 + n],
                                     start=(kc == 0), stop=(kc == 3)) for kc in range(4)]
                            fl = [MM(psum[pl][:, 0:n], wl[:, kc, fcl * 128:(fcl + 1) * 128], hT[:, kc, t0:t0 + n],
                                     start=(kc == 0), stop=(kc == 7)) for kc in range(8)]
                            tr.group(pe, fz + fl, reads=[wbrB, wlB, hTB[t5]] + [yTB[kc][t5] for kc in range(4)],
                                     writes=[pb[pz], pb[pl]])
                            G, GB = g_r.next()
                            tr.op(act, ACTF(G[:, 0:n], psum[pl][:, 0:n], AF.Sigmoid, bias=vcol(l, V_BM + i * 8 + fc)),
                                  reads=[pb[pl], vecB], writes=[GB])
                            if i == 0:
                                tr.op(dve, TT(mT[:, fc, t0:t0 + n], psum[pz][:, 0:n], G[:, 0:n], ALU.mult),
                                      reads=[pb[pz], GB], writes=[mTB[fc][t5]])
                            else:
                                tm, tmB = t_r.next()
                                tr.op(dve, TT(tm[:, 0:n], psum[pz][:, 0:n], G[:, 0:n], ALU.mult), reads=[pb[pz], GB], writes=[tmB])
                                tr.op(pool, TT(mT[:, fc, t0:t0 + n], mT[:, fc, t0:t0 + n], tm[:, 0:n], ALU.add),
                                      reads=[tmB], writes=[mTB[fc][t5]])

            def dump(i):
                if debug:
                    dd = tr.dsem(f"dbg{l}_{i}")
                    tr.dma(pool, dd, [DMA(ddbg[l * 4 + i].rearrange("c p t -> p c t"), yT[:, :, :])],
                           reads=[yTB[c][t] for c in range(4) for t in range(5)], writes=[outB])

            ar.reset()
            HG = 2364
            hglu_r = Ring([ar.bf(HG) for _ in range(2)], "hglu")
            diag_ap = ar.bf(31 * 128).rearrange("p (k n) -> p k n", k=31)
            diagB = Buf("diag")
            sg_r = Ring([ar.f32(512) for _ in range(2)], "sg")
            cbuf = mT[:, :, :].rearrange("p c t -> p (c t)").bitcast(F32).rearrange("p (c t) -> p c t", c=4)

            def cB(j):
                return [mTB[2 * j][t] for t in range(5)] + [mTB[2 * j + 1][t] for t in range(5)]
            segs = [(0, 0, 512), (512, 512, 512), (1024, 1024, 512), (1536, 1536, 512), (2078, 2048, 256)]
            segs = segs if need_ctx else segs[:4]
            for j in range(4):
                w_ap, wb = wget(f"A{l}_{j}")
                w = w3(w_ap, 8, 512)
                for k in range(31):
                    tr.op(dve, TS(diag_ap[:, k, :], ident, vcol(l, V_CW + j * 31 + k)), reads=[constB, vecB], writes=[diagB])
                hg, hgB = hglu_r.next()
                for (a, b) in ((0, 15), (2063, 2093), (2349, 2364)):
                    tr.op(pool, MSET(hg[:, a:b], 0.0), writes=[hgB])
                for t5, (bb, t0, n) in enumerate(segs):
                    proj_fm(0 + (t5 % 2) * 2, w, 0, t0, n, wb, t5)
                    proj_fm(1 + (t5 % 2) * 2, w, 128, t0, n, wb, t5)
                    pa, pg = (t5 % 2) * 2, 1 + (t5 % 2) * 2
                    sg, sgB = sg_r.next()
                    tr.op(act, ACTF(sg[:, 0:n], psum[pg][:, 0:n], AF.Sigmoid), reads=[pb[pg]], writes=[sgB])
                    tr.op(dve, TT(hg[:, bb + 15:bb + 15 + n], psum[pa][:, 0:n], sg[:, 0:n], ALU.mult),
                          reads=[pb[pa], sgB], writes=[hgB])
                for t5, (bb, t0, n) in enumerate(segs):
                    pc = 4 + (t5 % 2)
                    tr.group(pe, [MM(psum[pc][:, 0:n], diag_ap[:, k, :], hg[:, bb + k:bb + k + n], start=(k == 0), stop=(k == 30))
                                  for k in range(31)], reads=[diagB, hgB], writes=[pb[pc]])
                    tr.op(act, ACTF(cbuf[:, j, t0:t0 + n], psum[pc][:, 0:n], AF.Identity, bias=vcol(l, V_CB + j)),
                          reads=[pb[pc], vecB], writes=cB(j))
                    pgt = 6 + (t5 % 2)
                    proj_fm(pgt, w, 256, t0, n, wb, t5)
                    tr.op(act, ACTF(yT[:, j, t0:t0 + n], psum[pgt][:, 0:n], AF.Silu), reads=[pb[pgt]], writes=[yTB[j][t5]])
            ar.reset()
            onesf = ar.f32(128)
            onesfB = Buf("onesf")
            tr.op(dve, MSET(onesf, 1.0), writes=[onesfB])
            sq_r = Ring([ar.f32(512) for _ in range(2)], "csq")
            mean = ar.f32(512)
            rstd = ar.f32(512)
            msq = ar.f32(512)
            stB2 = Buf("lnstat")
            d_r = Ring([ar.f32(512) for _ in range(2)], "lnd")
            allc = [b for j in range(4) for b in cB(j)]
            for t5, (t0, n) in enumerate(qtiles):
                tr.group(pe, [MM(psum[0][:, 0:n], onesf, cbuf[:, j, t0:t0 + n], start=(j == 0), stop=(j == 3)) for j in range(4)],
                         reads=allc + [onesfB], writes=[pb[0]])
                sqs = []
                for j in range(4):
                    sq, sqB = sq_r.next()
                    tr.op(act, ACTF(sq[:, 0:n], cbuf[:, j, t0:t0 + n], AF.Square), reads=allc, writes=[sqB])
                    tr.group(pe, [MM(psum[1][:, 0:n], onesf, sq[:, 0:n], start=(j == 0), stop=(j == 3))],
                             reads=[sqB, onesfB], writes=[pb[1]])
                tr.op(dve, TS(mean[:, 0:n], psum[0][:, 0:n], 1.0 / 512), reads=[pb[0]], writes=[stB2])
                tr.op(dve, TT(msq[:, 0:n], mean[:, 0:n], mean[:, 0:n], ALU.mult), reads=[stB2], writes=[stB2])
                tr.op(dve, STT(msq[:, 0:n], psum[1][:, 0:n], 1.0 / 512, msq[:, 0:n], ALU.mult, ALU.subtract),
                      reads=[pb[1], stB2], writes=[stB2])
                tr.op(act, ACTF(msq[:, 0:n], msq[:, 0:n], AF.Sqrt, bias=EPS, scale=1.0), reads=[stB2], writes=[stB2])
                tr.op(dve, RCP(rstd[:, 0:n], msq[:, 0:n]), reads=[stB2], writes=[stB2])
                for j in range(4):
                    d, dB = d_r.next()
                    tr.op(dve, TT(d[:, 0:n], cbuf[:, j, t0:t0 + n], mean[:, 0:n], ALU.subtract), reads=allc + [stB2], writes=[dB])
                    tr.op(dve, TT(d[:, 0:n], d[:, 0:n], rstd[:, 0:n], ALU.mult), reads=[stB2], writes=[dB])
                    tr.op(act, ACTF(d[:, 0:n], d[:, 0:n], AF.Silu, scale=vcol(l, V_LG + j), bias=vcol(l, V_LB + j)),
                          reads=[vecB], writes=[dB])
                    tr.op(dve, TT(yT[:, j, t0:t0 + n], yT[:, j, t0:t0 + n], d[:, 0:n], ALU.mult), reads=[dB], writes=[yTB[j][t5]])
            dump(0)
            merge_branch(0)

            def attn_branch(kind):
                ar.reset()
                QT = ar.bf(2 * T).rearrange("p (h t) -> p h t", h=2)
                KT = ar.bf(T)
                GT = ar.bf(T)
                Vp = ar.bf(18 * 256).rearrange("p (k n) -> p k n", k=18)
                QTB = [Buf(f"QT{t}") for t in range(5)]
                KTB = [Buf(f"KT{t}") for t in range(5)]
                GTB = [Buf(f"GT{t}") for t in range(5)]
                VB = [Buf(f"V{g}") for g in range(5)]
                sq_r = Ring([ar.bf(512) for _ in range(2)], "sq")
                rs_r = Ring([ar.f32(512) for _ in range(2)], "rs")
                xn_r2 = Ring([ar.bf(512) for _ in range(2)], "xnb")
                t1_r = Ring([ar.f32(512) for _ in range(1)], "t1")
                t2_r = Ring([ar.f32(512) for _ in range(1)], "t2")
                P_r = Ring([ar.bf(512) for _ in range(4)], "P")
                ya_r = Ring([ar.f32(512) for _ in range(1)], "ya")
                yb_r = Ring([ar.f32(512) for _ in range(1)], "yb")
                if kind == "B":
                    tab = ar.bf(2 * NJ * 64).rearrange("p (h n) -> p h n", h=2)
                    tab64 = ar.bf(2 * NJ * 64).rearrange("p (t j q) -> p t j q", t=2, j=NJ)
                    rt2 = ar.bf(64)
                    rp = ar.f32(32)
                    tabB = [Buf("tab0"), Buf("tab1")]
                    tab64B, rt2B, rpB = Buf("tab64"), Buf("rt2"), Buf("rp")
                    rpd = tr.dsem(f"rp{l}")
                tr.op(pool, MSET(QT[:, :, :], 0.0), writes=QTB)
                if kind != "D":
                    tr.op(pool, MSET(Vp[:, :, 64:128], 1.0), writes=VB)
                    tr.op(pool, MSET(Vp[:, :, 192:256], 1.0), writes=VB)
                rope = kind != "B"
                qg = {"B": V_NAQ, "C": V_GQ, "D": V_DQ}[kind]
                kg = {"B": V_NAK, "C": V_GK, "D": V_DK}[kind]

                def normed(psi, dst, dstB, gcol, t0, n, do_rope):
                    sq, sqB = sq_r.next()
                    tr.op(act, ACTF(sq[:, 0:n], psum[psi][:, 0:n], AF.Square), reads=[pb[psi]], writes=[sqB])
                    tr.group(pe, [MM(psum[5][:, 0:n], blk, sq[:, 0:n])], reads=[sqB, constB], writes=[pb[5]])
                    rs, rsB = rs_r.next()
                    tr.op(act, ACTF(rs[:, 0:n], psum[5][:, 0:n], AF.Sqrt, scale=1.0 / 64, bias=EPS), reads=[pb[5]], writes=[rsB])
                    tr.op(dve, RCP(rs[:, 0:n], rs[:, 0:n]), reads=[rsB], writes=[rsB])
                    if not do_rope:
                        for (r0, r1, d_ap) in dst:
                            tr.op(dve, STT(d_ap, psum[psi][r0:r1, 0:n], vecs[r0:r1, l * NVL + gcol:l * NVL + gcol + 1], rs[r0:r1, 0:n],
                                           ALU.mult, ALU.mult), reads=[pb[psi], rsB, vecB], writes=[dstB])
                        return
                    xb, xbB = xn_r2.next()
                    tr.op(dve, STT(xb[:, 0:n], psum[psi][:, 0:n], vcol(l, gcol), rs[:, 0:n], ALU.mult, ALU.mult),
                          reads=[pb[psi], rsB, vecB], writes=[xbB])
                    tr.group(pe, [MM(psum[6][:, 0:n], perm, xb[:, 0:n])], reads=[xbB, constB], writes=[pb[6]])
                    t1, t1B = t1_r.next()
                    t2, t2B = t2_r.next()
                    tr.op(dve, TT(t1[:, 0:n], xb[:, 0:n], COS[:, t0:t0 + n], ALU.mult), reads=[xbB, constB], writes=[t1B])
                    tr.op(dve, TT(t2[:, 0:n], psum[6][:, 0:n], SIN[:, t0:t0 + n], ALU.mult), reads=[pb[6], constB], writes=[t2B])
                    for (r0, r1, d_ap) in dst:
                        tr.op(dve, TT(d_ap, t1[r0:r1, 0:n], t2[r0:r1, 0:n], ALU.add), reads=[t1B, t2B], writes=[dstB])

                for pr in range(4):
                    w_ap, wb = wget(f"{kind}{l}_{pr}")
                    w = w3(w_ap, 8, 512)
                    if kind == "B":
                        for h2 in range(2):
                            h = 2 * pr + h2
                            tr.dma(sp, rpd, [DMA(rp[0:15, 0:31], drpb[l, h])], writes=[rpB])
                            tr.group(pe, [MM(psum[7][0:31, 0:2 * NJ], rp[0:15, 0:31], selR[0:15, 0:2 * NJ])],
                                     reads=[rpB, constB], writes=[pb[7]])
                            tr.op(dve, CP(rt2[0:31, 0:2 * NJ], psum[7][0:31, 0:2 * NJ]), reads=[pb[7]], writes=[rt2B])
                            for g in range(8):
                                bk = g % 4
                                tr.group(pe, [MM(psum[bk][0:64, q8 * 2 * NJ:(q8 + 1) * 2 * NJ],
                                                 band[0:31, 63 - (8 * g + q8):127 - (8 * g + q8)], rt2[0:31, 0:2 * NJ])
                                              for q8 in range(8)], reads=[rt2B, constB], writes=[pb[bk]])
                                pv = psum[bk][0:64, 0:8 * 2 * NJ].rearrange("p (q t j) -> p t j q", q=8, t=2)
                                for hf in range(2):
                                    tr.op(dve, TT(tab64[0:64, hf, :, 8 * g:8 * g + 8], pv[:, hf], maskT[0:64, hf, :, 8 * g:8 * g + 8], ALU.add),
                                          reads=[pb[bk], constB], writes=[tab64B])
                            tr.op(dve, CP(tab[0:64, h2, :], tab64[0:64, 0].rearrange("p j q -> p (j q)")), reads=[tab64B], writes=[tabB[h2]])
                            tr.op(dve, CP(tab[64:128, h2, :], tab64[0:64, 1].rearrange("p j q -> p (j q)")), reads=[tab64B], writes=[tabB[h2]])
                    cnt = 0
                    for t5, (t0, n) in enumerate(qtiles):
                        psi = cnt % 4
                        cnt += 1
                        proj_fm(psi, w, 0, t0, n, wb, t5)
                        normed(psi, [(0, 64, QT[0:64, 0, t0:t0 + n]), (64, 128, QT[64:128, 1, t0:t0 + n])], QTB[t5], qg, t0, n,
                               rope and t5 < 4)
                    for t5, (t0, n) in enumerate(TILES):
                        psi = cnt % 4
                        cnt += 1
                        proj_fm(psi, w, 128, t0, n, wb, t5)
                        normed(psi, [(0, 128, KT[:, t0:t0 + n])], KTB[t5], kg, t0, n, rope and t5 < 4)
                    for t5, (t0, n) in enumerate(qtiles):
                        psi = cnt % 4
                        cnt += 1
                        proj_fm(psi, w, 384, t0, n, wb, t5)
                        tr.op(act, ACTF(GT[:, t0:t0 + n], psum[psi][:, 0:n], AF.Silu), reads=[pb[psi]], writes=[GTB[t5]])
                    nv = 64 if kind == "C" else 128
                    for g5 in range(5):
                        kts = list(range(4 * g5, min(4 * g5 + 4, 18)))
                        psi = cnt % 4
                        cnt += 1
                        fns = []
                        for ii, kt in enumerate(kts):
                            for kc in range(8):
                                fns.append(MM(psum[psi][:, ii * 128:ii * 128 + nv], hT[:, kc, kt * 128:(kt + 1) * 128],
                                              w[:, kc, 256:256 + nv], start=(kc == 0), stop=(kc == 7)))
                        tr.group(pe, fns, reads=[wb, hTB[g5]], writes=[pb[psi]])
                        nk = len(kts)
                        pvv = psum[psi][:, 0:nk * 128].rearrange("p (k n) -> p k n", k=nk)
                        k0 = kts[0]
                        if kind == "B":
                            tr.op(act, ACTF(Vp[:, k0:k0 + nk, 0:64], pvv[:, :, 0:64], AF.Copy), reads=[pb[psi]], writes=[VB[g5]])
                            tr.op(dve, CP(Vp[:, k0:k0 + nk, 128:192], pvv[:, :, 64:128]), reads=[pb[psi]], writes=[VB[g5]])
                        elif kind == "C":
                            tr.op(act, ACTF(Vp[:, k0:k0 + nk, 0:64], pvv[:, :, 0:64], AF.Copy), reads=[pb[psi]], writes=[VB[g5]])
                        else:
                            tr.op(act, ACTF(Vp[:, k0:k0 + nk, 0:128], pvv[:, :, 0:128], AF.Copy), reads=[pb[psi]], writes=[VB[g5]])

                    QW = 256

                    def both(ap512, qa, qb):
                        if qa == 0 and qb == QW:
                            return ap512
                        return ap512.rearrange("p (h q) -> p h q", h=2)[:, :, qa:qb]

                    qts = []
                    if kind == "B":
                        for r_lo in range(0, 32, 4):
                            ch = [(16, 0, QW, None), (17, 0, QW, None)]
                            if r_lo in (0, 28):
                                a0 = 0 if r_lo == 0 else 12
                                for a in range(a0, a0 + 4):
                                    ch.append((a, 0, QW, (10 + 7 - 2 * a + r_lo) * 64))
                            else:
                                for a in range(16):
                                    rs_ = max(r_lo, 2 * a - 3)
                                    re_ = min(r_lo + 3, 2 * a + 5)
                                    if rs_ <= re_:
                                        ch.append((a, (rs_ - r_lo) * 64, (re_ - r_lo + 1) * 64, (4 - 2 * a + rs_) * 64))
                            qts.append((r_lo * 64, ch))
                    else:
                        for q0 in range(0, S, QW):
                            qts.append((q0, [(kt, 0, QW, None) for kt in range(18)]))
                    if need_ctx:
                        qts.append((2048, [(16, 0, QW, None), (17, 0, QW, None)]))

                    import os as _os
                    NCH = int(_os.environ.get('KNCH', '2'))
                    nacc = {"B": 2, "C": 1, "D": 2}[kind]
                    steps = []
                    for ti, (q0, ch) in enumerate(qts):
                        t5 = min(q0 // 512, 4)
                        if nacc == 2:
                            accs = [(ti % 2) * 2, (ti % 2) * 2 + 1]
                        else:
                            accs = [ti % 4]
                        nch = len(ch)
                        for c0 in range(0, nch, NCH):
                            subs = []
                            for ci in range(c0, min(c0 + NCH, nch)):
                                kt, qa, qb, bcol = ch[ci]
                                subs.append(dict(kt=kt, qa=qa, qb=qb, bcol=bcol, first=(ci == 0), last=(ci == nch - 1)))
                            steps.append(dict(q0=q0, t5=t5, accs=accs, subs=subs))
                        steps.append(dict(epi=True, q0=q0, t5=t5, accs=accs))

                    sring = [4, 5, 6, 7]
                    scount = [0]

                    def do_qk(st):
                        q0, t5 = st["q0"], st["t5"]
                        fns = []
                        rd = [QTB[t5], constB]
                        wr = []
                        for sub in st["subs"]:
                            si = sring[scount[0] % 4]
                            scount[0] += 1
                            sub["si"] = si
                            rd.append(KTB[sub["kt"] // 4])
                            wr.append(pb[si])

                        for sub in st["subs"]:
                            si, kt, qa, qb, bcol = sub["si"], sub["kt"], sub["qa"], sub["qb"], sub["bcol"]
                            o3 = psum[si][:, 0:512].rearrange("p (h q) -> p h q", h=2)[:, :, qa:qb]
                            fns.append(MM(o3, KT[:, kt * 128:(kt + 1) * 128], QT[:, :, q0 + qa:q0 + qb], start=True, stop=(bcol is None)))
                            if bcol is not None:
                                fns.append(MM(o3, ident, tab[:, :, bcol:bcol + (qb - qa)], start=False, stop=True))
                                rd += tabB
                        tr.group(pe, fns, reads=rd, writes=wr)
                        for sub in st["subs"]:
                            P, PB = P_r.next()
                            sub["P"], sub["PB"] = P, PB
                            si, qa, qb = sub["si"], sub["qa"], sub["qb"]
                            tr.op(act, ACTF(both(P[:, 0:512], qa, qb), both(psum[si][:, 0:512], qa, qb), AF.Exp, scale=0.125),
                                  reads=[pb[si]], writes=[PB])

                    def do_pv(st):
                        accs = st["accs"]
                        fns = []
                        rd = [constB]
                        for sub in st["subs"]:
                            kt, qa, qb, P = sub["kt"], sub["qa"], sub["qb"], sub["P"]
                            rd += [sub["PB"], VB[kt // 4]]
                            f, la = sub["first"], sub["last"]
                            if kind == "B":
                                for hf in range(2):
                                    fns.append(MM(psum[accs[hf]][:, qa:qb], Vp[:, kt, hf * 128:(hf + 1) * 128],
                                                  P[:, hf * QW + qa:hf * QW + qb], start=f, stop=la))
                            elif kind == "C":
                                fns.append(MM(psum[accs[0]][:, 0:512], Vp[:, kt, 0:128], P[:, 0:512], start=f, stop=la))
                            else:
                                fns.append(MM(psum[accs[0]][:, 0:512], Vp[:, kt, 0:128], P[:, 0:512], start=f, stop=la))
                                fns.append(MM(psum[accs[1]][:, 0:512], ones, P[:, 0:512], start=f, stop=la))
                        tr.group(pe, fns, reads=rd, writes=[pb[a] for a in accs])

                    def do_epi(st):
                        q0, accs, t5 = st["q0"], st["accs"], st["t5"]
                        qn = QW
                        ywr = [yTB[pr][t5]]
                        grd = [GTB[t5]]
                        ya, yaB = ya_r.next()
                        yb, ybB = yb_r.next()
                        if kind == "B":
                            for hf in range(2):
                                o = accs[hf]
                                tr.op(dve, RCP(ya[0:64, hf * QW:hf * QW + qn], psum[o][64:128, 0:qn]), reads=[pb[o]], writes=[yaB])
                                tr.op(dve, TT(yb[hf * 64:(hf + 1) * 64, 0:qn], psum[o][0:64, 0:qn], ya[0:64, hf * QW:hf * QW + qn], ALU.mult),
                                      reads=[pb[o], yaB], writes=[ybB])
                            tr.op(dve, TT(yT[:, pr, q0:q0 + qn], yb[:, 0:qn], GT[:, q0:q0 + qn], ALU.mult),
                                  reads=[ybB] + grd, writes=ywr)
                        elif kind == "C":
                            o = accs[0]
                            tr.op(dve, RCP(ya[0:64, 0:512], psum[o][64:128, 0:512]), reads=[pb[o]], writes=[yaB])
                            for hf in range(2):
                                tr.op(dve, TT(yb[hf * 64:(hf + 1) * 64, 0:qn], psum[o][0:64, hf * QW:hf * QW + qn],
                                              ya[0:64, hf * QW:hf * QW + qn], ALU.mult), reads=[pb[o], yaB], writes=[ybB])
                            tr.op(dve, TT(yT[:, pr, q0:q0 + qn], yb[:, 0:qn], GT[:, q0:q0 + qn], ALU.mult),
                                  reads=[ybB] + grd, writes=ywr)
                        else:
                            o, dn = accs
                            tr.op(dve, RCP(ya[:, 0:512], psum[dn][:, 0:512]), reads=[pb[dn]], writes=[yaB])
                            tr.op(dve, TT(ya[:, 0:512], psum[o][:, 0:512], ya[:, 0:512], ALU.mult), reads=[pb[o]], writes=[yaB])
                            tr.op(dve, STT(yb[:, 0:qn], ya[:, QW:QW + qn], neglam, ya[:, 0:qn], ALU.mult, ALU.add),
                                  reads=[yaB, smallB], writes=[ybB])
                            sq, sqB = sq_r.next()
                            tr.op(act, ACTF(sq[:, 0:qn], yb[:, 0:qn], AF.Square), reads=[ybB], writes=[sqB])
                            tr.group(pe, [MM(psum[dn][:, 0:qn], ones, sq[:, 0:qn])], reads=[sqB, constB], writes=[pb[dn]])
                            tr.op(act, ACTF(ya[:, 0:qn], psum[dn][:, 0:qn], AF.Sqrt, scale=1.0 / 128, bias=EPS), reads=[pb[dn]], writes=[yaB])
                            tr.op(dve, RCP(ya[:, 0:qn], ya[:, 0:qn]), reads=[], writes=[yaB])
                            tr.op(dve, TT(yb[:, 0:qn], yb[:, 0:qn], ya[:, 0:qn], ALU.mult), reads=[yaB], writes=[ybB])
                            tr.op(dve, STT(yT[:, pr, q0:q0 + qn], yb[:, 0:qn], gsub, GT[:, q0:q0 + qn], ALU.mult, ALU.mult),
                                  reads=[ybB, smallB] + grd, writes=ywr)

                    LOOK = int(_os.environ.get('KLOOK', '1'))
                    if kind in _os.environ.get("KSKIP", ""):
                        steps = []
                    qk_list = [s_ for s_ in steps if "epi" not in s_]
                    qi = 0
                    done = 0
                    for s_ in steps:
                        if "epi" in s_:
                            do_epi(s_)
                            continue
                        while qi < len(qk_list) and qi <= done + LOOK:
                            do_qk(qk_list[qi])
                            qi += 1
                        do_pv(s_)
                        done += 1

            attn_branch("B")
            dump(1)
            merge_branch(1)
            attn_branch("C")
            dump(2)
            merge_branch(2)
            attn_branch("D")
            dump(3)
            merge_branch(3)

            ar.reset()
            if debug and l == 0:
                ddm = tr.dsem("dbgm")
                tr.dma(pool, ddm, [DMA(ddbgm.rearrange("c p t -> p c t"), mT[:, :, :])],
                       reads=[mTB[c][t] for c in range(8) for t in range(5)], writes=[outB])
            screp = ar.bf(8 * 128).rearrange("p (k n) -> p k n", k=8)
            sccrep = ar.bf(8 * 128).rearrange("p (k n) -> p k n", k=8)
            repB = Buf("rep")
            for kc in range(8):
                tr.op(dve, CP(screp[:, kc, :], s2[:, kc, 0:1].to_broadcast([128, 128])), reads=[s2B], writes=[repB])
                tr.op(dve, CP(sccrep[:, kc, :], s2[:, kc, 1:2].to_broadcast([128, 128])), reads=[s2B], writes=[repB])
            tr.dma(pool, bgd, [DMA(bgrow[:, :], dbg_rows[:, l * DM:(l + 1) * DM])], writes=[bgB])
            gx = ar.f32(1024)
            gc = ar.f32(1024)
            gB = Buf("gates")
            for pi in range(2):
                w_ap, wb = wget(f"adag{l}_{pi}")
                w = w3(w_ap, 8, 512)
                for which, (rep, dst) in enumerate(((screp, gx), (sccrep, gc))):
                    if which == 1 and not need_ctx:
                        continue
                    psi = pi * 2 + which
                    fns = [MM(psum[psi][:, :], rep[:, kc, :], w[:, kc, :], start=(kc == 0), stop=False) for kc in range(8)]
                    fns.append(MM(psum[psi][:, :], ones[0:1, :], bgrow[0:1, pi * 512:(pi + 1) * 512], start=False, stop=True))
                    tr.group(pe, fns, reads=[wb, repB, constB, bgB], writes=[pb[psi]])
                    tr.op(act, ACTF(dst[:, pi * 512:(pi + 1) * 512], psum[psi][:, :], AF.Copy), reads=[pb[psi]], writes=[gB])
            if debug and l == 0:
                ddg = tr.dsem("dbgg")
                tr.dma(sp, ddg, [DMA(ddbgg[:, 0:1024], gx), DMA(ddbgg[:, 1024:2048], gc)], reads=[gB], writes=[outB])
            xo_r = Ring([ar.f32(512) for _ in range(2)], "xo")
            res_r = Ring([ar.f32(512) for _ in range(2)], "res")
            tm_r = Ring([ar.f32(512) for _ in range(2)], "otm")
            ntile = 18 if need_ctx else 16
            cnt = 0
            for ph in range(2):
                w_ap, wb = wget(f"wo{l}_{ph}")
                w = w3(w_ap, 8, 512)
                for i in range(ntile):
                    psi = 4 + cnt % 4
                    t5 = min(i // 4, 4)
                    tr.group(pe, [MM(psum[psi][:, :], mT[:, kc, i * 128:(i + 1) * 128], w[:, kc, :], start=(kc == 0), stop=(kc == 7))
                                  for kc in range(8)], reads=[wb] + [mTB[kc][t5] for kc in range(8)], writes=[pb[psi]])
                    xo, xoB = xo_r.next()
                    if l == 0:
                        src = dx[i * 128:(i + 1) * 128, ph * 512:(ph + 1) * 512] if i < 16 else \
                            dctx[(i - 16) * 128:(i - 15) * 128, ph * 512:(ph + 1) * 512]
                        rd = []
                    else:
                        src = dx1[i * 128:(i + 1) * 128, ph * 512:(ph + 1) * 512]
                        rd = [x1B[i]]
                    tr.dma(sp, xld[cnt % 2], [DMA(xo, src)], reads=rd, writes=[xoB])
                    gate = gx if i < 16 else gc
                    tm, tmB = tm_r.next()
                    rs_, rsB_ = res_r.next()
                    tr.op(dve, TT(tm, psum[psi][:, :], gate[:, ph * 512:(ph + 1) * 512], ALU.mult), reads=[pb[psi], gB], writes=[tmB])
                    tr.op(dve, TT(rs_, tm, xo, ALU.add), reads=[tmB, xoB], writes=[rsB_])
                    if l == NL - 1:
                        tr.dma(sp, std[cnt % 2], [DMA(dout[i * 128:(i + 1) * 128, ph * 512:(ph + 1) * 512], rs_)],
                               reads=[rsB_], writes=[outB])
                    else:
                        tr.dma(sp, std[cnt % 2], [DMA(dx1[i * 128:(i + 1) * 128, ph * 512:(ph + 1) * 512], rs_)],
                               reads=[rsB_], writes=[x1B[i]])
                    cnt += 1

        for l in range(NL):
            run_layer(l, l < NL - 1)
        tr.barrier()

        block = es.enter_context(nc.Block())

        @block.tensor
        def _(h):
            Tracer.replay(pe, h)

        @block.scalar
        def _(h):
            Tracer.replay(act, h)

        @block.vector
        def _(h):
            Tracer.replay(dve, h)

        @block.gpsimd
        def _(h):
            Tracer.replay(pool, h)

        @block.sync
        def _(h):
            Tracer.replay(sp, h)
    return nc


_CACHE = {}


def _prep_inputs(inputs):
    f = lambda a: np.ascontiguousarray(np.asarray(a, dtype=np.float32))
    x, c, ctx, c_ctx = f(inputs["x"]), f(inputs["c"]), f(inputs["ctx"]), f(inputs["c_ctx"])
    cf, cb = _host_consts()
    w_br = np.ascontiguousarray(np.stack([f(inputs["w_br_a"]), f(inputs["w_br_b"]), f(inputs["w_br_c"]), f(inputs["w_br_d"])], axis=1))
    vecs = np.zeros((128, NL * NVL), np.float32)

    def fm(v, n):
        return np.ascontiguousarray(v.reshape(n, 128).T)

    for l in range(NL):
        o = l * NVL
        vecs[:, o + V_G:o + V_G + 8] = fm(f(inputs["norm_g"])[l], 8)
        b_ada = f(inputs["b_ada"])[l]
        vecs[:, o + V_BSH:o + V_BSH + 8] = fm(b_ada[0:1024], 8)
        vecs[:, o + V_BSC:o + V_BSC + 8] = fm(b_ada[1024:2048], 8)
        vecs[:, o + V_BM:o + V_BM + 32] = fm(f(inputs["b_merge"])[l], 32)
        vecs[:, o + V_CB:o + V_CB + 4] = fm(f(inputs["conv_b"])[l], 4)
        vecs[:, o + V_LG:o + V_LG + 4] = fm(f(inputs["conv_ln_g"])[l], 4)
        vecs[:, o + V_LB:o + V_LB + 4] = fm(f(inputs["conv_ln_b"])[l], 4)
        cw = f(inputs["conv_w"])[l]
        vecs[:, o + V_CW:o + V_CW + 124] = cw.T.reshape(4, 128, 31).transpose(1, 0, 2).reshape(128, 124)
        for nm, off in (("na_qn_g", V_NAQ), ("na_kn_g", V_NAK), ("gqa_qn_g", V_GQ), ("gqa_kn_g", V_GK),
                        ("diff_qn_g", V_DQ), ("diff_kn_g", V_DK)):
            vecs[:, o + off] = np.tile(f(inputs[nm])[l], 2)
        vecs[:, o + V_SUB] = f(inputs["diff_subln_g"])[l]
        for k, nm in enumerate(("lam_q1", "lam_k1", "lam_q2", "lam_k2")):
            vecs[:, o + V_L + 64 * k:o + V_L + 64 * (k + 1)] = f(inputs[nm])[l][None, :]
    bgrow = np.ascontiguousarray(f(inputs["b_ada"])[:, 2048:3072].reshape(1, NL * DM))
    shared = {"w_ada": f(inputs["w_ada"]), "w_in": f(inputs["w_in"]), "w_br": w_br, "w_out": f(inputs["w_out"]),
              "vecs": vecs, "bgrow": bgrow, "rpb": f(inputs["na_rpb"]), "cf": cf, "cb": cb}
    maps = []
    for b in range(8):
        cT = np.concatenate([fm(c[b], 8), fm(c_ctx, 8)], axis=1)
        m = dict(shared)
        m.update({"x": x[b], "ctx": ctx[b], "cT": np.ascontiguousarray(cT)})
        maps.append(m)
    return maps


def kernel(**inputs):
    if "nc" not in _CACHE:
        _CACHE["nc"] = build_program(False)
    maps = _prep_inputs(inputs)
    res = run_bass_kernel_spmd(_CACHE["nc"], maps, core_ids=list(range(8)))
    return np.stack([np.asarray(r["out"], dtype=np.float32) for r in res.results], axis=0)
```

```python
import math
import numpy as np
from contextlib import ExitStack
import concourse.bass as bass
import concourse.mybir as mybir
from concourse.bass_utils import run_bass_kernel_spmd

F32 = mybir.dt.float32
BF16 = mybir.dt.bfloat16
AF = mybir.ActivationFunctionType
ALU = mybir.AluOpType

DM = 1024
S = 2048
LC = 256
T = S + LC
NL = 2
INW = 11008
EPS = 1e-6
NEGM = -30000.0
NVL = 455
NJ = 26
TILES = [(0, 512), (512, 512), (1024, 512), (1536, 512), (2048, 256)]

V_G, V_BSH, V_BSC, V_BM, V_CB, V_LG, V_LB, V_CW = 0, 8, 16, 24, 56, 60, 64, 68
V_NAQ, V_NAK, V_GQ, V_GK, V_DQ, V_DK, V_SUB = 192, 193, 194, 195, 196, 197, 198
V_L = 199


class Buf:
    __slots__ = ("w", "r", "name")

    def __init__(self, name=""):
        self.w = None
        self.r = {}
        self.name = name


class DSem:
    def __init__(self, h):
        self.h = h
        self.count = 0


class Eng:
    def __init__(self, tr, name):
        self.tr = tr
        self.name = name
        self.items = []
        self.seen = {}
        self.sems = []
        self.count = 0
        self.newsem()

    def newsem(self):
        h = self.tr.es.enter_context(self.tr.nc.semaphore(f"s_{self.name}{len(self.sems)}"))
        self.sems.append(h)
        self.count = 0


class Tracer:
    def __init__(self, nc, es):
        self.nc = nc
        self.es = es
        self.pe = Eng(self, "pe")
        self.act = Eng(self, "act")
        self.dve = Eng(self, "dve")
        self.pool = Eng(self, "pool")
        self.sp = Eng(self, "sp")
        self.engs = [self.pe, self.act, self.dve, self.pool, self.sp]
        self.dsems = []

    def dsem(self, name):
        d = DSem(self.es.enter_context(self.nc.semaphore("d_" + name)))
        self.dsems.append(d)
        return d

    def _deps(self, eng, reads, writes):
        need = {}

        def add(tok):
            if tok is None:
                return
            sem, val, src = tok
            if src is eng and eng.name == "pe":
                return
            k = id(sem)
            if k not in need or need[k][1] < val:
                need[k] = (sem, val)

        for b in reads:
            add(b.w)
        for b in writes:
            add(b.w)
            for t in b.r.values():
                add(t)
        for k, (sem, val) in need.items():
            if eng.seen.get(k, 0) < val:
                eng.items.append(("wait", sem, val))
                eng.seen[k] = val

    @staticmethod
    def _commit(tok, reads, writes):
        for b in writes:
            b.w = tok
            b.r = {}
        k = id(tok[0])
        for b in reads:
            b.r[k] = tok

    def op(self, eng, fn, reads=(), writes=()):
        self._deps(eng, reads, writes)
        if eng.count >= 16000:
            eng.newsem()
        eng.count += 1
        tok = (eng.sems[-1], eng.count, eng)
        eng.items.append(("ins", fn, eng.sems[-1], 1))
        self._commit(tok, reads, writes)
        return tok

    def group(self, eng, fns, reads=(), writes=()):
        self._deps(eng, reads, writes)
        if eng.count >= 16000:
            eng.newsem()
        for f in fns[:-1]:
            eng.items.append(("ins", f, None, 0))
        eng.count += 1
        tok = (eng.sems[-1], eng.count, eng)
        eng.items.append(("ins", fns[-1], eng.sems[-1], 1))
        self._commit(tok, reads, writes)
        return tok

    def dma(self, eng, dsem, fns, reads=(), writes=()):
        self._deps(eng, reads, writes)
        for f in fns:
            eng.items.append(("ins", f, dsem.h, 16))
            dsem.count += 16
        tok = (dsem.h, dsem.count, None)
        self._commit(tok, reads, writes)
        return tok

    def barrier(self):
        toks = [(e.sems[-1], e.count, e) for e in self.engs if e.count > 0]
        toks += [(d.h, d.count, None) for d in self.dsems if d.count > 0]
        for e in self.engs:
            for sem, val, src in toks:
                if src is e:
                    continue
                k = id(sem)
                if e.seen.get(k, 0) < val:
                    e.items.append(("wait", sem, val))
                    e.seen[k] = val

    @staticmethod
    def replay(eng, h):
        for it in eng.items:
            if it[0] == "wait":
                h.wait_ge(it[1], it[2])
            else:
                ins = it[1](h)
                if it[2] is not None:
                    ins.then_inc(it[2], it[3])


def MM(out, lhsT, rhs, start=True, stop=True):
    return lambda h: h.matmul(out, lhsT=lhsT, rhs=rhs, start=start, stop=stop)


def TRN(out, in_, ident):
    return lambda h: h.transpose(out, in_, ident)


def ACTF(out, in_, func, **kw):
    return lambda h: h.activation(out=out, in_=in_, func=func, **kw)


def TT(out, in0, in1, op):
    return lambda h: h.tensor_tensor(out=out, in0=in0, in1=in1, op=op)


def TS(out, in0, s1, s2=None, op0=ALU.mult, op1=None):
    if op1 is None:
        return lambda h: h.tensor_scalar(out=out, in0=in0, scalar1=s1, scalar2=None, op0=op0)
    return lambda h: h.tensor_scalar(out=out, in0=in0, scalar1=s1, scalar2=s2, op0=op0, op1=op1)


def STT(out, in0, scalar, in1, op0, op1):
    return lambda h: h.scalar_tensor_tensor(out=out, in0=in0, scalar=scalar, in1=in1, op0=op0, op1=op1)


def CP(out, in_):
    return lambda h: h.tensor_copy(out=out, in_=in_)


def RCP(out, in_):
    return lambda h: h.reciprocal(out=out, in_=in_)


def MSET(ap, v):
    return lambda h: h.memset(ap, v)


def DMA(out, in_):
    return lambda h: h.dma_start(out=out, in_=in_)


class Ring:
    def __init__(self, aps, name):
        self.aps = aps
        self.bufs = [Buf(f"{name}{i}") for i in range(len(aps))]
        self.i = 0

    def next(self):
        k = self.i % len(self.aps)
        self.i += 1
        return self.aps[k], self.bufs[k]


def _host_consts():
    identf = np.eye(128, dtype=np.float32)
    p = np.arange(128)
    hd = p % 64
    half = hd // 32
    fi = (hd % 32) % 16
    freq = (10000.0 ** (-(2.0 * fi) / 32.0)).astype(np.float32)
    t = np.arange(S)
    rows = (t // 64).astype(np.float32)
    cols = (t % 64).astype(np.float32)
    pos = np.where(half[:, None] == 0, rows[None, :], cols[None, :]).astype(np.float32)
    ang = (pos * freq[:, None]).astype(np.float32)
    cos = np.cos(ang).astype(np.float32)
    sgn = np.where((hd % 32) < 16, -1.0, 1.0).astype(np.float32)
    sin = (np.sin(ang).astype(np.float32) * sgn[:, None]).astype(np.float32)
    selR = np.zeros((128, 64), np.float32)

    def delta(hf, jj):
        return (4 - jj) + hf if jj < 10 else (7 - (jj - 10)) + hf

    for hf in range(2):
        for jj in range(NJ):
            dr = delta(hf, jj) + 7
            if 0 <= dr <= 14:
                selR[dr, hf * NJ + jj] = 8.0
    cf = np.concatenate([identf, cos, sin, selR], axis=1)

    ident = identf
    blk = ((p[:, None] // 64) == (p[None, :] // 64)).astype(np.float32)
    partner = np.where((p % 32) < 16, p + 16, p - 16)
    perm = np.zeros((128, 128), np.float32)
    perm[partner, p] = 1.0
    band = np.zeros((128, 128), np.float32)
    for c in range(31):
        band[c, c + 48] = 1.0
    mask = np.zeros((128, 2, NJ, 64), np.float32)
    qc = np.arange(64)
    c0 = np.clip(qc - 8, 0, 48)
    for kc in range(64):
        colok = (kc >= c0) & (kc < c0 + 16)
        for hf in range(2):
            for jj in range(NJ):
                d = delta(hf, jj)
                ok = (-4 <= d <= 3) if jj < 10 else (-7 <= d <= 7)
                mask[kc, hf, jj, :] = np.where(colok & ok, 0.0, NEGM)
    cb = np.concatenate([ident, blk, perm, band, np.ones((128, 128), np.float32),
                         mask.reshape(128, -1)], axis=1)
    return np.ascontiguousarray(cf), np.ascontiguousarray(cb)


CF_ID, CF_COS, CF_SIN, CF_SEL, CF_N = 0, 128, 128 + 2048, 128 + 4096, 128 + 4096 + 64
CB_ID, CB_BLK, CB_PERM, CB_BAND, CB_ONES, CB_MASK, CB_N = 0, 128, 256, 384, 512, 640, 640 + 2 * NJ * 64


def build_program(debug=False):
    nc = bass.Bass("TRN2", target_bir_lowering=False)
    dx = nc.dram_tensor("x", [S, DM], F32, kind="ExternalInput").ap()
    dctx = nc.dram_tensor("ctx", [LC, DM], F32, kind="ExternalInput").ap()
    dcT = nc.dram_tensor("cT", [128, 16], F32, kind="ExternalInput").ap()
    dwada = nc.dram_tensor("w_ada", [NL, DM, 3 * DM], F32, kind="ExternalInput").ap()
    dwin = nc.dram_tensor("w_in", [NL, DM, INW], F32, kind="ExternalInput").ap()
    dwbr = nc.dram_tensor("w_br", [NL, 4, 512, DM], F32, kind="ExternalInput").ap()
    dwout = nc.dram_tensor("w_out", [NL, DM, DM], F32, kind="ExternalInput").ap()
    dvecs = nc.dram_tensor("vecs", [128, NL * NVL], F32, kind="ExternalInput").ap()
    dbg_rows = nc.dram_tensor("bgrow", [1, NL * DM], F32, kind="ExternalInput").ap()
    drpb = nc.dram_tensor("rpb", [NL, 8, 15, 31], F32, kind="ExternalInput").ap()
    dcf = nc.dram_tensor("cf", [128, CF_N], F32, kind="ExternalInput").ap()
    dcb = nc.dram_tensor("cb", [128, CB_N], F32, kind="ExternalInput").ap()
    dout = nc.dram_tensor("out", [S, DM], F32, kind="ExternalOutput").ap()
    dx1 = nc.dram_tensor("x1s", [T, DM], F32, kind="ExternalOutput" if debug else "Internal").ap()
    ddbg = None
    if debug:
        ddbg = nc.dram_tensor("dbg", [8, 4, 128, T], F32, kind="ExternalOutput").ap()
        ddbgm = nc.dram_tensor("dbgm", [8, 128, T], F32, kind="ExternalOutput").ap()
        ddbgg = nc.dram_tensor("dbgg", [128, 2048], F32, kind="ExternalOutput").ap()

    es = ExitStack()
    with es:
        tr = Tracer(nc, es)
        pe, act, dve, pool, sp = tr.pe, tr.act, tr.dve, tr.pool, tr.sp

        def sb(name, shape, dt):
            return es.enter_context(nc.sbuf_tensor("sb_" + name, shape, dt))

        hT = sb("hT", [128, 8, T], BF16)
        mT = sb("mT", [128, 8, T], BF16)
        yT = sb("yT", [128, 4, T], BF16)
        cfs = sb("cfs", [128, CF_N], F32)
        cbs = sb("cbs", [128, CB_N], BF16)
        vecs = sb("vecs", [128, NL * NVL], F32)
        bgrow = sb("bgrow", [1, DM], BF16)
        bgB = Buf("bgrow")
        bgd = tr.dsem("bgrow")
        modv = sb("modv", [128, 4, 8], F32)
        s2 = sb("s2", [128, 8, 2], BF16)
        small = sb("small", [128, 64], F32)
        wring_t = [sb(f"wr{i}", [128, 4096], BF16) for i in range(3)]
        ARENA_N = 31780
        arena = sb("arena", [128, ARENA_N], BF16)
        psum = [es.enter_context(nc.psum_tensor(f"ps{i}", [128, 512], F32)) for i in range(8)]
        pb = [Buf(f"ps{i}") for i in range(8)]

        hTB = [Buf(f"hT{t}") for t in range(5)]
        mTB = [[Buf(f"mT{c}_{t}") for t in range(5)] for c in range(8)]
        yTB = [[Buf(f"yT{c}_{t}") for t in range(5)] for c in range(4)]
        constB = Buf("const")
        vecB = Buf("vecs")
        modB = Buf("modv")
        s2B = Buf("s2")
        smallB = Buf("small")
        x1B = [Buf(f"x1_{i}") for i in range(18)]
        outB = Buf("out")

        identf = cfs[:, CF_ID:CF_ID + 128]
        COS = cfs[:, CF_COS:CF_COS + S]
        SIN = cfs[:, CF_SIN:CF_SIN + S]
        selR = cfs[:, CF_SEL:CF_SEL + 64]
        ident = cbs[:, CB_ID:CB_ID + 128]
        blk = cbs[:, CB_BLK:CB_BLK + 128]
        perm = cbs[:, CB_PERM:CB_PERM + 128]
        band = cbs[:, CB_BAND:CB_BAND + 128]
        ones = cbs[:, CB_ONES:CB_ONES + 128]
        maskT = cbs[:, CB_MASK:CB_MASK + 2 * NJ * 64].rearrange("p (t j q) -> p t j q", t=2, j=NJ)

        class Arena:
            def __init__(self):
                self.off = 0

            def reset(self):
                tr.barrier()
                self.off = 0

            def bf(self, n):
                ap = arena[:, self.off:self.off + n]
                self.off += n
                assert self.off <= ARENA_N, self.off
                return ap

            def f32(self, n):
                ap = arena[:, self.off:self.off + 2 * n].bitcast(F32)
                self.off += 2 * n
                assert self.off <= ARENA_N, self.off
                return ap

        ar = Arena()

        wsl = [Buf(f"w{i}") for i in range(3)]
        wds = [tr.dsem(f"w{i}") for i in range(3)]
        pieces = []

        def w3(slot, kc, n):
            return slot[:, 0:kc * n].rearrange("p (k n) -> p k n", k=kc)

        def piece_cols(tag, src2d, cols):
            specs = []
            for (do, c0, n) in cols:
                specs.append((lambda sl, do=do, n=n: w3(sl, 8, 512)[:, :, do:do + n],
                              src2d[:, c0:c0 + n].rearrange("(k p) n -> p k n", p=128)))
            pieces.append((tag, specs))

        def layer_pieces(l):
            win = dwin[l]
            for pi in range(4):
                piece_cols(f"ada{l}_{pi}", dwada[l], [(0, pi * 512, 512)])
            for j in range(4):
                piece_cols(f"A{l}_{j}", win, [(0, j * 128, 128), (128, 512 + j * 128, 128), (256, 1024 + j * 128, 128)])
            merge_pieces(l, 0)
            for hp in range(4):
                piece_cols(f"B{l}_{hp}", win, [(0, 1536 + hp * 128, 128), (128, 2048 + hp * 128, 128),
                                               (256, 2560 + hp * 128, 128), (384, 3072 + hp * 128, 128)])
            merge_pieces(l, 1)
            for cp in range(4):
                n = cp // 2
                piece_cols(f"C{l}_{cp}", win, [(0, 3584 + cp * 128, 128), (128, 4096 + n * 64, 64), (192, 4096 + n * 64, 64),
                                               (256, 4224 + n * 64, 64), (384, 4352 + cp * 128, 128)])
            merge_pieces(l, 2)
            for hd in range(4):
                piece_cols(f"D{l}_{hd}", win, [(0, 4864 + hd * 128, 128), (128, 5376 + hd * 128, 128),
                                               (256, 5888 + hd * 128, 128), (384, 6400 + hd * 128, 128)])
            merge_pieces(l, 3)
            for pi in range(2):
                piece_cols(f"adag{l}_{pi}", dwada[l], [(0, 2048 + pi * 512, 512)])
            for ph in range(2):
                piece_cols(f"wo{l}_{ph}", dwout[l], [(0, ph * 512, 512)])

        def merge_pieces(l, i):
            for hf in range(2):
                pieces.append((f"br{l}_{i}_{hf}", [(lambda sl: w3(sl, 4, 1024),
                                                    dwbr[l, i].rearrange("(k p) n -> p k n", p=128))]))
                piece_cols(f"lg{l}_{i}_{hf}", dwin[l], [(0, 6912 + i * 1024 + hf * 512, 512)])

        for l in range(NL):
            layer_pieces(l)
        wstate = {"issued": 0, "next": 0}

        def w_issue(upto):
            while wstate["issued"] < min(upto, len(pieces)):
                i = wstate["issued"]
                tag, specs = pieces[i]
                sl = wring_t[i % 3]
                tr.dma(pool, wds[i % 3], [DMA(f(sl), src) for (f, src) in specs], writes=[wsl[i % 3]])
                wstate["issued"] += 1

        def wget(tag):
            i = wstate["next"]
            assert pieces[i][0] == tag, (pieces[i][0], tag)
            w_issue(i + 2)
            wstate["next"] += 1
            return wring_t[i % 3], wsl[i % 3]

        d_init = tr.dsem("init")
        tr.dma(sp, d_init, [DMA(cfs[:, :], dcf), DMA(vecs[:, :], dvecs), DMA(small[:, 0:16], dcT)],
               writes=[constB, vecB, smallB])
        d_init2 = tr.dsem("init2")
        tr.dma(pool, d_init2, [DMA(cbs[:, :], dcb)], writes=[constB])
        w_issue(2)

        def vcol(l, off, n=1):
            return vecs[:, l * NVL + off: l * NVL + off + n]

        xld = [tr.dsem(f"xld{i}") for i in range(3)]
        std = [tr.dsem(f"st{i}") for i in range(2)]

        def run_layer(l, need_ctx):
            lam_init = 0.8 - 0.6 * math.exp(-0.3 * l)
            qtiles = TILES if need_ctx else TILES[:4]

            ar.reset()
            tr.op(act, ACTF(s2[:, :, 0], small[:, 0:8], AF.Silu), reads=[smallB], writes=[s2B])
            tr.op(act, ACTF(s2[:, :, 1], small[:, 8:16], AF.Silu), reads=[smallB], writes=[s2B])
            pm = psum[7]
            for pi in range(4):
                wsl_ap, wb = wget(f"ada{l}_{pi}")
                w = w3(wsl_ap, 8, 512)
                for fc in range(4):
                    g = pi * 4 + fc
                    tr.group(pe, [MM(pm[:, g * 2:g * 2 + 2], w[:, kc, fc * 128:(fc + 1) * 128], s2[:, kc, :],
                                     start=(kc == 0), stop=(kc == 7)) for kc in range(8)],
                             reads=[wb, s2B], writes=[pb[7]])
            pmv = pm[:, 0:32].rearrange("p (f w) -> p f w", w=2)
            tmp8 = small[:, 16:24]
            for which in range(2):
                tr.op(dve, TT(modv[:, 2 * which + 1, :], pmv[:, 0:8, which], vcol(l, V_BSH, 8), ALU.add),
                      reads=[pb[7], vecB], writes=[modB])
                tr.op(dve, TT(tmp8, pmv[:, 8:16, which], vcol(l, V_BSC, 8), ALU.add),
                      reads=[pb[7], vecB], writes=[smallB])
                tr.op(dve, STT(modv[:, 2 * which, :], tmp8, 1.0, vcol(l, V_G, 8), ALU.add, ALU.mult),
                      reads=[smallB, vecB], writes=[modB])
            lt = small[:, 24:28]
            prod = ar.f32(64)
            prodB = Buf("prod")
            for k in range(2):
                tr.op(dve, TT(prod, vcol(l, V_L + 128 * k, 64), vcol(l, V_L + 128 * k + 64, 64), ALU.mult),
                      reads=[vecB], writes=[prodB])
                tr.op(dve, MSET(lt[:, k:k + 1], 0.0), writes=[smallB])
                tr.op(act, ACTF(prod, prod, AF.Identity, accum_out=lt[:, k:k + 1]), reads=[prodB, smallB], writes=[prodB, smallB])
                tr.op(act, ACTF(lt[:, k:k + 1], lt[:, k:k + 1], AF.Exp), reads=[smallB], writes=[smallB])
            neglam = small[:, 28:29]
            gsub = small[:, 29:30]
            epsc = small[:, 30:31]
            tr.op(dve, MSET(epsc, EPS), writes=[smallB])
            tr.op(dve, TT(lt[:, 2:3], lt[:, 0:1], lt[:, 1:2], ALU.subtract), reads=[smallB], writes=[smallB])
            tr.op(dve, TS(neglam, lt[:, 2:3], lam_init, -1.0, ALU.add, ALU.mult), reads=[smallB], writes=[smallB])
            tr.op(dve, TS(gsub, vcol(l, V_SUB), 1.0 - lam_init), reads=[vecB], writes=[smallB])

            ar.reset()
            xt_r = Ring([ar.f32(1024) for _ in range(3)], "xt")
            xn_r = Ring([ar.f32(1024) for _ in range(3)], "xn")
            junk = ar.bf(1024)
            junkB = Buf("junk")
            st_r = Ring([small[:, 32 + 4 * i: 36 + 4 * i] for i in range(3)], "st")
            for i in range(18):
                xt, xtB = xt_r.next()
                xn, xnB = xn_r.next()
                stt_, stB = st_r.next()
                if l == 0:
                    src = dx[i * 128:(i + 1) * 128, :] if i < 16 else dctx[(i - 16) * 128:(i - 15) * 128, :]
                    rd = []
                else:
                    src = dx1[i * 128:(i + 1) * 128, :]
                    rd = [x1B[i]]
                tr.dma(sp, xld[i % 3], [DMA(xt, src)], reads=rd, writes=[xtB])
                tr.op(dve, MSET(stt_[:, 0:1], 0.0), writes=[stB])
                tr.op(act, ACTF(junk, xt, AF.Square, accum_out=stt_[:, 0:1]), reads=[xtB, stB], writes=[junkB, stB])
                tr.op(act, ACTF(stt_[:, 1:2], stt_[:, 0:1], AF.Sqrt, scale=1.0 / DM, bias=EPS), reads=[stB], writes=[stB])
                tr.op(dve, RCP(stt_[:, 2:3], stt_[:, 1:2]), reads=[stB], writes=[stB])
                tr.op(dve, TS(xn, xt, stt_[:, 2:3]), reads=[xtB, stB], writes=[xnB])
                which = 0 if i < 16 else 1
                for hb in range(2):
                    bk = (2 * i + hb) % 8
                    tr.group(pe, [TRN(psum[bk][:, k4 * 128:(k4 + 1) * 128], xn[:, (hb * 4 + k4) * 128:(hb * 4 + k4 + 1) * 128], identf)
                                  for k4 in range(4)], reads=[xnB, constB], writes=[pb[bk]])
                    for k4 in range(4):
                        kc = hb * 4 + k4
                        dst = hT[:, kc, i * 128:(i + 1) * 128]
                        srcp = psum[bk][:, k4 * 128:(k4 + 1) * 128]
                        A = modv[:, 2 * which, kc:kc + 1]
                        Bc = modv[:, 2 * which + 1, kc:kc + 1]
                        if k4 % 2 == 0:
                            tr.op(dve, TS(dst, srcp, A, Bc, ALU.mult, ALU.add), reads=[pb[bk], modB], writes=[hTB[i // 4]])
                        else:
                            tr.op(act, ACTF(dst, srcp, AF.Identity, scale=A, bias=Bc), reads=[pb[bk], modB], writes=[hTB[i // 4]])

            def proj_fm(ps_i, w, col0, t0, n, wb, t5):
                tr.group(pe, [MM(psum[ps_i][:, 0:n], w[:, kc, col0:col0 + 128], hT[:, kc, t0:t0 + n],
                                 start=(kc == 0), stop=(kc == 7)) for kc in range(8)],
                         reads=[wb, hTB[t5]], writes=[pb[ps_i]])

            def merge_branch(i):
                ar.reset()
                g_r = Ring([ar.f32(512) for _ in range(4)], "G")
                t_r = Ring([ar.f32(512) for _ in range(4)], "mt")
                cnt = 0
                for hf in range(2):
                    wbr_ap, wbrB = wget(f"br{l}_{i}_{hf}")
                    wbr = w3(wbr_ap, 4, 1024)
                    wl_ap, wlB = wget(f"lg{l}_{i}_{hf}")
                    wl = w3(wl_ap, 8, 512)
                    for fcl in range(4):
                        fc = hf * 4 + fcl
                        for t5, (t0, n) in enumerate(qtiles):
                            pz = (2 * cnt) % 8
                            pl = (2 * cnt + 1) % 8
                            cnt += 1
                            fz = [MM(psum[pz][:, 0:n], wbr[:, kc, fc * 128:(fc + 1) * 128], yT[:, kc, t0:t0 + n],
                                     start=(kc == 0), stop=(kc == 3)) for kc in range(4)]
                            fl = [MM(psum[pl][:, 0:n], wl[:, kc, fcl * 128:(fcl + 1) * 128], hT[:, kc, t0:t0 + n],
                                     start=(kc == 0), stop=(kc == 7)) for kc in range(8)]
                            tr.group(pe, fz + fl, reads=[wbrB, wlB, hTB[t5]] + [yTB[kc][t5] for kc in range(4)],
                                     writes=[pb[pz], pb[pl]])
                            G, GB = g_r.next()
                            tr.op(act, ACTF(G[:, 0:n], psum[pl][:, 0:n], AF.Sigmoid, bias=vcol(l, V_BM + i * 8 + fc)),
                                  reads=[pb[pl], vecB], writes=[GB])
                            if i == 0:
                                tr.op(dve, TT(mT[:, fc, t0:t0 + n], psum[pz][:, 0:n], G[:, 0:n], ALU.mult),
                                      reads=[pb[pz], GB], writes=[mTB[fc][t5]])
                            else:
                                tm, tmB = t_r.next()
                                tr.op(dve, TT(tm[:, 0:n], psum[pz][:, 0:n], G[:, 0:n], ALU.mult), reads=[pb[pz], GB], writes=[tmB])
                                tr.op(pool, TT(mT[:, fc, t0:t0 + n], mT[:, fc, t0:t0 + n], tm[:, 0:n], ALU.add),
                                      reads=[tmB], writes=[mTB[fc][t5]])

            def dump(i):
                if debug:
                    dd = tr.dsem(f"dbg{l}_{i}")
                    tr.dma(pool, dd, [DMA(ddbg[l * 4 + i].rearrange("c p t -> p c t"), yT[:, :, :])],
                           reads=[yTB[c][t] for c in range(4) for t in range(5)], writes=[outB])

            ar.reset()
            HG = 2364
            hglu_r = Ring([ar.bf(HG) for _ in range(2)], "hglu")
            diag_ap = ar.bf(31 * 128).rearrange("p (k n) -> p k n", k=31)
            diagB = Buf("diag")
            sg_r = Ring([ar.f32(512) for _ in range(2)], "sg")
            cbuf = mT[:, :, :].rearrange("p c t -> p (c t)").bitcast(F32).rearrange("p (c t) -> p c t", c=4)

            def cB(j):
                return [mTB[2 * j][t] for t in range(5)] + [mTB[2 * j + 1][t] for t in range(5)]
            segs = [(0, 0, 512), (512, 512, 512), (1024, 1024, 512), (1536, 1536, 512), (2078, 2048, 256)]
            segs = segs if need_ctx else segs[:4]
            for j in range(4):
                w_ap, wb = wget(f"A{l}_{j}")
                w = w3(w_ap, 8, 512)
                for k in range(31):
                    tr.op(dve, TS(diag_ap[:, k, :], ident, vcol(l, V_CW + j * 31 + k)), reads=[constB, vecB], writes=[diagB])
                hg, hgB = hglu_r.next()
                for (a, b) in ((0, 15), (2063, 2093), (2349, 2364)):
                    tr.op(pool, MSET(hg[:, a:b], 0.0), writes=[hgB])
                for t5, (bb, t0, n) in enumerate(segs):
                    proj_fm(0 + (t5 % 2) * 2, w, 0, t0, n, wb, t5)
                    proj_fm(1 + (t5 % 2) * 2, w, 128, t0, n, wb, t5)
                    pa, pg = (t5 % 2) * 2, 1 + (t5 % 2) * 2
                    sg, sgB = sg_r.next()
                    tr.op(act, ACTF(sg[:, 0:n], psum[pg][:, 0:n], AF.Sigmoid), reads=[pb[pg]], writes=[sgB])
                    tr.op(dve, TT(hg[:, bb + 15:bb + 15 + n], psum[pa][:, 0:n], sg[:, 0:n], ALU.mult),
                          reads=[pb[pa], sgB], writes=[hgB])
                for t5, (bb, t0, n) in enumerate(segs):
                    pc = 4 + (t5 % 2)
                    tr.group(pe, [MM(psum[pc][:, 0:n], diag_ap[:, k, :], hg[:, bb + k:bb + k + n], start=(k == 0), stop=(k == 30))
                                  for k in range(31)], reads=[diagB, hgB], writes=[pb[pc]])
                    tr.op(act, ACTF(cbuf[:, j, t0:t0 + n], psum[pc][:, 0:n], AF.Identity, bias=vcol(l, V_CB + j)),
                          reads=[pb[pc], vecB], writes=cB(j))
                    pgt = 6 + (t5 % 2)
                    proj_fm(pgt, w, 256, t0, n, wb, t5)
                    tr.op(act, ACTF(yT[:, j, t0:t0 + n], psum[pgt][:, 0:n], AF.Silu), reads=[pb[pgt]], writes=[yTB[j][t5]])
            ar.reset()
            onesf = ar.f32(128)
            onesfB = Buf("onesf")
            tr.op(dve, MSET(onesf, 1.0), writes=[onesfB])
            sq_r = Ring([ar.f32(512) for _ in range(2)], "csq")
            mean = ar.f32(512)
            rstd = ar.f32(512)
            msq = ar.f32(512)
            stB2 = Buf("lnstat")
            d_r = Ring([ar.f32(512) for _ in range(2)], "lnd")
            allc = [b for j in range(4) for b in cB(j)]
            for t5, (t0, n) in enumerate(qtiles):
                tr.group(pe, [MM(psum[0][:, 0:n], onesf, cbuf[:, j, t0:t0 + n], start=(j == 0), stop=(j == 3)) for j in range(4)],
                         reads=allc + [onesfB], writes=[pb[0]])
                sqs = []
                for j in range(4):
                    sq, sqB = sq_r.next()
                    tr.op(act, ACTF(sq[:, 0:n], cbuf[:, j, t0:t0 + n], AF.Square), reads=allc, writes=[sqB])
                    tr.group(pe, [MM(psum[1][:, 0:n], onesf, sq[:, 0:n], start=(j == 0), stop=(j == 3))],
                             reads=[sqB, onesfB], writes=[pb[1]])
                tr.op(dve, TS(mean[:, 0:n], psum[0][:, 0:n], 1.0 / 512), reads=[pb[0]], writes=[stB2])
                tr.op(dve, TT(msq[:, 0:n], mean[:, 0:n], mean[:, 0:n], ALU.mult), reads=[stB2], writes=[stB2])
                tr.op(dve, STT(msq[:, 0:n], psum[1][:, 0:n], 1.0 / 512, msq[:, 0:n], ALU.mult, ALU.subtract),
                      reads=[pb[1], stB2], writes=[stB2])
                tr.op(act, ACTF(msq[:, 0:n], msq[:, 0:n], AF.Sqrt, bias=EPS, scale=1.0), reads=[stB2], writes=[stB2])
                tr.op(dve, RCP(rstd[:, 0:n], msq[:, 0:n]), reads=[stB2], writes=[stB2])
                for j in range(4):
                    d, dB = d_r.next()
                    tr.op(dve, TT(d[:, 0:n], cbuf[:, j, t0:t0 + n], mean[:, 0:n], ALU.subtract), reads=allc + [stB2], writes=[dB])
                    tr.op(dve, TT(d[:, 0:n], d[:, 0:n], rstd[:, 0:n], ALU.mult), reads=[stB2], writes=[dB])
                    tr.op(act, ACTF(d[:, 0:n], d[:, 0:n], AF.Silu, scale=vcol(l, V_LG + j), bias=vcol(l, V_LB + j)),
                          reads=[vecB], writes=[dB])
                    tr.op(dve, TT(yT[:, j, t0:t0 + n], yT[:, j, t0:t0 + n], d[:, 0:n], ALU.mult), reads=[dB], writes=[yTB[j][t5]])
            dump(0)
            merge_branch(0)

            def attn_branch(kind):
                ar.reset()
                QT = ar.bf(2 * T).rearrange("p (h t) -> p h t", h=2)
                KT = ar.bf(T)
                GT = ar.bf(T)
                Vp = ar.bf(18 * 256).rearrange("p (k n) -> p k n", k=18)
                QTB = [Buf(f"QT{t}") for t in range(5)]
                KTB = [Buf(f"KT{t}") for t in range(5)]
                GTB = [Buf(f"GT{t}") for t in range(5)]
                VB = [Buf(f"V{g}") for g in range(5)]
                sq_r = Ring([ar.bf(512) for _ in range(2)], "sq")
                rs_r = Ring([ar.f32(512) for _ in range(2)], "rs")
                xn_r2 = Ring([ar.bf(512) for _ in range(2)], "xnb")
                t1_r = Ring([ar.f32(512) for _ in range(1)], "t1")
                t2_r = Ring([ar.f32(512) for _ in range(1)], "t2")
                P_r = Ring([ar.bf(512) for _ in range(4 if kind == "B" else 6)], "P")
                ya_r = Ring([ar.f32(512) for _ in range(1)], "ya")
                yb_r = Ring([ar.f32(512) for _ in range(1)], "yb")
                if kind == "B":
                    tab = ar.bf(2 * NJ * 64).rearrange("p (h n) -> p h n", h=2)
                    tab64 = ar.bf(2 * NJ * 64).rearrange("p (t j q) -> p t j q", t=2, j=NJ)
                    rt2 = ar.bf(64)
                    rp = ar.f32(32)
                    tabB = [Buf("tab0"), Buf("tab1")]
                    tab64B, rt2B, rpB = Buf("tab64"), Buf("rt2"), Buf("rp")
                    rpd = tr.dsem(f"rp{l}")
                tr.op(pool, MSET(QT[:, :, :], 0.0), writes=QTB)
                if kind != "D":
                    tr.op(pool, MSET(Vp[:, :, 64:128], 1.0), writes=VB)
                    tr.op(pool, MSET(Vp[:, :, 192:256], 1.0), writes=VB)
                rope = kind != "B"
                qg = {"B": V_NAQ, "C": V_GQ, "D": V_DQ}[kind]
                kg = {"B": V_NAK, "C": V_GK, "D": V_DK}[kind]

                def normed(psi, dst, dstB, gcol, t0, n, do_rope):
                    sq, sqB = sq_r.next()
                    tr.op(act, ACTF(sq[:, 0:n], psum[psi][:, 0:n], AF.Square), reads=[pb[psi]], writes=[sqB])
                    tr.group(pe, [MM(psum[5][:, 0:n], blk, sq[:, 0:n])], reads=[sqB, constB], writes=[pb[5]])
                    rs, rsB = rs_r.next()
                    tr.op(act, ACTF(rs[:, 0:n], psum[5][:, 0:n], AF.Sqrt, scale=1.0 / 64, bias=EPS), reads=[pb[5]], writes=[rsB])
                    tr.op(dve, RCP(rs[:, 0:n], rs[:, 0:n]), reads=[rsB], writes=[rsB])
                    if not do_rope:
                        for (r0, r1, d_ap) in dst:
                            tr.op(dve, STT(d_ap, psum[psi][r0:r1, 0:n], vecs[r0:r1, l * NVL + gcol:l * NVL + gcol + 1], rs[r0:r1, 0:n],
                                           ALU.mult, ALU.mult), reads=[pb[psi], rsB, vecB], writes=[dstB])
                        return
                    xb, xbB = xn_r2.next()
                    tr.op(dve, STT(xb[:, 0:n], psum[psi][:, 0:n], vcol(l, gcol), rs[:, 0:n], ALU.mult, ALU.mult),
                          reads=[pb[psi], rsB, vecB], writes=[xbB])
                    tr.group(pe, [MM(psum[6][:, 0:n], perm, xb[:, 0:n])], reads=[xbB, constB], writes=[pb[6]])
                    t1, t1B = t1_r.next()
                    t2, t2B = t2_r.next()
                    tr.op(dve, TT(t1[:, 0:n], xb[:, 0:n], COS[:, t0:t0 + n], ALU.mult), reads=[xbB, constB], writes=[t1B])
                    tr.op(dve, TT(t2[:, 0:n], psum[6][:, 0:n], SIN[:, t0:t0 + n], ALU.mult), reads=[pb[6], constB], writes=[t2B])
                    for (r0, r1, d_ap) in dst:
                        tr.op(dve, TT(d_ap, t1[r0:r1, 0:n], t2[r0:r1, 0:n], ALU.add), reads=[t1B, t2B], writes=[dstB])

                for pr in range(4):
                    w_ap, wb = wget(f"{kind}{l}_{pr}")
                    w = w3(w_ap, 8, 512)
                    if kind == "B":
                        for h2 in range(2):
                            h = 2 * pr + h2
                            tr.dma(sp, rpd, [DMA(rp[0:15, 0:31], drpb[l, h])], writes=[rpB])
                            tr.group(pe, [MM(psum[7][0:31, 0:2 * NJ], rp[0:15, 0:31], selR[0:15, 0:2 * NJ])],
                                     reads=[rpB, constB], writes=[pb[7]])
                            tr.op(dve, CP(rt2[0:31, 0:2 * NJ], psum[7][0:31, 0:2 * NJ]), reads=[pb[7]], writes=[rt2B])
                            for g in range(8):
                                bk = g % 4
                                tr.group(pe, [MM(psum[bk][0:64, q8 * 2 * NJ:(q8 + 1) * 2 * NJ],
                                                 band[0:31, 63 - (8 * g + q8):127 - (8 * g + q8)], rt2[0:31, 0:2 * NJ])
                                              for q8 in range(8)], reads=[rt2B, constB], writes=[pb[bk]])
                                pv = psum[bk][0:64, 0:8 * 2 * NJ].rearrange("p (q t j) -> p t j q", q=8, t=2)
                                for hf in range(2):
                                    tr.op(dve, TT(tab64[0:64, hf, :, 8 * g:8 * g + 8], pv[:, hf], maskT[0:64, hf, :, 8 * g:8 * g + 8], ALU.add),
                                          reads=[pb[bk], constB], writes=[tab64B])
                            tr.op(dve, CP(tab[0:64, h2, :], tab64[0:64, 0].rearrange("p j q -> p (j q)")), reads=[tab64B], writes=[tabB[h2]])
                            tr.op(dve, CP(tab[64:128, h2, :], tab64[0:64, 1].rearrange("p j q -> p (j q)")), reads=[tab64B], writes=[tabB[h2]])
                    items = []
                    cnt = [0]

                    def qk_item(col0, dst, dstB, gcol, t0, n, t5, do_rope):
                        psi = cnt[0] % 4
                        cnt[0] += 1
                        st = {}

                        def s0():
                            proj_fm(psi, w, col0, t0, n, wb, t5)
                            sq, sqB = sq_r.next()
                            st["sq"] = (sq, sqB)
                            tr.op(act, ACTF(sq[:, 0:n], psum[psi][:, 0:n], AF.Square), reads=[pb[psi]], writes=[sqB])

                        def s1():
                            sq, sqB = st["sq"]
                            tr.group(pe, [MM(psum[5][:, 0:n], blk, sq[:, 0:n])], reads=[sqB, constB], writes=[pb[5]])
                            rs, rsB = rs_r.next()
                            tr.op(act, ACTF(rs[:, 0:n], psum[5][:, 0:n], AF.Sqrt, scale=1.0 / 64, bias=EPS), reads=[pb[5]], writes=[rsB])
                            tr.op(dve, RCP(rs[:, 0:n], rs[:, 0:n]), reads=[rsB], writes=[rsB])
                            if not do_rope:
                                for (r0, r1, d_ap) in dst:
                                    tr.op(dve, STT(d_ap, psum[psi][r0:r1, 0:n], vecs[r0:r1, l * NVL + gcol:l * NVL + gcol + 1],
                                                   rs[r0:r1, 0:n], ALU.mult, ALU.mult), reads=[pb[psi], rsB, vecB], writes=[dstB])
                                return
                            xb, xbB = xn_r2.next()
                            st["xb"] = (xb, xbB)
                            tr.op(dve, STT(xb[:, 0:n], psum[psi][:, 0:n], vcol(l, gcol), rs[:, 0:n], ALU.mult, ALU.mult),
                                  reads=[pb[psi], rsB, vecB], writes=[xbB])

                        def s2():
                            if not do_rope:
                                return
                            xb, xbB = st["xb"]
                            tr.group(pe, [MM(psum[6][:, 0:n], perm, xb[:, 0:n])], reads=[xbB, constB], writes=[pb[6]])
                            t1, t1B = t1_r.next()
                            t2, t2B = t2_r.next()
                            tr.op(dve, TT(t1[:, 0:n], xb[:, 0:n], COS[:, t0:t0 + n], ALU.mult), reads=[xbB, constB], writes=[t1B])
                            tr.op(dve, TT(t2[:, 0:n], psum[6][:, 0:n], SIN[:, t0:t0 + n], ALU.mult), reads=[pb[6], constB], writes=[t2B])
                            for (r0, r1, d_ap) in dst:
                                tr.op(dve, TT(d_ap, t1[r0:r1, 0:n], t2[r0:r1, 0:n], ALU.add), reads=[t1B, t2B], writes=[dstB])
                        return [s0, s1, s2]

                    def gate_item(t0, n, t5):
                        psi = cnt[0] % 4
                        cnt[0] += 1

                        def s0():
                            proj_fm(psi, w, 384, t0, n, wb, t5)
                            tr.op(act, ACTF(GT[:, t0:t0 + n], psum[psi][:, 0:n], AF.Silu), reads=[pb[psi]], writes=[GTB[t5]])
                        return [s0]

                    nv = 64 if kind == "C" else 128

                    def v_item(g5):
                        psi = cnt[0] % 4
                        cnt[0] += 1

                        def s0():
                            kts = list(range(4 * g5, min(4 * g5 + 4, 18)))
                            fns = []
                            for ii, kt in enumerate(kts):
                                for kc in range(8):
                                    fns.append(MM(psum[psi][:, ii * 128:ii * 128 + nv], hT[:, kc, kt * 128:(kt + 1) * 128],
                                                  w[:, kc, 256:256 + nv], start=(kc == 0), stop=(kc == 7)))
                            tr.group(pe, fns, reads=[wb, hTB[g5]], writes=[pb[psi]])
                            nk = len(kts)
                            pvv = psum[psi][:, 0:nk * 128].rearrange("p (k n) -> p k n", k=nk)
                            k0 = kts[0]
                            if kind == "B":
                                tr.op(act, ACTF(Vp[:, k0:k0 + nk, 0:64], pvv[:, :, 0:64], AF.Copy), reads=[pb[psi]], writes=[VB[g5]])
                                tr.op(dve, CP(Vp[:, k0:k0 + nk, 128:192], pvv[:, :, 64:128]), reads=[pb[psi]], writes=[VB[g5]])
                            elif kind == "C":
                                tr.op(act, ACTF(Vp[:, k0:k0 + nk, 0:64], pvv[:, :, 0:64], AF.Copy), reads=[pb[psi]], writes=[VB[g5]])
                            else:
                                tr.op(act, ACTF(Vp[:, k0:k0 + nk, 0:128], pvv[:, :, 0:128], AF.Copy), reads=[pb[psi]], writes=[VB[g5]])
                        return [s0]

                    qk_items = []
                    for t5, (t0, n) in enumerate(qtiles):
                        qk_items.append(qk_item(0, [(0, 64, QT[0:64, 0, t0:t0 + n]), (64, 128, QT[64:128, 1, t0:t0 + n])], QTB[t5], qg,
                                                t0, n, t5, rope and t5 < 4))
                    for t5, (t0, n) in enumerate(TILES):
                        qk_items.append(qk_item(128, [(0, 128, KT[:, t0:t0 + n])], KTB[t5], kg, t0, n, t5, rope and t5 < 4))
                    items = [gate_item(t0, n, t5) for t5, (t0, n) in enumerate(qtiles)] + qk_items + [v_item(g5) for g5 in range(5)]
                    KST = 3
                    for step in range(len(items) + KST - 1):
                        for k in range(KST):
                            i = step - k
                            if 0 <= i < len(items) and k < len(items[i]):
                                items[i][k]()

                    QW = 256

                    def both(ap512, qa, qb):
                        if qa == 0 and qb == QW:
                            return ap512
                        return ap512.rearrange("p (h q) -> p h q", h=2)[:, :, qa:qb]

                    qts = []
                    if kind == "B":
                        for r_lo in range(0, 32, 4):
                            ch = [(16, 0, QW, None), (17, 0, QW, None)]
                            if r_lo in (0, 28):
                                a0 = 0 if r_lo == 0 else 12
                                for a in range(a0, a0 + 4):
                                    ch.append((a, 0, QW, (10 + 7 - 2 * a + r_lo) * 64))
                            else:
                                for a in range(16):
                                    rs_ = max(r_lo, 2 * a - 3)
                                    re_ = min(r_lo + 3, 2 * a + 5)
                                    if rs_ <= re_:
                                        ch.append((a, (rs_ - r_lo) * 64, (re_ - r_lo + 1) * 64, (4 - 2 * a + rs_) * 64))
                            qts.append((r_lo * 64, ch))
                    else:
                        for q0 in range(0, S, QW):
                            qts.append((q0, [(kt, 0, QW, None) for kt in range(18)]))
                    if need_ctx:
                        qts.append((2048, [(16, 0, QW, None), (17, 0, QW, None)]))

                    NCH = 2
                    nacc = {"B": 2, "C": 1, "D": 2}[kind]
                    steps = []
                    pending_b = [None]
                    for ti, (q0, ch) in enumerate(qts):
                        t5 = min(q0 // 512, 4)
                        if nacc == 2:
                            accs = [(ti % 2) * 2, (ti % 2) * 2 + 1]
                        else:
                            accs = [ti % 2]
                        nch = len(ch)
                        for si_, c0 in enumerate(range(0, nch, NCH)):
                            subs = []
                            for ci in range(c0, min(c0 + NCH, nch)):
                                kt, qa, qb, bcol = ch[ci]
                                subs.append(dict(kt=kt, qa=qa, qb=qb, bcol=bcol, first=(ci == 0), last=(ci == nch - 1)))
                            steps.append(dict(q0=q0, t5=t5, accs=accs, subs=subs))
                            if si_ == 2 and pending_b[0] is not None:
                                steps.append(pending_b[0])
                                pending_b[0] = None
                        if pending_b[0] is not None:
                            steps.append(pending_b[0])
                            pending_b[0] = None
                        ea = dict(epi=True, part="a", q0=q0, t5=t5, accs=accs)
                        steps.append(ea)
                        if kind == "D":
                            pending_b[0] = dict(epi=True, part="b", ref=ea, q0=q0, t5=t5, accs=accs)
                    if pending_b[0] is not None:
                        steps.append(pending_b[0])
                        pending_b[0] = None

                    sring = [2, 3, 4, 5, 6, 7] if kind == "C" else [4, 5, 6, 7]
                    scount = [0]

                    def do_qk(st):
                        q0, t5 = st["q0"], st["t5"]
                        fns = []
                        rd = [QTB[t5], constB]
                        wr = []
                        for sub in st["subs"]:
                            si = sring[scount[0] % len(sring)]
                            scount[0] += 1
                            sub["si"] = si
                            rd.append(KTB[sub["kt"] // 4])
                            wr.append(pb[si])

                        for sub in st["subs"]:
                            si, kt, qa, qb, bcol = sub["si"], sub["kt"], sub["qa"], sub["qb"], sub["bcol"]
                            o3 = psum[si][:, 0:512].rearrange("p (h q) -> p h q", h=2)[:, :, qa:qb]
                            fns.append(MM(o3, KT[:, kt * 128:(kt + 1) * 128], QT[:, :, q0 + qa:q0 + qb], start=True, stop=(bcol is None)))
                            if bcol is not None:
                                fns.append(MM(o3, ident, tab[:, :, bcol:bcol + (qb - qa)], start=False, stop=True))
                                rd += tabB
                        tr.group(pe, fns, reads=rd, writes=wr)
                        for sub in st["subs"]:
                            P, PB = P_r.next()
                            sub["P"], sub["PB"] = P, PB
                            si, qa, qb = sub["si"], sub["qa"], sub["qb"]
                            tr.op(act, ACTF(both(P[:, 0:512], qa, qb), both(psum[si][:, 0:512], qa, qb), AF.Exp, scale=0.125),
                                  reads=[pb[si]], writes=[PB])

                    def do_pv(st):
                        accs = st["accs"]
                        fns = []
                        rd = [constB]
                        for sub in st["subs"]:
                            kt, qa, qb, P = sub["kt"], sub["qa"], sub["qb"], sub["P"]
                            rd += [sub["PB"], VB[kt // 4]]
                            f, la = sub["first"], sub["last"]
                            if kind == "B":
                                for hf in range(2):
                                    fns.append(MM(psum[accs[hf]][:, qa:qb], Vp[:, kt, hf * 128:(hf + 1) * 128],
                                                  P[:, hf * QW + qa:hf * QW + qb], start=f, stop=la))
                            elif kind == "C":
                                fns.append(MM(psum[accs[0]][:, 0:512], Vp[:, kt, 0:128], P[:, 0:512], start=f, stop=la))
                            else:
                                fns.append(MM(psum[accs[0]][:, 0:512], Vp[:, kt, 0:128], P[:, 0:512], start=f, stop=la))
                                fns.append(MM(psum[accs[1]][:, 0:512], ones, P[:, 0:512], start=f, stop=la))
                        tr.group(pe, fns, reads=rd, writes=[pb[a] for a in accs])

                    def do_epi(st):
                        q0, accs, t5 = st["q0"], st["accs"], st["t5"]
                        qn = QW
                        ywr = [yTB[pr][t5]]
                        grd = [GTB[t5]]
                        ya, yaB = ya_r.next()
                        yb, ybB = yb_r.next()
                        if kind == "B":
                            for hf in range(2):
                                o = accs[hf]
                                tr.op(dve, RCP(ya[0:64, hf * QW:hf * QW + qn], psum[o][64:128, 0:qn]), reads=[pb[o]], writes=[yaB])
                                tr.op(dve, TT(yb[hf * 64:(hf + 1) * 64, 0:qn], psum[o][0:64, 0:qn], ya[0:64, hf * QW:hf * QW + qn], ALU.mult),
                                      reads=[pb[o], yaB], writes=[ybB])
                            tr.op(dve, TT(yT[:, pr, q0:q0 + qn], yb[:, 0:qn], GT[:, q0:q0 + qn], ALU.mult),
                                  reads=[ybB] + grd, writes=ywr)
                        elif kind == "C":
                            o = accs[0]
                            tr.op(dve, RCP(ya[0:64, 0:512], psum[o][64:128, 0:512]), reads=[pb[o]], writes=[yaB])
                            for hf in range(2):
                                tr.op(dve, TT(yb[hf * 64:(hf + 1) * 64, 0:qn], psum[o][0:64, hf * QW:hf * QW + qn],
                                              ya[0:64, hf * QW:hf * QW + qn], ALU.mult), reads=[pb[o], yaB], writes=[ybB])
                            tr.op(dve, TT(yT[:, pr, q0:q0 + qn], yb[:, 0:qn], GT[:, q0:q0 + qn], ALU.mult),
                                  reads=[ybB] + grd, writes=ywr)
                        else:
                            o, dn = accs
                            tr.op(dve, RCP(ya[:, 0:512], psum[dn][:, 0:512]), reads=[pb[dn]], writes=[yaB])
                            tr.op(dve, TT(ya[:, 0:512], psum[o][:, 0:512], ya[:, 0:512], ALU.mult), reads=[pb[o]], writes=[yaB])
                            tr.op(dve, STT(yb[:, 0:qn], ya[:, QW:QW + qn], neglam, ya[:, 0:qn], ALU.mult, ALU.add),
                                  reads=[yaB, smallB], writes=[ybB])
                            sq, sqB = sq_r.next()
                            tr.op(act, ACTF(sq[:, 0:qn], yb[:, 0:qn], AF.Square), reads=[ybB], writes=[sqB])
                            st["bufs"] = (ya, yaB, yb, ybB, sq, sqB)

                    def do_epi_b(st):
                        q0, accs, t5 = st["q0"], st["accs"], st["t5"]
                        qn = QW
                        o, dn = accs
                        ya, yaB, yb, ybB, sq, sqB = st["ref"]["bufs"]
                        tr.group(pe, [MM(psum[dn][:, 0:qn], ones, sq[:, 0:qn])], reads=[sqB, constB], writes=[pb[dn]])
                        tr.op(act, ACTF(ya[:, 0:qn], psum[dn][:, 0:qn], AF.Ln, scale=1.0 / 128, bias=epsc), reads=[pb[dn], smallB], writes=[yaB])
                        tr.op(act, ACTF(ya[:, 0:qn], ya[:, 0:qn], AF.Exp, scale=-0.5), reads=[], writes=[yaB])
                        tr.op(dve, TT(yb[:, 0:qn], yb[:, 0:qn], ya[:, 0:qn], ALU.mult), reads=[yaB], writes=[ybB])
                        tr.op(dve, STT(yT[:, pr, q0:q0 + qn], yb[:, 0:qn], gsub, GT[:, q0:q0 + qn], ALU.mult, ALU.mult),
                              reads=[ybB, smallB, GTB[t5]], writes=[yTB[pr][t5]])

                    LOOK = 2 if kind == "C" else 1
                    qk_list = [s_ for s_ in steps if "epi" not in s_]
                    qi = 0
                    done = 0
                    for s_ in steps:
                        if "epi" in s_:
                            if s_["part"] == "a":
                                do_epi(s_)
                            else:
                                do_epi_b(s_)
                            continue
                        while qi < len(qk_list) and qi <= done + LOOK:
                            do_qk(qk_list[qi])
                            qi += 1
                        do_pv(s_)
                        done += 1

            attn_branch("B")
            dump(1)
            merge_branch(1)
            attn_branch("C")
            dump(2)
            merge_branch(2)
            attn_branch("D")
            dump(3)
            merge_branch(3)

            ar.reset()
            if debug and l == 0:
                ddm = tr.dsem("dbgm")
                tr.dma(pool, ddm, [DMA(ddbgm.rearrange("c p t -> p c t"), mT[:, :, :])],
                       reads=[mTB[c][t] for c in range(8) for t in range(5)], writes=[outB])
            screp = ar.bf(8 * 128).rearrange("p (k n) -> p k n", k=8)
            sccrep = ar.bf(8 * 128).rearrange("p (k n) -> p k n", k=8)
            repB = Buf("rep")
            for kc in range(8):
                tr.op(dve, CP(screp[:, kc, :], s2[:, kc, 0:1].to_broadcast([128, 128])), reads=[s2B], writes=[repB])
                tr.op(dve, CP(sccrep[:, kc, :], s2[:, kc, 1:2].to_broadcast([128, 128])), reads=[s2B], writes=[repB])
            tr.dma(pool, bgd, [DMA(bgrow[:, :], dbg_rows[:, l * DM:(l + 1) * DM])], writes=[bgB])
            gx = ar.f32(1024)
            gc = ar.f32(1024)
            gB = Buf("gates")
            for pi in range(2):
                w_ap, wb = wget(f"adag{l}_{pi}")
                w = w3(w_ap, 8, 512)
                for which, (rep, dst) in enumerate(((screp, gx), (sccrep, gc))):
                    if which == 1 and not need_ctx:
                        continue
                    psi = pi * 2 + which
                    fns = [MM(psum[psi][:, :], rep[:, kc, :], w[:, kc, :], start=(kc == 0), stop=False) for kc in range(8)]
                    fns.append(MM(psum[psi][:, :], ones[0:1, :], bgrow[0:1, pi * 512:(pi + 1) * 512], start=False, stop=True))
                    tr.group(pe, fns, reads=[wb, repB, constB, bgB], writes=[pb[psi]])
                    tr.op(act, ACTF(dst[:, pi * 512:(pi + 1) * 512], psum[psi][:, :], AF.Copy), reads=[pb[psi]], writes=[gB])
            if debug and l == 0:
                ddg = tr.dsem("dbgg")
                tr.dma(sp, ddg, [DMA(ddbgg[:, 0:1024], gx), DMA(ddbgg[:, 1024:2048], gc)], reads=[gB], writes=[outB])
            xo_r = Ring([ar.f32(512) for _ in range(2)], "xo")
            res_r = Ring([ar.f32(512) for _ in range(2)], "res")
            tm_r = Ring([ar.f32(512) for _ in range(2)], "otm")
            ntile = 18 if need_ctx else 16
            cnt = 0
            for ph in range(2):
                w_ap, wb = wget(f"wo{l}_{ph}")
                w = w3(w_ap, 8, 512)
                for i in range(ntile):
                    psi = 4 + cnt % 4
                    t5 = min(i // 4, 4)
                    tr.group(pe, [MM(psum[psi][:, :], mT[:, kc, i * 128:(i + 1) * 128], w[:, kc, :], start=(kc == 0), stop=(kc == 7))
                                  for kc in range(8)], reads=[wb] + [mTB[kc][t5] for kc in range(8)], writes=[pb[psi]])
                    xo, xoB = xo_r.next()
                    if l == 0:
                        src = dx[i * 128:(i + 1) * 128, ph * 512:(ph + 1) * 512] if i < 16 else \
                            dctx[(i - 16) * 128:(i - 15) * 128, ph * 512:(ph + 1) * 512]
                        rd = []
                    else:
                        src = dx1[i * 128:(i + 1) * 128, ph * 512:(ph + 1) * 512]
                        rd = [x1B[i]]
                    tr.dma(sp, xld[cnt % 2], [DMA(xo, src)], reads=rd, writes=[xoB])
                    gate = gx if i < 16 else gc
                    tm, tmB = tm_r.next()
                    rs_, rsB_ = res_r.next()
                    tr.op(dve, TT(tm, psum[psi][:, :], gate[:, ph * 512:(ph + 1) * 512], ALU.mult), reads=[pb[psi], gB], writes=[tmB])
                    tr.op(dve, TT(rs_, tm, xo, ALU.add), reads=[tmB, xoB], writes=[rsB_])
                    if l == NL - 1:
                        tr.dma(sp, std[cnt % 2], [DMA(dout[i * 128:(i + 1) * 128, ph * 512:(ph + 1) * 512], rs_)],
                               reads=[rsB_], writes=[outB])
                    else:
                        tr.dma(sp, std[cnt % 2], [DMA(dx1[i * 128:(i + 1) * 128, ph * 512:(ph + 1) * 512], rs_)],
                               reads=[rsB_], writes=[x1B[i]])
                    cnt += 1

        for l in range(NL):
            run_layer(l, l < NL - 1)
        tr.barrier()

        block = es.enter_context(nc.Block())

        @block.tensor
        def _(h):
            Tracer.replay(pe, h)

        @block.scalar
        def _(h):
            Tracer.replay(act, h)

        @block.vector
        def _(h):
            Tracer.replay(dve, h)

        @block.gpsimd
        def _(h):
            Tracer.replay(pool, h)

        @block.sync
        def _(h):
            Tracer.replay(sp, h)
    return nc


_CACHE = {}


def _prep_inputs(inputs):
    f = lambda a: np.ascontiguousarray(np.asarray(a, dtype=np.float32))
    x, c, ctx, c_ctx = f(inputs["x"]), f(inputs["c"]), f(inputs["ctx"]), f(inputs["c_ctx"])
    cf, cb = _host_consts()
    w_br = np.ascontiguousarray(np.stack([f(inputs["w_br_a"]), f(inputs["w_br_b"]), f(inputs["w_br_c"]), f(inputs["w_br_d"])], axis=1))
    vecs = np.zeros((128, NL * NVL), np.float32)

    def fm(v, n):
        return np.ascontiguousarray(v.reshape(n, 128).T)

    for l in range(NL):
        o = l * NVL
        vecs[:, o + V_G:o + V_G + 8] = fm(f(inputs["norm_g"])[l], 8)
        b_ada = f(inputs["b_ada"])[l]
        vecs[:, o + V_BSH:o + V_BSH + 8] = fm(b_ada[0:1024], 8)
        vecs[:, o + V_BSC:o + V_BSC + 8] = fm(b_ada[1024:2048], 8)
        vecs[:, o + V_BM:o + V_BM + 32] = fm(f(inputs["b_merge"])[l], 32)
        vecs[:, o + V_CB:o + V_CB + 4] = fm(f(inputs["conv_b"])[l], 4)
        vecs[:, o + V_LG:o + V_LG + 4] = fm(f(inputs["conv_ln_g"])[l], 4)
        vecs[:, o + V_LB:o + V_LB + 4] = fm(f(inputs["conv_ln_b"])[l], 4)
        cw = f(inputs["conv_w"])[l]
        vecs[:, o + V_CW:o + V_CW + 124] = cw.T.reshape(4, 128, 31).transpose(1, 0, 2).reshape(128, 124)
        for nm, off in (("na_qn_g", V_NAQ), ("na_kn_g", V_NAK), ("gqa_qn_g", V_GQ), ("gqa_kn_g", V_GK),
                        ("diff_qn_g", V_DQ), ("diff_kn_g", V_DK)):
            vecs[:, o + off] = np.tile(f(inputs[nm])[l], 2)
        vecs[:, o + V_SUB] = f(inputs["diff_subln_g"])[l]
        for k, nm in enumerate(("lam_q1", "lam_k1", "lam_q2", "lam_k2")):
            vecs[:, o + V_L + 64 * k:o + V_L + 64 * (k + 1)] = f(inputs[nm])[l][None, :]
    bgrow = np.ascontiguousarray(f(inputs["b_ada"])[:, 2048:3072].reshape(1, NL * DM))
    shared = {"w_ada": f(inputs["w_ada"]), "w_in": f(inputs["w_in"]), "w_br": w_br, "w_out": f(inputs["w_out"]),
              "vecs": vecs, "bgrow": bgrow, "rpb": f(inputs["na_rpb"]), "cf": cf, "cb": cb}
    maps = []
    for b in range(8):
        cT = np.concatenate([fm(c[b], 8), fm(c_ctx, 8)], axis=1)
        m = dict(shared)
        m.update({"x": x[b], "ctx": ctx[b], "cT": np.ascontiguousarray(cT)})
        maps.append(m)
    return maps


def kernel(**inputs):
    if "nc" not in _CACHE:
        _CACHE["nc"] = build_program(False)
    maps = _prep_inputs(inputs)
    res = run_bass_kernel_spmd(_CACHE["nc"], maps, core_ids=list(range(8)))
    return np.stack([np.asarray(r["out"], dtype=np.float32) for r in res.results], axis=0)
```

```python
import math
import numpy as np
from contextlib import ExitStack
import concourse.bass as bass
import concourse.mybir as mybir
from concourse.bass_utils import run_bass_kernel_spmd

F32 = mybir.dt.float32
BF16 = mybir.dt.bfloat16
AF = mybir.ActivationFunctionType
ALU = mybir.AluOpType

DM = 1024
S = 2048
LC = 256
T = S + LC
NL = 2
INW = 11008
EPS = 1e-6
NEGM = -30000.0
NVL = 455
NJ = 26
TILES = [(0, 512), (512, 512), (1024, 512), (1536, 512), (2048, 256)]

V_G, V_BSH, V_BSC, V_BM, V_CB, V_LG, V_LB, V_CW = 0, 8, 16, 24, 56, 60, 64, 68
V_NAQ, V_NAK, V_GQ, V_GK, V_DQ, V_DK, V_SUB = 192, 193, 194, 195, 196, 197, 198
V_L = 199


class Buf:
    __slots__ = ("w", "r", "name")

    def __init__(self, name=""):
        self.w = None
        self.r = {}
        self.name = name


class DSem:
    def __init__(self, h):
        self.h = h
        self.count = 0


class Eng:
    def __init__(self, tr, name):
        self.tr = tr
        self.name = name
        self.items = []
        self.seen = {}
        self.sems = []
        self.count = 0
        self.newsem()

    def newsem(self):
        h = self.tr.es.enter_context(self.tr.nc.semaphore(f"s_{self.name}{len(self.sems)}"))
        self.sems.append(h)
        self.count = 0


class Tracer:
    def __init__(self, nc, es):
        self.nc = nc
        self.es = es
        self.pe = Eng(self, "pe")
        self.act = Eng(self, "act")
        self.dve = Eng(self, "dve")
        self.pool = Eng(self, "pool")
        self.sp = Eng(self, "sp")
        self.engs = [self.pe, self.act, self.dve, self.pool, self.sp]
        self.dsems = []

    def dsem(self, name):
        d = DSem(self.es.enter_context(self.nc.semaphore("d_" + name)))
        self.dsems.append(d)
        return d

    def _deps(self, eng, reads, writes):
        need = {}

        def add(tok):
            if tok is None:
                return
            sem, val, src = tok
            if src is eng and eng.name == "pe":
                return
            k = id(sem)
            if k not in need or need[k][1] < val:
                need[k] = (sem, val)

        for b in reads:
            add(b.w)
        for b in writes:
            add(b.w)
            for t in b.r.values():
                add(t)
        for k, (sem, val) in need.items():
            if eng.seen.get(k, 0) < val:
                eng.items.append(("wait", sem, val))
                eng.seen[k] = val

    @staticmethod
    def _commit(tok, reads, writes):
        for b in writes:
            b.w = tok
            b.r = {}
        k = id(tok[0])
        for b in reads:
            b.r[k] = tok

    def op(self, eng, fn, reads=(), writes=()):
        self._deps(eng, reads, writes)
        if eng.count >= 16000:
            eng.newsem()
        eng.count += 1
        tok = (eng.sems[-1], eng.count, eng)
        eng.items.append(("ins", fn, eng.sems[-1], 1))
        self._commit(tok, reads, writes)
        return tok

    def group(self, eng, fns, reads=(), writes=()):
        self._deps(eng, reads, writes)
        if eng.count >= 16000:
            eng.newsem()
        for f in fns[:-1]:
            eng.items.append(("ins", f, None, 0))
        eng.count += 1
        tok = (eng.sems[-1], eng.count, eng)
        eng.items.append(("ins", fns[-1], eng.sems[-1], 1))
        self._commit(tok, reads, writes)
        return tok

    def dma(self, eng, dsem, fns, reads=(), writes=()):
        self._deps(eng, reads, writes)
        for f in fns:
            eng.items.append(("ins", f, dsem.h, 16))
            dsem.count += 16
        tok = (dsem.h, dsem.count, None)
        self._commit(tok, reads, writes)
        return tok

    def barrier(self):
        toks = [(e.sems[-1], e.count, e) for e in self.engs if e.count > 0]
        toks += [(d.h, d.count, None) for d in self.dsems if d.count > 0]
        for e in self.engs:
            for sem, val, src in toks:
                if src is e:
                    continue
                k = id(sem)
                if e.seen.get(k, 0) < val:
                    e.items.append(("wait", sem, val))
                    e.seen[k] = val

    @staticmethod
    def replay(eng, h):
        for it in eng.items:
            if it[0] == "wait":
                h.wait_ge(it[1], it[2])
            else:
                ins = it[1](h)
                if it[2] is not None:
                    ins.then_inc(it[2], it[3])


def MM(out, lhsT, rhs, start=True, stop=True):
    return lambda h: h.matmul(out, lhsT=lhsT, rhs=rhs, start=start, stop=stop)


def TRN(out, in_, ident):
    return lambda h: h.transpose(out, in_, ident)


def ACTF(out, in_, func, **kw):
    return lambda h: h.activation(out=out, in_=in_, func=func, **kw)


def TT(out, in0, in1, op):
    return lambda h: h.tensor_tensor(out=out, in0=in0, in1=in1, op=op)


def TS(out, in0, s1, s2=None, op0=ALU.mult, op1=None):
    if op1 is None:
        return lambda h: h.tensor_scalar(out=out, in0=in0, scalar1=s1, scalar2=None, op0=op0)
    return lambda h: h.tensor_scalar(out=out, in0=in0, scalar1=s1, scalar2=s2, op0=op0, op1=op1)


def STT(out, in0, scalar, in1, op0, op1):
    return lambda h: h.scalar_tensor_tensor(out=out, in0=in0, scalar=scalar, in1=in1, op0=op0, op1=op1)


def CP(out, in_):
    return lambda h: h.tensor_copy(out=out, in_=in_)


def RCP(out, in_):
    return lambda h: h.reciprocal(out=out, in_=in_)


def MSET(ap, v):
    return lambda h: h.memset(ap, v)


def DMA(out, in_):
    return lambda h: h.dma_start(out=out, in_=in_)


class Ring:
    def __init__(self, aps, name):
        self.aps = aps
        self.bufs = [Buf(f"{name}{i}") for i in range(len(aps))]
        self.i = 0

    def next(self):
        k = self.i % len(self.aps)
        self.i += 1
        return self.aps[k], self.bufs[k]


def _host_consts():
    identf = np.eye(128, dtype=np.float32)
    p = np.arange(128)
    hd = p % 64
    half = hd // 32
    fi = (hd % 32) % 16
    freq = (10000.0 ** (-(2.0 * fi) / 32.0)).astype(np.float32)
    t = np.arange(S)
    rows = (t // 64).astype(np.float32)
    cols = (t % 64).astype(np.float32)
    pos = np.where(half[:, None] == 0, rows[None, :], cols[None, :]).astype(np.float32)
    ang = (pos * freq[:, None]).astype(np.float32)
    cos = np.cos(ang).astype(np.float32)
    sgn = np.where((hd % 32) < 16, -1.0, 1.0).astype(np.float32)
    sin = (np.sin(ang).astype(np.float32) * sgn[:, None]).astype(np.float32)
    selR = np.zeros((128, 64), np.float32)

    def delta(hf, jj):
        return (4 - jj) + hf if jj < 10 else (7 - (jj - 10)) + hf

    for hf in range(2):
        for jj in range(NJ):
            dr = delta(hf, jj) + 7
            if 0 <= dr <= 14:
                selR[dr, hf * NJ + jj] = 8.0
    cf = np.concatenate([identf, cos, sin, selR], axis=1)

    ident = identf
    blk = ((p[:, None] // 64) == (p[None, :] // 64)).astype(np.float32)
    partner = np.where((p % 32) < 16, p + 16, p - 16)
    perm = np.zeros((128, 128), np.float32)
    perm[partner, p] = 1.0
    band = np.zeros((128, 128), np.float32)
    for c in range(31):
        band[c, c + 48] = 1.0
    mask = np.zeros((128, 2, NJ, 64), np.float32)
    qc = np.arange(64)
    c0 = np.clip(qc - 8, 0, 48)
    for kc in range(64):
        colok = (kc >= c0) & (kc < c0 + 16)
        for hf in range(2):
            for jj in range(NJ):
                d = delta(hf, jj)
                ok = (-4 <= d <= 3) if jj < 10 else (-7 <= d <= 7)
                mask[kc, hf, jj, :] = np.where(colok & ok, 0.0, NEGM)
    cb = np.concatenate([ident, blk, perm, band, np.ones((128, 128), np.float32),
                         mask.reshape(128, -1)], axis=1)
    return np.ascontiguousarray(cf), np.ascontiguousarray(cb)


CF_ID, CF_COS, CF_SIN, CF_SEL, CF_N = 0, 128, 128 + 2048, 128 + 4096, 128 + 4096 + 64
CB_ID, CB_BLK, CB_PERM, CB_BAND, CB_ONES, CB_MASK, CB_N = 0, 128, 256, 384, 512, 640, 640 + 2 * NJ * 64


def build_program(debug=False):
    nc = bass.Bass("TRN2", target_bir_lowering=False)
    dx = nc.dram_tensor("x", [S, DM], F32, kind="ExternalInput").ap()
    dctx = nc.dram_tensor("ctx", [LC, DM], F32, kind="ExternalInput").ap()
    dcT = nc.dram_tensor("cT", [128, 16], F32, kind="ExternalInput").ap()
    dwada = nc.dram_tensor("w_ada", [NL, DM, 3 * DM], F32, kind="ExternalInput").ap()
    dwin = nc.dram_tensor("w_in", [NL, DM, INW], F32, kind="ExternalInput").ap()
    dwbr = nc.dram_tensor("w_br", [NL, 4, 512, DM], F32, kind="ExternalInput").ap()
    dwout = nc.dram_tensor("w_out", [NL, DM, DM], F32, kind="ExternalInput").ap()
    dvecs = nc.dram_tensor("vecs", [128, NL * NVL], F32, kind="ExternalInput").ap()
    dbg_rows = nc.dram_tensor("bgrow", [1, NL * DM], F32, kind="ExternalInput").ap()
    drpb = nc.dram_tensor("rpb", [NL, 8, 15, 31], F32, kind="ExternalInput").ap()
    dcf = nc.dram_tensor("cf", [128, CF_N], F32, kind="ExternalInput").ap()
    dcb = nc.dram_tensor("cb", [128, CB_N], F32, kind="ExternalInput").ap()
    dout = nc.dram_tensor("out", [S, DM], F32, kind="ExternalOutput").ap()
    dx1 = nc.dram_tensor("x1s", [T, DM], F32, kind="ExternalOutput" if debug else "Internal").ap()
    ddbg = None
    if debug:
        ddbg = nc.dram_tensor("dbg", [8, 4, 128, T], F32, kind="ExternalOutput").ap()
        ddbgm = nc.dram_tensor("dbgm", [8, 128, T], F32, kind="ExternalOutput").ap()
        ddbgg = nc.dram_tensor("dbgg", [128, 2048], F32, kind="ExternalOutput").ap()

    es = ExitStack()
    with es:
        tr = Tracer(nc, es)
        pe, act, dve, pool, sp = tr.pe, tr.act, tr.dve, tr.pool, tr.sp

        def sb(name, shape, dt):
            return es.enter_context(nc.sbuf_tensor("sb_" + name, shape, dt))

        hT = sb("hT", [128, 8, T], BF16)
        mT = sb("mT", [128, 8, T], BF16)
        yT = sb("yT", [128, 4, T], BF16)
        cfs = sb("cfs", [128, CF_N], F32)
        cbs = sb("cbs", [128, CB_N], BF16)
        vecs = sb("vecs", [128, NL * NVL], F32)
        bgrow = sb("bgrow", [1, DM], BF16)
        bgB = Buf("bgrow")
        bgd = tr.dsem("bgrow")
        modv = sb("modv", [128, 4, 8], F32)
        s2 = sb("s2", [128, 8, 2], BF16)
        small = sb("small", [128, 64], F32)
        wring_t = [sb(f"wr{i}", [128, 4096], BF16) for i in range(3)]
        ARENA_N = 31780
        arena = sb("arena", [128, ARENA_N], BF16)
        psum = [es.enter_context(nc.psum_tensor(f"ps{i}", [128, 512], F32)) for i in range(8)]
        pb = [Buf(f"ps{i}") for i in range(8)]

        hTB = [Buf(f"hT{t}") for t in range(5)]
        hTB2 = [Buf(f"hTa{t}") for t in range(5)]
        mTB = [[Buf(f"mT{c}_{t}") for t in range(5)] for c in range(8)]
        yTB = [[Buf(f"yT{c}_{t}") for t in range(5)] for c in range(4)]
        constB = Buf("const")
        vecB = Buf("vecs")
        modB = Buf("modv")
        s2B = Buf("s2")
        smallB = Buf("small")
        x1B = [Buf(f"x1_{i}") for i in range(18)]
        outB = Buf("out")

        identf = cfs[:, CF_ID:CF_ID + 128]
        COS = cfs[:, CF_COS:CF_COS + S]
        SIN = cfs[:, CF_SIN:CF_SIN + S]
        selR = cfs[:, CF_SEL:CF_SEL + 64]
        ident = cbs[:, CB_ID:CB_ID + 128]
        blk = cbs[:, CB_BLK:CB_BLK + 128]
        perm = cbs[:, CB_PERM:CB_PERM + 128]
        band = cbs[:, CB_BAND:CB_BAND + 128]
        ones = cbs[:, CB_ONES:CB_ONES + 128]
        maskT = cbs[:, CB_MASK:CB_MASK + 2 * NJ * 64].rearrange("p (t j q) -> p t j q", t=2, j=NJ)

        class Arena:
            def __init__(self):
                self.off = 0

            def reset(self):
                tr.barrier()
                self.off = 0

            def bf(self, n):
                ap = arena[:, self.off:self.off + n]
                self.off += n
                assert self.off <= ARENA_N, self.off
                return ap

            def f32(self, n):
                ap = arena[:, self.off:self.off + 2 * n].bitcast(F32)
                self.off += 2 * n
                assert self.off <= ARENA_N, self.off
                return ap

        ar = Arena()

        wsl = [Buf(f"w{i}") for i in range(3)]
        wds = [tr.dsem(f"w{i}") for i in range(3)]
        pieces = []

        def w3(slot, kc, n):
            return slot[:, 0:kc * n].rearrange("p (k n) -> p k n", k=kc)

        def piece_cols(tag, src2d, cols):
            specs = []
            for (do, c0, n) in cols:
                specs.append((lambda sl, do=do, n=n: w3(sl, 8, 512)[:, :, do:do + n],
                              src2d[:, c0:c0 + n].rearrange("(k p) n -> p k n", p=128)))
            pieces.append((tag, specs))

        def layer_pieces(l):
            win = dwin[l]
            for pi in range(4):
                piece_cols(f"ada{l}_{pi}", dwada[l], [(0, pi * 512, 512)])
            for j in range(4):
                piece_cols(f"A{l}_{j}", win, [(0, j * 128, 128), (128, 512 + j * 128, 128), (256, 1024 + j * 128, 128)])
            merge_pieces(l, 0)
            for hp in range(4):
                piece_cols(f"B{l}_{hp}", win, [(0, 1536 + hp * 128, 128), (128, 2048 + hp * 128, 128),
                                               (256, 2560 + hp * 128, 128), (384, 3072 + hp * 128, 128)])
            merge_pieces(l, 1)
            for cp in range(4):
                n = cp // 2
                piece_cols(f"C{l}_{cp}", win, [(0, 3584 + cp * 128, 128), (128, 4096 + n * 64, 64), (192, 4096 + n * 64, 64),
                                               (256, 4224 + n * 64, 64), (384, 4352 + cp * 128, 128)])
            merge_pieces(l, 2)
            for hd in range(4):
                piece_cols(f"D{l}_{hd}", win, [(0, 4864 + hd * 128, 128), (128, 5376 + hd * 128, 128),
                                               (256, 5888 + hd * 128, 128), (384, 6400 + hd * 128, 128)])
            merge_pieces(l, 3)
            for pi in range(2):
                piece_cols(f"adag{l}_{pi}", dwada[l], [(0, 2048 + pi * 512, 512)])
            for ph in range(2):
                piece_cols(f"wo{l}_{ph}", dwout[l], [(0, ph * 512, 512)])

        def merge_pieces(l, i):
            for hf in range(2):
                pieces.append((f"br{l}_{i}_{hf}", [(lambda sl: w3(sl, 4, 1024),
                                                    dwbr[l, i].rearrange("(k p) n -> p k n", p=128))]))
                piece_cols(f"lg{l}_{i}_{hf}", dwin[l], [(0, 6912 + i * 1024 + hf * 512, 512)])

        for l in range(NL):
            layer_pieces(l)
        wstate = {"issued": 0, "next": 0}

        def w_issue(upto):
            while wstate["issued"] < min(upto, len(pieces)):
                i = wstate["issued"]
                tag, specs = pieces[i]
                sl = wring_t[i % 3]
                tr.dma(pool, wds[i % 3], [DMA(f(sl), src) for (f, src) in specs], writes=[wsl[i % 3]])
                wstate["issued"] += 1

        def wget(tag):
            i = wstate["next"]
            assert pieces[i][0] == tag, (pieces[i][0], tag)
            w_issue(i + 2)
            wstate["next"] += 1
            return wring_t[i % 3], wsl[i % 3]

        d_init = tr.dsem("init")
        tr.dma(sp, d_init, [DMA(cfs[:, :], dcf), DMA(vecs[:, :], dvecs), DMA(small[:, 0:16], dcT)],
               writes=[constB, vecB, smallB])
        d_init2 = tr.dsem("init2")
        tr.dma(pool, d_init2, [DMA(cbs[:, :], dcb)], writes=[constB])
        w_issue(2)

        def vcol(l, off, n=1):
            return vecs[:, l * NVL + off: l * NVL + off + n]

        xld = [tr.dsem(f"xld{i}") for i in range(3)]
        std = [tr.dsem(f"st{i}") for i in range(2)]

        def run_layer(l, need_ctx):
            lam_init = 0.8 - 0.6 * math.exp(-0.3 * l)
            qtiles = TILES if need_ctx else TILES[:4]

            ar.reset()
            tr.op(act, ACTF(s2[:, :, 0], small[:, 0:8], AF.Silu), reads=[smallB], writes=[s2B])
            tr.op(act, ACTF(s2[:, :, 1], small[:, 8:16], AF.Silu), reads=[smallB], writes=[s2B])
            pm = psum[7]
            for pi in range(4):
                wsl_ap, wb = wget(f"ada{l}_{pi}")
                w = w3(wsl_ap, 8, 512)
                for fc in range(4):
                    g = pi * 4 + fc
                    tr.group(pe, [MM(pm[:, g * 2:g * 2 + 2], w[:, kc, fc * 128:(fc + 1) * 128], s2[:, kc, :],
                                     start=(kc == 0), stop=(kc == 7)) for kc in range(8)],
                             reads=[wb, s2B], writes=[pb[7]])
            pmv = pm[:, 0:32].rearrange("p (f w) -> p f w", w=2)
            tmp8 = small[:, 16:24]
            for which in range(2):
                tr.op(dve, TT(modv[:, 2 * which + 1, :], pmv[:, 0:8, which], vcol(l, V_BSH, 8), ALU.add),
                      reads=[pb[7], vecB], writes=[modB])
                tr.op(dve, TT(tmp8, pmv[:, 8:16, which], vcol(l, V_BSC, 8), ALU.add),
                      reads=[pb[7], vecB], writes=[smallB])
                tr.op(dve, STT(modv[:, 2 * which, :], tmp8, 1.0, vcol(l, V_G, 8), ALU.add, ALU.mult),
                      reads=[smallB, vecB], writes=[modB])
            lt = small[:, 24:28]
            prod = ar.f32(64)
            prodB = Buf("prod")
            for k in range(2):
                tr.op(dve, TT(prod, vcol(l, V_L + 128 * k, 64), vcol(l, V_L + 128 * k + 64, 64), ALU.mult),
                      reads=[vecB], writes=[prodB])
                tr.op(dve, MSET(lt[:, k:k + 1], 0.0), writes=[smallB])
                tr.op(act, ACTF(prod, prod, AF.Identity, accum_out=lt[:, k:k + 1]), reads=[prodB, smallB], writes=[prodB, smallB])
                tr.op(act, ACTF(lt[:, k:k + 1], lt[:, k:k + 1], AF.Exp), reads=[smallB], writes=[smallB])
            neglam = small[:, 28:29]
            gsub = small[:, 29:30]
            epsc = small[:, 30:31]
            tr.op(dve, MSET(epsc, EPS), writes=[smallB])
            tr.op(dve, TT(lt[:, 2:3], lt[:, 0:1], lt[:, 1:2], ALU.subtract), reads=[smallB], writes=[smallB])
            tr.op(dve, TS(neglam, lt[:, 2:3], lam_init, -1.0, ALU.add, ALU.mult), reads=[smallB], writes=[smallB])
            tr.op(dve, TS(gsub, vcol(l, V_SUB), 1.0 - lam_init), reads=[vecB], writes=[smallB])

            ar.reset()
            xt_r = Ring([ar.f32(1024) for _ in range(3)], "xt")
            xn_r = Ring([ar.f32(1024) for _ in range(3)], "xn")
            junk = ar.bf(1024)
            junkB = Buf("junk")
            st_r = Ring([small[:, 32 + 4 * i: 36 + 4 * i] for i in range(3)], "st")
            def p1_a(i):
                xt, xtB = xt_r.next()
                xn, xnB = xn_r.next()
                stt_, stB = st_r.next()
                if l == 0:
                    src = dx[i * 128:(i + 1) * 128, :] if i < 16 else dctx[(i - 16) * 128:(i - 15) * 128, :]
                    rd = []
                else:
                    src = dx1[i * 128:(i + 1) * 128, :]
                    rd = [x1B[i]]
                tr.dma(sp, xld[i % 3], [DMA(xt, src)], reads=rd, writes=[xtB])
                tr.op(dve, MSET(stt_[:, 0:1], 0.0), writes=[stB])
                tr.op(act, ACTF(junk, xt, AF.Square, accum_out=stt_[:, 0:1]), reads=[xtB, stB], writes=[junkB, stB])
                tr.op(act, ACTF(stt_[:, 1:2], stt_[:, 0:1], AF.Sqrt, scale=1.0 / DM, bias=EPS), reads=[stB], writes=[stB])
                tr.op(dve, RCP(stt_[:, 2:3], stt_[:, 1:2]), reads=[stB], writes=[stB])
                tr.op(dve, TS(xn, xt, stt_[:, 2:3]), reads=[xtB, stB], writes=[xnB])
                return xn, xnB

            def p1_b(i, xn, xnB):
                which = 0 if i < 16 else 1
                for hb in range(2):
                    bk = (2 * i + hb) % 8
                    tr.group(pe, [TRN(psum[bk][:, k4 * 128:(k4 + 1) * 128], xn[:, (hb * 4 + k4) * 128:(hb * 4 + k4 + 1) * 128], identf)
                                  for k4 in range(4)], reads=[xnB, constB], writes=[pb[bk]])
                    for k4 in range(4):
                        kc = hb * 4 + k4
                        dst = hT[:, kc, i * 128:(i + 1) * 128]
                        srcp = psum[bk][:, k4 * 128:(k4 + 1) * 128]
                        A = modv[:, 2 * which, kc:kc + 1]
                        Bc = modv[:, 2 * which + 1, kc:kc + 1]
                        if hb == 0:
                            tr.op(dve, TS(dst, srcp, A, Bc, ALU.mult, ALU.add), reads=[pb[bk], modB], writes=[hTB[i // 4]])
                        else:
                            tr.op(act, ACTF(dst, srcp, AF.Identity, scale=A, bias=Bc), reads=[pb[bk], modB], writes=[hTB2[i // 4]])

            prev = None
            for i in range(18):
                cur = p1_a(i)
                if prev is not None:
                    p1_b(i - 1, *prev)
                prev = cur
            p1_b(17, *prev)

            def proj_fm(ps_i, w, col0, t0, n, wb, t5):
                tr.group(pe, [MM(psum[ps_i][:, 0:n], w[:, kc, col0:col0 + 128], hT[:, kc, t0:t0 + n],
                                 start=(kc == 0), stop=(kc == 7)) for kc in range(8)],
                         reads=[wb, hTB[t5], hTB2[t5]], writes=[pb[ps_i]])

            def merge_branch(i):
                ar.reset()
                g_r = Ring([ar.f32(512) for _ in range(4)], "G")
                t_r = Ring([ar.f32(512) for _ in range(4)], "mt")
                cnt = 0
                for hf in range(2):
                    wbr_ap, wbrB = wget(f"br{l}_{i}_{hf}")
                    wbr = w3(wbr_ap, 4, 1024)
                    wl_ap, wlB = wget(f"lg{l}_{i}_{hf}")
                    wl = w3(wl_ap, 8, 512)
                    for fcl in range(4):
                        fc = hf * 4 + fcl
                        for t5, (t0, n) in enumerate(qtiles):
                            pz = (2 * cnt) % 8
                            pl = (2 * cnt + 1) % 8
                            cnt += 1
                            fz = [MM(psum[pz][:, 0:n], wbr[:, kc, fc * 128:(fc + 1) * 128], yT[:, kc, t0:t0 + n],
                                     start=(kc == 0), stop=(kc == 3)) for kc in range(4)]
                            fl = [MM(psum[pl][:, 0:n], wl[:, kc, fcl * 128:(fcl + 1) * 128], hT[:, kc, t0:t0 + n],
                                     start=(kc == 0), stop=(kc == 7)) for kc in range(8)]
                            tr.group(pe, fz + fl, reads=[wbrB, wlB, hTB[t5], hTB2[t5]] + [yTB[kc][t5] for kc in range(4)],
                                     writes=[pb[pz], pb[pl]])
                            G, GB = g_r.next()
                            tr.op(act, ACTF(G[:, 0:n], psum[pl][:, 0:n], AF.Sigmoid, bias=vcol(l, V_BM + i * 8 + fc)),
                                  reads=[pb[pl], vecB], writes=[GB])
                            if i == 0:
                                tr.op(dve, TT(mT[:, fc, t0:t0 + n], psum[pz][:, 0:n], G[:, 0:n], ALU.mult),
                                      reads=[pb[pz], GB], writes=[mTB[fc][t5]])
                            else:
                                tm, tmB = t_r.next()
                                tr.op(dve, TT(tm[:, 0:n], psum[pz][:, 0:n], G[:, 0:n], ALU.mult), reads=[pb[pz], GB], writes=[tmB])
                                tr.op(pool, TT(mT[:, fc, t0:t0 + n], mT[:, fc, t0:t0 + n], tm[:, 0:n], ALU.add),
                                      reads=[tmB], writes=[mTB[fc][t5]])

            def dump(i):
                if debug:
                    dd = tr.dsem(f"dbg{l}_{i}")
                    tr.dma(pool, dd, [DMA(ddbg[l * 4 + i].rearrange("c p t -> p c t"), yT[:, :, :])],
                           reads=[yTB[c][t] for c in range(4) for t in range(5)], writes=[outB])

            ar.reset()
            HG = 2364
            hglu_r = Ring([ar.bf(HG) for _ in range(2)], "hglu")
            diag_ap = ar.bf(31 * 128).rearrange("p (k n) -> p k n", k=31)
            diagB = Buf("diag")
            sg_r = Ring([ar.f32(512) for _ in range(2)], "sg")
            cbuf = mT[:, :, :].rearrange("p c t -> p (c t)").bitcast(F32).rearrange("p (c t) -> p c t", c=4)

            def cB(j):
                return [mTB[2 * j][t] for t in range(5)] + [mTB[2 * j + 1][t] for t in range(5)]
            segs = [(0, 0, 512), (512, 512, 512), (1024, 1024, 512), (1536, 1536, 512), (2078, 2048, 256)]
            segs = segs if need_ctx else segs[:4]
            for j in range(4):
                w_ap, wb = wget(f"A{l}_{j}")
                w = w3(w_ap, 8, 512)
                for k in range(31):
                    tr.op(dve, TS(diag_ap[:, k, :], ident, vcol(l, V_CW + j * 31 + k)), reads=[constB, vecB], writes=[diagB])
                hg, hgB = hglu_r.next()
                for (a, b) in ((0, 15), (2063, 2093), (2349, 2364)):
                    tr.op(pool, MSET(hg[:, a:b], 0.0), writes=[hgB])
                for t5, (bb, t0, n) in enumerate(segs):
                    proj_fm(0 + (t5 % 2) * 2, w, 0, t0, n, wb, t5)
                    proj_fm(1 + (t5 % 2) * 2, w, 128, t0, n, wb, t5)
                    pa, pg = (t5 % 2) * 2, 1 + (t5 % 2) * 2
                    sg, sgB = sg_r.next()
                    tr.op(act, ACTF(sg[:, 0:n], psum[pg][:, 0:n], AF.Sigmoid), reads=[pb[pg]], writes=[sgB])
                    tr.op(dve, TT(hg[:, bb + 15:bb + 15 + n], psum[pa][:, 0:n], sg[:, 0:n], ALU.mult),
                          reads=[pb[pa], sgB], writes=[hgB])
                for t5, (bb, t0, n) in enumerate(segs):
                    pc = 4 + (t5 % 2)
                    tr.group(pe, [MM(psum[pc][:, 0:n], diag_ap[:, k, :], hg[:, bb + k:bb + k + n], start=(k == 0), stop=(k == 30))
                                  for k in range(31)], reads=[diagB, hgB], writes=[pb[pc]])
                    tr.op(act, ACTF(cbuf[:, j, t0:t0 + n], psum[pc][:, 0:n], AF.Identity, bias=vcol(l, V_CB + j)),
                          reads=[pb[pc], vecB], writes=cB(j))
                    pgt = 6 + (t5 % 2)
                    proj_fm(pgt, w, 256, t0, n, wb, t5)
                    tr.op(act, ACTF(yT[:, j, t0:t0 + n], psum[pgt][:, 0:n], AF.Silu), reads=[pb[pgt]], writes=[yTB[j][t5]])
            ar.reset()
            onesf = ar.f32(128)
            onesfB = Buf("onesf")
            tr.op(dve, MSET(onesf, 1.0), writes=[onesfB])
            sq_r = Ring([ar.f32(512) for _ in range(2)], "csq")
            mean = ar.f32(512)
            rstd = ar.f32(512)
            msq = ar.f32(512)
            stB2 = Buf("lnstat")
            d_r = Ring([ar.f32(512) for _ in range(2)], "lnd")
            allc = [b for j in range(4) for b in cB(j)]
            for t5, (t0, n) in enumerate(qtiles):
                tr.group(pe, [MM(psum[0][:, 0:n], onesf, cbuf[:, j, t0:t0 + n], start=(j == 0), stop=(j == 3)) for j in range(4)],
                         reads=allc + [onesfB], writes=[pb[0]])
                sqs = []
                for j in range(4):
                    sq, sqB = sq_r.next()
                    tr.op(act, ACTF(sq[:, 0:n], cbuf[:, j, t0:t0 + n], AF.Square), reads=allc, writes=[sqB])
                    tr.group(pe, [MM(psum[1][:, 0:n], onesf, sq[:, 0:n], start=(j == 0), stop=(j == 3))],
                             reads=[sqB, onesfB], writes=[pb[1]])
                tr.op(dve, TS(mean[:, 0:n], psum[0][:, 0:n], 1.0 / 512), reads=[pb[0]], writes=[stB2])
                tr.op(dve, TT(msq[:, 0:n], mean[:, 0:n], mean[:, 0:n], ALU.mult), reads=[stB2], writes=[stB2])
                tr.op(dve, STT(msq[:, 0:n], psum[1][:, 0:n], 1.0 / 512, msq[:, 0:n], ALU.mult, ALU.subtract),
                      reads=[pb[1], stB2], writes=[stB2])
                tr.op(act, ACTF(msq[:, 0:n], msq[:, 0:n], AF.Sqrt, bias=EPS, scale=1.0), reads=[stB2], writes=[stB2])
                tr.op(dve, RCP(rstd[:, 0:n], msq[:, 0:n]), reads=[stB2], writes=[stB2])
                for j in range(4):
                    d, dB = d_r.next()
                    tr.op(dve, TT(d[:, 0:n], cbuf[:, j, t0:t0 + n], mean[:, 0:n], ALU.subtract), reads=allc + [stB2], writes=[dB])
                    tr.op(dve, TT(d[:, 0:n], d[:, 0:n], rstd[:, 0:n], ALU.mult), reads=[stB2], writes=[dB])
                    tr.op(act, ACTF(d[:, 0:n], d[:, 0:n], AF.Silu, scale=vcol(l, V_LG + j), bias=vcol(l, V_LB + j)),
                          reads=[vecB], writes=[dB])
                    tr.op(dve, TT(yT[:, j, t0:t0 + n], yT[:, j, t0:t0 + n], d[:, 0:n], ALU.mult), reads=[dB], writes=[yTB[j][t5]])
            dump(0)
            merge_branch(0)

            def attn_branch(kind):
                ar.reset()
                QT = ar.bf(2 * T).rearrange("p (h t) -> p h t", h=2)
                KT = ar.bf(T)
                GT = ar.bf(T)
                Vp = ar.bf(18 * 256).rearrange("p (k n) -> p k n", k=18)
                QTB = [Buf(f"QT{t}") for t in range(5)]
                KTB = [Buf(f"KT{t}") for t in range(5)]
                GTB = [Buf(f"GT{t}") for t in range(5)]
                VB = [Buf(f"V{g}") for g in range(5)]
                sq_r = Ring([ar.bf(512) for _ in range(2)], "sq")
                rs_r = Ring([ar.f32(512) for _ in range(2)], "rs")
                xn_r2 = Ring([ar.bf(512) for _ in range(2)], "xnb")
                t1_r = Ring([ar.f32(512) for _ in range(1)], "t1")
                t2_r = Ring([ar.f32(512) for _ in range(1)], "t2")
                P_r = Ring([ar.bf(512) for _ in range(4 if kind == "B" else 6)], "P")
                ya_r = Ring([ar.f32(512) for _ in range(1)], "ya")
                yb_r = Ring([ar.f32(512) for _ in range(1)], "yb")
                if kind == "B":
                    tab = ar.bf(2 * NJ * 64).rearrange("p (h n) -> p h n", h=2)
                    tab64 = ar.bf(2 * NJ * 64).rearrange("p (t j q) -> p t j q", t=2, j=NJ)
                    rt2 = ar.bf(64)
                    rp = ar.f32(32)
                    tabB = [Buf("tab0"), Buf("tab1")]
                    tab64B, rt2B, rpB = Buf("tab64"), Buf("rt2"), Buf("rp")
                    rpd = tr.dsem(f"rp{l}")
                tr.op(pool, MSET(QT[:, :, :], 0.0), writes=QTB)
                if kind != "D":
                    tr.op(pool, MSET(Vp[:, :, 64:128], 1.0), writes=VB)
                    tr.op(pool, MSET(Vp[:, :, 192:256], 1.0), writes=VB)
                rope = kind != "B"
                qg = {"B": V_NAQ, "C": V_GQ, "D": V_DQ}[kind]
                kg = {"B": V_NAK, "C": V_GK, "D": V_DK}[kind]

                def normed(psi, dst, dstB, gcol, t0, n, do_rope):
                    sq, sqB = sq_r.next()
                    tr.op(act, ACTF(sq[:, 0:n], psum[psi][:, 0:n], AF.Square), reads=[pb[psi]], writes=[sqB])
                    tr.group(pe, [MM(psum[5][:, 0:n], blk, sq[:, 0:n])], reads=[sqB, constB], writes=[pb[5]])
                    rs, rsB = rs_r.next()
                    tr.op(act, ACTF(rs[:, 0:n], psum[5][:, 0:n], AF.Sqrt, scale=1.0 / 64, bias=EPS), reads=[pb[5]], writes=[rsB])
                    tr.op(dve, RCP(rs[:, 0:n], rs[:, 0:n]), reads=[rsB], writes=[rsB])
                    if not do_rope:
                        for (r0, r1, d_ap) in dst:
                            tr.op(dve, STT(d_ap, psum[psi][r0:r1, 0:n], vecs[r0:r1, l * NVL + gcol:l * NVL + gcol + 1], rs[r0:r1, 0:n],
                                           ALU.mult, ALU.mult), reads=[pb[psi], rsB, vecB], writes=[dstB])
                        return
                    xb, xbB = xn_r2.next()
                    tr.op(dve, STT(xb[:, 0:n], psum[psi][:, 0:n], vcol(l, gcol), rs[:, 0:n], ALU.mult, ALU.mult),
                          reads=[pb[psi], rsB, vecB], writes=[xbB])
                    tr.group(pe, [MM(psum[6][:, 0:n], perm, xb[:, 0:n])], reads=[xbB, constB], writes=[pb[6]])
                    t1, t1B = t1_r.next()
                    t2, t2B = t2_r.next()
                    tr.op(dve, TT(t1[:, 0:n], xb[:, 0:n], COS[:, t0:t0 + n], ALU.mult), reads=[xbB, constB], writes=[t1B])
                    tr.op(dve, TT(t2[:, 0:n], psum[6][:, 0:n], SIN[:, t0:t0 + n], ALU.mult), reads=[pb[6], constB], writes=[t2B])
                    for (r0, r1, d_ap) in dst:
                        tr.op(dve, TT(d_ap, t1[r0:r1, 0:n], t2[r0:r1, 0:n], ALU.add), reads=[t1B, t2B], writes=[dstB])

                for pr in range(4):
                    w_ap, wb = wget(f"{kind}{l}_{pr}")
                    w = w3(w_ap, 8, 512)
                    if kind == "B":
                        for h2 in range(2):
                            h = 2 * pr + h2
                            tr.dma(sp, rpd, [DMA(rp[0:15, 0:31], drpb[l, h])], writes=[rpB])
                            tr.group(pe, [MM(psum[7][0:31, 0:2 * NJ], rp[0:15, 0:31], selR[0:15, 0:2 * NJ])],
                                     reads=[rpB, constB], writes=[pb[7]])
                            tr.op(dve, CP(rt2[0:31, 0:2 * NJ], psum[7][0:31, 0:2 * NJ]), reads=[pb[7]], writes=[rt2B])
                            for g in range(8):
                                bk = g % 4
                                tr.group(pe, [MM(psum[bk][0:64, q8 * 2 * NJ:(q8 + 1) * 2 * NJ],
                                                 band[0:31, 63 - (8 * g + q8):127 - (8 * g + q8)], rt2[0:31, 0:2 * NJ])
                                              for q8 in range(8)], reads=[rt2B, constB], writes=[pb[bk]])
                                pv = psum[bk][0:64, 0:8 * 2 * NJ].rearrange("p (q t j) -> p t j q", q=8, t=2)
                                for hf in range(2):
                                    tr.op(dve, TT(tab64[0:64, hf, :, 8 * g:8 * g + 8], pv[:, hf], maskT[0:64, hf, :, 8 * g:8 * g + 8], ALU.add),
                                          reads=[pb[bk], constB], writes=[tab64B])
                            tr.op(dve, CP(tab[0:64, h2, :], tab64[0:64, 0].rearrange("p j q -> p (j q)")), reads=[tab64B], writes=[tabB[h2]])
                            tr.op(dve, CP(tab[64:128, h2, :], tab64[0:64, 1].rearrange("p j q -> p (j q)")), reads=[tab64B], writes=[tabB[h2]])
                    items = []
                    cnt = [0]

                    def qk_item(col0, dst, dstB, gcol, t0, n, t5, do_rope):
                        psi = cnt[0] % 4
                        cnt[0] += 1
                        st = {}

                        def s0():
                            proj_fm(psi, w, col0, t0, n, wb, t5)
                            sq, sqB = sq_r.next()
                            st["sq"] = (sq, sqB)
                            tr.op(act, ACTF(sq[:, 0:n], psum[psi][:, 0:n], AF.Square), reads=[pb[psi]], writes=[sqB])

                        def s1():
                            sq, sqB = st["sq"]
                            tr.group(pe, [MM(psum[5][:, 0:n], blk, sq[:, 0:n])], reads=[sqB, constB], writes=[pb[5]])
                            rs, rsB = rs_r.next()
                            tr.op(act, ACTF(rs[:, 0:n], psum[5][:, 0:n], AF.Sqrt, scale=1.0 / 64, bias=EPS), reads=[pb[5]], writes=[rsB])
                            tr.op(dve, RCP(rs[:, 0:n], rs[:, 0:n]), reads=[rsB], writes=[rsB])
                            if not do_rope:
                                for (r0, r1, d_ap) in dst:
                                    tr.op(dve, STT(d_ap, psum[psi][r0:r1, 0:n], vecs[r0:r1, l * NVL + gcol:l * NVL + gcol + 1],
                                                   rs[r0:r1, 0:n], ALU.mult, ALU.mult), reads=[pb[psi], rsB, vecB], writes=[dstB])
                                return
                            xb, xbB = xn_r2.next()
                            st["xb"] = (xb, xbB)
                            tr.op(dve, STT(xb[:, 0:n], psum[psi][:, 0:n], vcol(l, gcol), rs[:, 0:n], ALU.mult, ALU.mult),
                                  reads=[pb[psi], rsB, vecB], writes=[xbB])

                        def s2():
                            if not do_rope:
                                return
                            xb, xbB = st["xb"]
                            tr.group(pe, [MM(psum[6][:, 0:n], perm, xb[:, 0:n])], reads=[xbB, constB], writes=[pb[6]])
                            t1, t1B = t1_r.next()
                            t2, t2B = t2_r.next()
                            tr.op(dve, TT(t1[:, 0:n], xb[:, 0:n], COS[:, t0:t0 + n], ALU.mult), reads=[xbB, constB], writes=[t1B])
                            tr.op(dve, TT(t2[:, 0:n], psum[6][:, 0:n], SIN[:, t0:t0 + n], ALU.mult), reads=[pb[6], constB], writes=[t2B])
                            for (r0, r1, d_ap) in dst:
                                tr.op(dve, TT(d_ap, t1[r0:r1, 0:n], t2[r0:r1, 0:n], ALU.add), reads=[t1B, t2B], writes=[dstB])
                        return [s0, s1, s2]

                    def gate_item(t0, n, t5):
                        psi = cnt[0] % 4
                        cnt[0] += 1

                        def s0():
                            proj_fm(psi, w, 384, t0, n, wb, t5)
                            tr.op(act, ACTF(GT[:, t0:t0 + n], psum[psi][:, 0:n], AF.Silu), reads=[pb[psi]], writes=[GTB[t5]])
                        return [s0]

                    nv = 64 if kind == "C" else 128

                    def v_item(g5):
                        psi = cnt[0] % 4
                        cnt[0] += 1

                        def s0():
                            kts = list(range(4 * g5, min(4 * g5 + 4, 18)))
                            fns = []
                            for ii, kt in enumerate(kts):
                                for kc in range(8):
                                    fns.append(MM(psum[psi][:, ii * 128:ii * 128 + nv], hT[:, kc, kt * 128:(kt + 1) * 128],
                                                  w[:, kc, 256:256 + nv], start=(kc == 0), stop=(kc == 7)))
                            tr.group(pe, fns, reads=[wb, hTB[g5], hTB2[g5]], writes=[pb[psi]])
                            nk = len(kts)
                            pvv = psum[psi][:, 0:nk * 128].rearrange("p (k n) -> p k n", k=nk)
                            k0 = kts[0]
                            if kind == "B":
                                tr.op(act, ACTF(Vp[:, k0:k0 + nk, 0:64], pvv[:, :, 0:64], AF.Copy), reads=[pb[psi]], writes=[VB[g5]])
                                tr.op(dve, CP(Vp[:, k0:k0 + nk, 128:192], pvv[:, :, 64:128]), reads=[pb[psi]], writes=[VB[g5]])
                            elif kind == "C":
                                tr.op(act, ACTF(Vp[:, k0:k0 + nk, 0:64], pvv[:, :, 0:64], AF.Copy), reads=[pb[psi]], writes=[VB[g5]])
                            else:
                                tr.op(act, ACTF(Vp[:, k0:k0 + nk, 0:128], pvv[:, :, 0:128], AF.Copy), reads=[pb[psi]], writes=[VB[g5]])
                        return [s0]

                    qk_items = []
                    for t5, (t0, n) in enumerate(qtiles):
                        qk_items.append(qk_item(0, [(0, 64, QT[0:64, 0, t0:t0 + n]), (64, 128, QT[64:128, 1, t0:t0 + n])], QTB[t5], qg,
                                                t0, n, t5, rope and t5 < 4))
                    for t5, (t0, n) in enumerate(TILES):
                        qk_items.append(qk_item(128, [(0, 128, KT[:, t0:t0 + n])], KTB[t5], kg, t0, n, t5, rope and t5 < 4))
                    items = [gate_item(t0, n, t5) for t5, (t0, n) in enumerate(qtiles)] + qk_items + [v_item(g5) for g5 in range(5)]
                    KST = 3
                    for step in range(len(items) + KST - 1):
                        for k in range(KST):
                            i = step - k
                            if 0 <= i < len(items) and k < len(items[i]):
                                items[i][k]()

                    QW = 256

                    def both(ap512, qa, qb):
                        if qa == 0 and qb == QW:
                            return ap512
                        return ap512.rearrange("p (h q) -> p h q", h=2)[:, :, qa:qb]

                    qts = []
                    if kind == "B":
                        for r_lo in range(0, 32, 4):
                            ch = [(16, 0, QW, None), (17, 0, QW, None)]
                            if r_lo in (0, 28):
                                a0 = 0 if r_lo == 0 else 12
                                for a in range(a0, a0 + 4):
                                    ch.append((a, 0, QW, (10 + 7 - 2 * a + r_lo) * 64))
                            else:
                                for a in range(16):
                                    rs_ = max(r_lo, 2 * a - 3)
                                    re_ = min(r_lo + 3, 2 * a + 5)
                                    if rs_ <= re_:
                                        ch.append((a, (rs_ - r_lo) * 64, (re_ - r_lo + 1) * 64, (4 - 2 * a + rs_) * 64))
                            qts.append((r_lo * 64, ch))
                    else:
                        for q0 in range(0, S, QW):
                            qts.append((q0, [(kt, 0, QW, None) for kt in range(18)]))
                    if need_ctx:
                        qts.append((2048, [(16, 0, QW, None), (17, 0, QW, None)]))

                    NCH = 2
                    nacc = {"B": 2, "C": 1, "D": 2}[kind]
                    steps = []
                    pending_b = [None]
                    for ti, (q0, ch) in enumerate(qts):
                        t5 = min(q0 // 512, 4)
                        if nacc == 2:
                            accs = [(ti % 2) * 2, (ti % 2) * 2 + 1]
                        else:
                            accs = [ti % 2]
                        nch = len(ch)
                        for si_, c0 in enumerate(range(0, nch, NCH)):
                            subs = []
                            for ci in range(c0, min(c0 + NCH, nch)):
                                kt, qa, qb, bcol = ch[ci]
                                subs.append(dict(kt=kt, qa=qa, qb=qb, bcol=bcol, first=(ci == 0), last=(ci == nch - 1)))
                            steps.append(dict(q0=q0, t5=t5, accs=accs, subs=subs))
                            if si_ == 2 and pending_b[0] is not None:
                                steps.append(pending_b[0])
                                pending_b[0] = None
                        if pending_b[0] is not None:
                            steps.append(pending_b[0])
                            pending_b[0] = None
                        ea = dict(epi=True, part="a", q0=q0, t5=t5, accs=accs)
                        steps.append(ea)
                        if kind == "D":
                            pending_b[0] = dict(epi=True, part="b", ref=ea, q0=q0, t5=t5, accs=accs)
                    if pending_b[0] is not None:
                        steps.append(pending_b[0])
                        pending_b[0] = None

                    sring = [2, 3, 4, 5, 6, 7] if kind == "C" else [4, 5, 6, 7]
                    scount = [0]

                    def do_qk(st):
                        q0, t5 = st["q0"], st["t5"]
                        fns = []
                        rd = [QTB[t5], constB]
                        wr = []
                        for sub in st["subs"]:
                            si = sring[scount[0] % len(sring)]
                            scount[0] += 1
                            sub["si"] = si
                            rd.append(KTB[sub["kt"] // 4])
                            wr.append(pb[si])

                        for sub in st["subs"]:
                            si, kt, qa, qb, bcol = sub["si"], sub["kt"], sub["qa"], sub["qb"], sub["bcol"]
                            o3 = psum[si][:, 0:512].rearrange("p (h q) -> p h q", h=2)[:, :, qa:qb]
                            fns.append(MM(o3, KT[:, kt * 128:(kt + 1) * 128], QT[:, :, q0 + qa:q0 + qb], start=True, stop=(bcol is None)))
                            if bcol is not None:
                                fns.append(MM(o3, ident, tab[:, :, bcol:bcol + (qb - qa)], start=False, stop=True))
                                rd += tabB
                        tr.group(pe, fns, reads=rd, writes=wr)
                        for sub in st["subs"]:
                            P, PB = P_r.next()
                            sub["P"], sub["PB"] = P, PB
                            si, qa, qb = sub["si"], sub["qa"], sub["qb"]
                            tr.op(act, ACTF(both(P[:, 0:512], qa, qb), both(psum[si][:, 0:512], qa, qb), AF.Exp, scale=0.125),
                                  reads=[pb[si]], writes=[PB])

                    def do_pv(st):
                        accs = st["accs"]
                        fns = []
                        rd = [constB]
                        for sub in st["subs"]:
                            kt, qa, qb, P = sub["kt"], sub["qa"], sub["qb"], sub["P"]
                            rd += [sub["PB"], VB[kt // 4]]
                            f, la = sub["first"], sub["last"]
                            if kind == "B":
                                for hf in range(2):
                                    fns.append(MM(psum[accs[hf]][:, qa:qb], Vp[:, kt, hf * 128:(hf + 1) * 128],
                                                  P[:, hf * QW + qa:hf * QW + qb], start=f, stop=la))
                            elif kind == "C":
                                fns.append(MM(psum[accs[0]][:, 0:512], Vp[:, kt, 0:128], P[:, 0:512], start=f, stop=la))
                            else:
                                fns.append(MM(psum[accs[0]][:, 0:512], Vp[:, kt, 0:128], P[:, 0:512], start=f, stop=la))
                                fns.append(MM(psum[accs[1]][:, 0:512], ones, P[:, 0:512], start=f, stop=la))
                        tr.group(pe, fns, reads=rd, writes=[pb[a] for a in accs])

                    def do_epi(st):
                        q0, accs, t5 = st["q0"], st["accs"], st["t5"]
                        qn = QW
                        ywr = [yTB[pr][t5]]
                        grd = [GTB[t5]]
                        ya, yaB = ya_r.next()
                        yb, ybB = yb_r.next()
                        if kind == "B":
                            for hf in range(2):
                                o = accs[hf]
                                tr.op(dve, RCP(ya[0:64, hf * QW:hf * QW + qn], psum[o][64:128, 0:qn]), reads=[pb[o]], writes=[yaB])
                                tr.op(dve, TT(yb[hf * 64:(hf + 1) * 64, 0:qn], psum[o][0:64, 0:qn], ya[0:64, hf * QW:hf * QW + qn], ALU.mult),
                                      reads=[pb[o], yaB], writes=[ybB])
                            tr.op(dve, TT(yT[:, pr, q0:q0 + qn], yb[:, 0:qn], GT[:, q0:q0 + qn], ALU.mult),
                                  reads=[ybB] + grd, writes=ywr)
                        elif kind == "C":
                            o = accs[0]
                            tr.op(dve, RCP(ya[0:64, 0:512], psum[o][64:128, 0:512]), reads=[pb[o]], writes=[yaB])
                            for hf in range(2):
                                tr.op(dve, TT(yb[hf * 64:(hf + 1) * 64, 0:qn], psum[o][0:64, hf * QW:hf * QW + qn],
                                              ya[0:64, hf * QW:hf * QW + qn], ALU.mult), reads=[pb[o], yaB], writes=[ybB])
                            tr.op(dve, TT(yT[:, pr, q0:q0 + qn], yb[:, 0:qn], GT[:, q0:q0 + qn], ALU.mult),
                                  reads=[ybB] + grd, writes=ywr)
                        else:
                            o, dn = accs
                            tr.op(dve, RCP(ya[:, 0:512], psum[dn][:, 0:512]), reads=[pb[dn]], writes=[yaB])
                            tr.op(dve, TT(ya[:, 0:512], psum[o][:, 0:512], ya[:, 0:512], ALU.mult), reads=[pb[o]], writes=[yaB])
                            tr.op(dve, STT(yb[:, 0:qn], ya[:, QW:QW + qn], neglam, ya[:, 0:qn], ALU.mult, ALU.add),
                                  reads=[yaB, smallB], writes=[ybB])
                            sq, sqB = sq_r.next()
                            tr.op(act, ACTF(sq[:, 0:qn], yb[:, 0:qn], AF.Square), reads=[ybB], writes=[sqB])
                            st["bufs"] = (ya, yaB, yb, ybB, sq, sqB)

                    def do_epi_b(st):
                        q0, accs, t5 = st["q0"], st["accs"], st["t5"]
                        qn = QW
                        o, dn = accs
                        ya, yaB, yb, ybB, sq, sqB = st["ref"]["bufs"]
                        tr.group(pe, [MM(psum[dn][:, 0:qn], ones, sq[:, 0:qn])], reads=[sqB, constB], writes=[pb[dn]])
                        tr.op(act, ACTF(ya[:, 0:qn], psum[dn][:, 0:qn], AF.Ln, scale=1.0 / 128, bias=epsc), reads=[pb[dn], smallB], writes=[yaB])
                        tr.op(act, ACTF(ya[:, 0:qn], ya[:, 0:qn], AF.Exp, scale=-0.5), reads=[], writes=[yaB])
                        tr.op(dve, TT(yb[:, 0:qn], yb[:, 0:qn], ya[:, 0:qn], ALU.mult), reads=[yaB], writes=[ybB])
                        tr.op(dve, STT(yT[:, pr, q0:q0 + qn], yb[:, 0:qn], gsub, GT[:, q0:q0 + qn], ALU.mult, ALU.mult),
                              reads=[ybB, smallB, GTB[t5]], writes=[yTB[pr][t5]])

                    LOOK = 2 if kind == "C" else 1
                    qk_list = [s_ for s_ in steps if "epi" not in s_]
                    qi = 0
                    done = 0
                    for s_ in steps:
                        if "epi" in s_:
                            if s_["part"] == "a":
                                do_epi(s_)
                            else:
                                do_epi_b(s_)
                            continue
                        while qi < len(qk_list) and qi <= done + LOOK:
                            do_qk(qk_list[qi])
                            qi += 1
                        do_pv(s_)
                        done += 1

            attn_branch("B")
            dump(1)
            merge_branch(1)
            attn_branch("C")
            dump(2)
            merge_branch(2)
            attn_branch("D")
            dump(3)
            merge_branch(3)

            ar.reset()
            if debug and l == 0:
                ddm = tr.dsem("dbgm")
                tr.dma(pool, ddm, [DMA(ddbgm.rearrange("c p t -> p c t"), mT[:, :, :])],
                       reads=[mTB[c][t] for c in range(8) for t in range(5)], writes=[outB])
            screp = ar.bf(8 * 128).rearrange("p (k n) -> p k n", k=8)
            sccrep = ar.bf(8 * 128).rearrange("p (k n) -> p k n", k=8)
            repB = Buf("rep")
            for kc in range(8):
                tr.op(dve, CP(screp[:, kc, :], s2[:, kc, 0:1].to_broadcast([128, 128])), reads=[s2B], writes=[repB])
                tr.op(dve, CP(sccrep[:, kc, :], s2[:, kc, 1:2].to_broadcast([128, 128])), reads=[s2B], writes=[repB])
            tr.dma(pool, bgd, [DMA(bgrow[:, :], dbg_rows[:, l * DM:(l + 1) * DM])], writes=[bgB])
            gx = ar.f32(1024)
            gc = ar.f32(1024)
            gB = Buf("gates")
            for pi in range(2):
                w_ap, wb = wget(f"adag{l}_{pi}")
                w = w3(w_ap, 8, 512)
                for which, (rep, dst) in enumerate(((screp, gx), (sccrep, gc))):
                    if which == 1 and not need_ctx:
                        continue
                    psi = pi * 2 + which
                    fns = [MM(psum[psi][:, :], rep[:, kc, :], w[:, kc, :], start=(kc == 0), stop=False) for kc in range(8)]
                    fns.append(MM(psum[psi][:, :], ones[0:1, :], bgrow[0:1, pi * 512:(pi + 1) * 512], start=False, stop=True))
                    tr.group(pe, fns, reads=[wb, repB, constB, bgB], writes=[pb[psi]])
                    tr.op(act, ACTF(dst[:, pi * 512:(pi + 1) * 512], psum[psi][:, :], AF.Copy), reads=[pb[psi]], writes=[gB])
            if debug and l == 0:
                ddg = tr.dsem("dbgg")
                tr.dma(sp, ddg, [DMA(ddbgg[:, 0:1024], gx), DMA(ddbgg[:, 1024:2048], gc)], reads=[gB], writes=[outB])
            xo_r = Ring([ar.f32(512) for _ in range(2)], "xo")
            res_r = Ring([ar.f32(512) for _ in range(2)], "res")
            tm_r = Ring([ar.f32(512) for _ in range(2)], "otm")
            ntile = 18 if need_ctx else 16
            cnt = 0
            for ph in range(2):
                w_ap, wb = wget(f"wo{l}_{ph}")
                w = w3(w_ap, 8, 512)
                for i in range(ntile):
                    psi = 4 + cnt % 4
                    t5 = min(i // 4, 4)
                    tr.group(pe, [MM(psum[psi][:, :], mT[:, kc, i * 128:(i + 1) * 128], w[:, kc, :], start=(kc == 0), stop=(kc == 7))
                                  for kc in range(8)], reads=[wb] + [mTB[kc][t5] for kc in range(8)], writes=[pb[psi]])
                    xo, xoB = xo_r.next()
                    if l == 0:
                        src = dx[i * 128:(i + 1) * 128, ph * 512:(ph + 1) * 512] if i < 16 else \
                            dctx[(i - 16) * 128:(i - 15) * 128, ph * 512:(ph + 1) * 512]
                        rd = []
                    else:
                        src = dx1[i * 128:(i + 1) * 128, ph * 512:(ph + 1) * 512]
                        rd = [x1B[i]]
                    tr.dma(sp, xld[cnt % 2], [DMA(xo, src)], reads=rd, writes=[xoB])
                    gate = gx if i < 16 else gc
                    tm, tmB = tm_r.next()
                    rs_, rsB_ = res_r.next()
                    tr.op(dve, TT(tm, psum[psi][:, :], gate[:, ph * 512:(ph + 1) * 512], ALU.mult), reads=[pb[psi], gB], writes=[tmB])
                    tr.op(dve, TT(rs_, tm, xo, ALU.add), reads=[tmB, xoB], writes=[rsB_])
                    if l == NL - 1:
                        tr.dma(sp, std[cnt % 2], [DMA(dout[i * 128:(i + 1) * 128, ph * 512:(ph + 1) * 512], rs_)],
                               reads=[rsB_], writes=[outB])
                    else:
                        tr.dma(sp, std[cnt % 2], [DMA(dx1[i * 128:(i + 1) * 128, ph * 512:(ph + 1) * 512], rs_)],
                               reads=[rsB_], writes=[x1B[i]])
                    cnt += 1

        for l in range(NL):
            run_layer(l, l < NL - 1)
        tr.barrier()

        block = es.enter_context(nc.Block())

        @block.tensor
        def _(h):
            Tracer.replay(pe, h)

        @block.scalar
        def _(h):
            Tracer.replay(act, h)

        @block.vector
        def _(h):
            Tracer.replay(dve, h)

        @block.gpsimd
        def _(h):
            Tracer.replay(pool, h)

        @block.sync
        def _(h):
            Tracer.replay(sp, h)
    return nc


_CACHE = {}


def _prep_inputs(inputs):
    f = lambda a: np.ascontiguousarray(np.asarray(a, dtype=np.float32))
    x, c, ctx, c_ctx = f(inputs["x"]), f(inputs["c"]), f(inputs["ctx"]), f(inputs["c_ctx"])
    cf, cb = _host_consts()
    w_br = np.ascontiguousarray(np.stack([f(inputs["w_br_a"]), f(inputs["w_br_b"]), f(inputs["w_br_c"]), f(inputs["w_br_d"])], axis=1))
    vecs = np.zeros((128, NL * NVL), np.float32)

    def fm(v, n):
        return np.ascontiguousarray(v.reshape(n, 128).T)

    for l in range(NL):
        o = l * NVL
        vecs[:, o + V_G:o + V_G + 8] = fm(f(inputs["norm_g"])[l], 8)
        b_ada = f(inputs["b_ada"])[l]
        vecs[:, o + V_BSH:o + V_BSH + 8] = fm(b_ada[0:1024], 8)
        vecs[:, o + V_BSC:o + V_BSC + 8] = fm(b_ada[1024:2048], 8)
        vecs[:, o + V_BM:o + V_BM + 32] = fm(f(inputs["b_merge"])[l], 32)
        vecs[:, o + V_CB:o + V_CB + 4] = fm(f(inputs["conv_b"])[l], 4)
        vecs[:, o + V_LG:o + V_LG + 4] = fm(f(inputs["conv_ln_g"])[l], 4)
        vecs[:, o + V_LB:o + V_LB + 4] = fm(f(inputs["conv_ln_b"])[l], 4)
        cw = f(inputs["conv_w"])[l]
        vecs[:, o + V_CW:o + V_CW + 124] = cw.T.reshape(4, 128, 31).transpose(1, 0, 2).reshape(128, 124)
        for nm, off in (("na_qn_g", V_NAQ), ("na_kn_g", V_NAK), ("gqa_qn_g", V_GQ), ("gqa_kn_g", V_GK),
                        ("diff_qn_g", V_DQ), ("diff_kn_g", V_DK)):
            vecs[:, o + off] = np.tile(f(inputs[nm])[l], 2)
        vecs[:, o + V_SUB] = f(inputs["diff_subln_g"])[l]
        for k, nm in enumerate(("lam_q1", "lam_k1", "lam_q2", "lam_k2")):
            vecs[:, o + V_L + 64 * k:o + V_L + 64 * (k + 1)] = f(inputs[nm])[l][None, :]
    bgrow = np.ascontiguousarray(f(inputs["b_ada"])[:, 2048:3072].reshape(1, NL * DM))
    shared = {"w_ada": f(inputs["w_ada"]), "w_in": f(inputs["w_in"]), "w_br": w_br, "w_out": f(inputs["w_out"]),
              "vecs": vecs, "bgrow": bgrow, "rpb": f(inputs["na_rpb"]), "cf": cf, "cb": cb}
    maps = []
    for b in range(8):
        cT = np.concatenate([fm(c[b], 8), fm(c_ctx, 8)], axis=1)
        m = dict(shared)
        m.update({"x": x[b], "ctx": ctx[b], "cT": np.ascontiguousarray(cT)})
        maps.append(m)
    return maps


def kernel(**inputs):
    if "nc" not in _CACHE:
        _CACHE["nc"] = build_program(False)
    maps = _prep_inputs(inputs)
    res = run_bass_kernel_spmd(_CACHE["nc"], maps, core_ids=list(range(8)))
    return np.stack([np.asarray(r["out"], dtype=np.float32) for r in res.results], axis=0)
```

```python
import math
import numpy as np
from contextlib import ExitStack
import concourse.bass as bass
import concourse.mybir as mybir
from concourse.bass_utils import run_bass_kernel_spmd

F32 = mybir.dt.float32
BF16 = mybir.dt.bfloat16
AF = mybir.ActivationFunctionType
ALU = mybir.AluOpType

DM = 1024
S = 2048
LC = 256
T = S + LC
NL = 2
INW = 11008
EPS = 1e-6
NEGM = -30000.0
NVL = 455
NJ = 26
TILES = [(0, 512), (512, 512), (1024, 512), (1536, 512), (2048, 256)]

V_G, V_BSH, V_BSC, V_BM, V_CB, V_LG, V_LB, V_CW = 0, 8, 16, 24, 56, 60, 64, 68
V_NAQ, V_NAK, V_GQ, V_GK, V_DQ, V_DK, V_SUB = 192, 193, 194, 195, 196, 197, 198
V_L = 199


class Buf:
    __slots__ = ("w", "r", "name")

    def __init__(self, name=""):
        self.w = None
        self.r = {}
        self.name = name


class DSem:
    def __init__(self, h):
        self.h = h
        self.count = 0


class Eng:
    def __init__(self, tr, name):
        self.tr = tr
        self.name = name
        self.items = []
        self.seen = {}
        self.sems = []
        self.count = 0
        self.newsem()

    def newsem(self):
        h = self.tr.es.enter_context(self.tr.nc.semaphore(f"s_{self.name}{len(self.sems)}"))
        self.sems.append(h)
        self.count = 0


class Tracer:
    def __init__(self, nc, es):
        self.nc = nc
        self.es = es
        self.pe = Eng(self, "pe")
        self.act = Eng(self, "act")
        self.dve = Eng(self, "dve")
        self.pool = Eng(self, "pool")
        self.sp = Eng(self, "sp")
        self.engs = [self.pe, self.act, self.dve, self.pool, self.sp]
        self.dsems = []

    def dsem(self, name):
        d = DSem(self.es.enter_context(self.nc.semaphore("d_" + name)))
        self.dsems.append(d)
        return d

    def _deps(self, eng, reads, writes):
        need = {}

        def add(tok):
            if tok is None:
                return
            sem, val, src = tok
            if src is eng and eng.name == "pe":
                return
            k = id(sem)
            if k not in need or need[k][1] < val:
                need[k] = (sem, val)

        for b in reads:
            add(b.w)
        for b in writes:
            add(b.w)
            for t in b.r.values():
                add(t)
        for k, (sem, val) in need.items():
            if eng.seen.get(k, 0) < val:
                eng.items.append(("wait", sem, val))
                eng.seen[k] = val

    @staticmethod
    def _commit(tok, reads, writes):
        for b in writes:
            b.w = tok
            b.r = {}
        k = id(tok[0])
        for b in reads:
            b.r[k] = tok

    def op(self, eng, fn, reads=(), writes=()):
        self._deps(eng, reads, writes)
        if eng.count >= 16000:
            eng.newsem()
        eng.count += 1
        tok = (eng.sems[-1], eng.count, eng)
        eng.items.append(("ins", fn, eng.sems[-1], 1))
        self._commit(tok, reads, writes)
        return tok

    def group(self, eng, fns, reads=(), writes=()):
        self._deps(eng, reads, writes)
        if eng.count >= 16000:
            eng.newsem()
        for f in fns[:-1]:
            eng.items.append(("ins", f, None, 0))
        eng.count += 1
        tok = (eng.sems[-1], eng.count, eng)
        eng.items.append(("ins", fns[-1], eng.sems[-1], 1))
        self._commit(tok, reads, writes)
        return tok

    def dma(self, eng, dsem, fns, reads=(), writes=()):
        self._deps(eng, reads, writes)
        for f in fns:
            eng.items.append(("ins", f, dsem.h, 16))
            dsem.count += 16
        tok = (dsem.h, dsem.count, None)
        self._commit(tok, reads, writes)
        return tok

    def barrier(self):
        toks = [(e.sems[-1], e.count, e) for e in self.engs if e.count > 0]
        toks += [(d.h, d.count, None) for d in self.dsems if d.count > 0]
        for e in self.engs:
            for sem, val, src in toks:
                if src is e:
                    continue
                k = id(sem)
                if e.seen.get(k, 0) < val:
                    e.items.append(("wait", sem, val))
                    e.seen[k] = val

    @staticmethod
    def replay(eng, h):
        for it in eng.items:
            if it[0] == "wait":
                h.wait_ge(it[1], it[2])
            else:
                ins = it[1](h)
                if it[2] is not None:
                    ins.then_inc(it[2], it[3])


def MM(out, lhsT, rhs, start=True, stop=True):
    return lambda h: h.matmul(out, lhsT=lhsT, rhs=rhs, start=start, stop=stop)


def TRN(out, in_, ident):
    return lambda h: h.transpose(out, in_, ident)


def ACTF(out, in_, func, **kw):
    return lambda h: h.activation(out=out, in_=in_, func=func, **kw)


def TT(out, in0, in1, op):
    return lambda h: h.tensor_tensor(out=out, in0=in0, in1=in1, op=op)


def TS(out, in0, s1, s2=None, op0=ALU.mult, op1=None):
    if op1 is None:
        return lambda h: h.tensor_scalar(out=out, in0=in0, scalar1=s1, scalar2=None, op0=op0)
    return lambda h: h.tensor_scalar(out=out, in0=in0, scalar1=s1, scalar2=s2, op0=op0, op1=op1)


def STT(out, in0, scalar, in1, op0, op1):
    return lambda h: h.scalar_tensor_tensor(out=out, in0=in0, scalar=scalar, in1=in1, op0=op0, op1=op1)


def CP(out, in_):
    return lambda h: h.tensor_copy(out=out, in_=in_)


def RCP(out, in_):
    return lambda h: h.reciprocal(out=out, in_=in_)


def MSET(ap, v):
    return lambda h: h.memset(ap, v)


def DMA(out, in_):
    return lambda h: h.dma_start(out=out, in_=in_)


class Ring:
    def __init__(self, aps, name):
        self.aps = aps
        self.bufs = [Buf(f"{name}{i}") for i in range(len(aps))]
        self.i = 0

    def next(self):
        k = self.i % len(self.aps)
        self.i += 1
        return self.aps[k], self.bufs[k]


def _host_consts():
    identf = np.eye(128, dtype=np.float32)
    p = np.arange(128)
    hd = p % 64
    half = hd // 32
    fi = (hd % 32) % 16
    freq = (10000.0 ** (-(2.0 * fi) / 32.0)).astype(np.float32)
    t = np.arange(S)
    rows = (t // 64).astype(np.float32)
    cols = (t % 64).astype(np.float32)
    pos = np.where(half[:, None] == 0, rows[None, :], cols[None, :]).astype(np.float32)
    ang = (pos * freq[:, None]).astype(np.float32)
    cos = np.cos(ang).astype(np.float32)
    sgn = np.where((hd % 32) < 16, -1.0, 1.0).astype(np.float32)
    sin = (np.sin(ang).astype(np.float32) * sgn[:, None]).astype(np.float32)
    selR = np.zeros((128, 64), np.float32)

    def delta(hf, jj):
        return (4 - jj) + hf if jj < 10 else (7 - (jj - 10)) + hf

    for hf in range(2):
        for jj in range(NJ):
            dr = delta(hf, jj) + 7
            if 0 <= dr <= 14:
                selR[dr, hf * NJ + jj] = 8.0
    cf = np.concatenate([identf, cos, sin, selR], axis=1)

    ident = identf
    blk = ((p[:, None] // 64) == (p[None, :] // 64)).astype(np.float32)
    partner = np.where((p % 32) < 16, p + 16, p - 16)
    perm = np.zeros((128, 128), np.float32)
    perm[partner, p] = 1.0
    band = np.zeros((128, 128), np.float32)
    for c in range(31):
        band[c, c + 48] = 1.0
    mask = np.zeros((128, 2, NJ, 64), np.float32)
    qc = np.arange(64)
    c0 = np.clip(qc - 8, 0, 48)
    for kc in range(64):
        colok = (kc >= c0) & (kc < c0 + 16)
        for hf in range(2):
            for jj in range(NJ):
                d = delta(hf, jj)
                ok = (-4 <= d <= 3) if jj < 10 else (-7 <= d <= 7)
                mask[kc, hf, jj, :] = np.where(colok & ok, 0.0, NEGM)
    cb = np.concatenate([ident, blk, perm, band, np.ones((128, 128), np.float32),
                         mask.reshape(128, -1)], axis=1)
    return np.ascontiguousarray(cf), np.ascontiguousarray(cb)


CF_ID, CF_COS, CF_SIN, CF_SEL, CF_N = 0, 128, 128 + 2048, 128 + 4096, 128 + 4096 + 64
CB_ID, CB_BLK, CB_PERM, CB_BAND, CB_ONES, CB_MASK, CB_N = 0, 128, 256, 384, 512, 640, 640 + 2 * NJ * 64


def build_program(debug=False):
    nc = bass.Bass("TRN2", target_bir_lowering=False)
    dx = nc.dram_tensor("x", [S, DM], F32, kind="ExternalInput").ap()
    dctx = nc.dram_tensor("ctx", [LC, DM], F32, kind="ExternalInput").ap()
    dcT = nc.dram_tensor("cT", [128, 16], F32, kind="ExternalInput").ap()
    dwada = nc.dram_tensor("w_ada", [NL, DM, 3 * DM], F32, kind="ExternalInput").ap()
    dwin = nc.dram_tensor("w_in", [NL, DM, INW], F32, kind="ExternalInput").ap()
    dwbr = nc.dram_tensor("w_br", [NL, 4, 512, DM], F32, kind="ExternalInput").ap()
    dwout = nc.dram_tensor("w_out", [NL, DM, DM], F32, kind="ExternalInput").ap()
    dvecs = nc.dram_tensor("vecs", [128, NL * NVL], F32, kind="ExternalInput").ap()
    dbg_rows = nc.dram_tensor("bgrow", [1, NL * DM], F32, kind="ExternalInput").ap()
    drpb = nc.dram_tensor("rpb", [NL, 8, 15, 31], F32, kind="ExternalInput").ap()
    dcf = nc.dram_tensor("cf", [128, CF_N], F32, kind="ExternalInput").ap()
    dcb = nc.dram_tensor("cb", [128, CB_N], F32, kind="ExternalInput").ap()
    dout = nc.dram_tensor("out", [S, DM], F32, kind="ExternalOutput").ap()
    dx1 = nc.dram_tensor("x1s", [T, DM], F32, kind="ExternalOutput" if debug else "Internal").ap()
    ddbg = None
    if debug:
        ddbg = nc.dram_tensor("dbg", [8, 4, 128, T], F32, kind="ExternalOutput").ap()
        ddbgm = nc.dram_tensor("dbgm", [8, 128, T], F32, kind="ExternalOutput").ap()
        ddbgg = nc.dram_tensor("dbgg", [128, 2048], F32, kind="ExternalOutput").ap()

    es = ExitStack()
    with es:
        tr = Tracer(nc, es)
        pe, act, dve, pool, sp = tr.pe, tr.act, tr.dve, tr.pool, tr.sp

        def sb(name, shape, dt):
            return es.enter_context(nc.sbuf_tensor("sb_" + name, shape, dt))

        hT = sb("hT", [128, 8, T], BF16)
        mT = sb("mT", [128, 8, T], BF16)
        yT = sb("yT", [128, 4, T], BF16)
        cfs = sb("cfs", [128, CF_N], F32)
        cbs = sb("cbs", [128, CB_N], BF16)
        vecs = sb("vecs", [128, NL * NVL], F32)
        bgrow = sb("bgrow", [1, DM], BF16)
        bgB = Buf("bgrow")
        bgd = tr.dsem("bgrow")
        modv = sb("modv", [128, 4, 8], F32)
        s2 = sb("s2", [128, 8, 2], BF16)
        small = sb("small", [128, 64], F32)
        wring_t = [sb(f"wr{i}", [128, 4096], BF16) for i in range(3)]
        ARENA_N = 31780
        arena = sb("arena", [128, ARENA_N], BF16)
        psum = [es.enter_context(nc.psum_tensor(f"ps{i}", [128, 512], F32)) for i in range(8)]
        pb = [Buf(f"ps{i}") for i in range(8)]

        hTB = [Buf(f"hT{t}") for t in range(5)]
        hTB2 = [Buf(f"hTa{t}") for t in range(5)]
        mTB = [[Buf(f"mT{c}_{t}") for t in range(5)] for c in range(8)]
        yTB = [[Buf(f"yT{c}_{t}") for t in range(5)] for c in range(4)]
        constB = Buf("const")
        vecB = Buf("vecs")
        modB = Buf("modv")
        s2B = Buf("s2")
        smallB = Buf("small")
        x1B = [Buf(f"x1_{i}") for i in range(18)]
        outB = Buf("out")

        identf = cfs[:, CF_ID:CF_ID + 128]
        COS = cfs[:, CF_COS:CF_COS + S]
        SIN = cfs[:, CF_SIN:CF_SIN + S]
        selR = cfs[:, CF_SEL:CF_SEL + 64]
        ident = cbs[:, CB_ID:CB_ID + 128]
        blk = cbs[:, CB_BLK:CB_BLK + 128]
        perm = cbs[:, CB_PERM:CB_PERM + 128]
        band = cbs[:, CB_BAND:CB_BAND + 128]
        ones = cbs[:, CB_ONES:CB_ONES + 128]
        maskT = cbs[:, CB_MASK:CB_MASK + 2 * NJ * 64].rearrange("p (t j q) -> p t j q", t=2, j=NJ)

        class Arena:
            def __init__(self):
                self.off = 0

            def reset(self):
                tr.barrier()
                self.off = 0

            def bf(self, n):
                ap = arena[:, self.off:self.off + n]
                self.off += n
                assert self.off <= ARENA_N, self.off
                return ap

            def f32(self, n):
                ap = arena[:, self.off:self.off + 2 * n].bitcast(F32)
                self.off += 2 * n
                assert self.off <= ARENA_N, self.off
                return ap

        ar = Arena()

        wsl = [Buf(f"w{i}") for i in range(3)]
        wds = [tr.dsem(f"w{i}") for i in range(3)]
        pieces = []

        def w3(slot, kc, n):
            return slot[:, 0:kc * n].rearrange("p (k n) -> p k n", k=kc)

        def piece_cols(tag, src2d, cols):
            specs = []
            for (do, c0, n) in cols:
                specs.append((lambda sl, do=do, n=n: w3(sl, 8, 512)[:, :, do:do + n],
                              src2d[:, c0:c0 + n].rearrange("(k p) n -> p k n", p=128)))
            pieces.append((tag, specs))

        def layer_pieces(l):
            win = dwin[l]
            for pi in range(4):
                piece_cols(f"ada{l}_{pi}", dwada[l], [(0, pi * 512, 512)])
            for j in range(4):
                piece_cols(f"A{l}_{j}", win, [(0, j * 128, 128), (128, 512 + j * 128, 128), (256, 1024 + j * 128, 128)])
            merge_pieces(l, 0)
            for hp in range(4):
                piece_cols(f"B{l}_{hp}", win, [(0, 1536 + hp * 128, 128), (128, 2048 + hp * 128, 128),
                                               (256, 2560 + hp * 128, 128), (384, 3072 + hp * 128, 128)])
            merge_pieces(l, 1)
            for cp in range(4):
                n = cp // 2
                piece_cols(f"C{l}_{cp}", win, [(0, 3584 + cp * 128, 128), (128, 4096 + n * 64, 64), (192, 4096 + n * 64, 64),
                                               (256, 4224 + n * 64, 64), (384, 4352 + cp * 128, 128)])
            merge_pieces(l, 2)
            for hd in range(4):
                piece_cols(f"D{l}_{hd}", win, [(0, 4864 + hd * 128, 128), (128, 5376 + hd * 128, 128),
                                               (256, 5888 + hd * 128, 128), (384, 6400 + hd * 128, 128)])
            merge_pieces(l, 3)
            for pi in range(2):
                piece_cols(f"adag{l}_{pi}", dwada[l], [(0, 2048 + pi * 512, 512)])
            for ph in range(2):
                piece_cols(f"wo{l}_{ph}", dwout[l], [(0, ph * 512, 512)])

        def merge_pieces(l, i):
            for hf in range(2):
                pieces.append((f"br{l}_{i}_{hf}", [(lambda sl: w3(sl, 4, 1024),
                                                    dwbr[l, i].rearrange("(k p) n -> p k n", p=128))]))
                piece_cols(f"lg{l}_{i}_{hf}", dwin[l], [(0, 6912 + i * 1024 + hf * 512, 512)])

        for l in range(NL):
            layer_pieces(l)
        wstate = {"issued": 0, "next": 0}

        def w_issue(upto):
            while wstate["issued"] < min(upto, len(pieces)):
                i = wstate["issued"]
                tag, specs = pieces[i]
                sl = wring_t[i % 3]
                tr.dma(pool, wds[i % 3], [DMA(f(sl), src) for (f, src) in specs], writes=[wsl[i % 3]])
                wstate["issued"] += 1

        def wget(tag):
            i = wstate["next"]
            assert pieces[i][0] == tag, (pieces[i][0], tag)
            w_issue(i + 2)
            wstate["next"] += 1
            return wring_t[i % 3], wsl[i % 3]

        d_init = tr.dsem("init")
        tr.dma(sp, d_init, [DMA(cfs[:, :], dcf), DMA(vecs[:, :], dvecs), DMA(small[:, 0:16], dcT)],
               writes=[constB, vecB, smallB])
        d_init2 = tr.dsem("init2")
        tr.dma(pool, d_init2, [DMA(cbs[:, :], dcb)], writes=[constB])
        w_issue(2)

        def vcol(l, off, n=1):
            return vecs[:, l * NVL + off: l * NVL + off + n]

        xld = [tr.dsem(f"xld{i}") for i in range(3)]
        std = [tr.dsem(f"st{i}") for i in range(2)]

        def run_layer(l, need_ctx):
            lam_init = 0.8 - 0.6 * math.exp(-0.3 * l)
            qtiles = TILES if need_ctx else TILES[:4]

            ar.reset()
            tr.op(act, ACTF(s2[:, :, 0], small[:, 0:8], AF.Silu), reads=[smallB], writes=[s2B])
            tr.op(act, ACTF(s2[:, :, 1], small[:, 8:16], AF.Silu), reads=[smallB], writes=[s2B])
            pm = psum[7]
            for pi in range(4):
                wsl_ap, wb = wget(f"ada{l}_{pi}")
                w = w3(wsl_ap, 8, 512)
                for fc in range(4):
                    g = pi * 4 + fc
                    tr.group(pe, [MM(pm[:, g * 2:g * 2 + 2], w[:, kc, fc * 128:(fc + 1) * 128], s2[:, kc, :],
                                     start=(kc == 0), stop=(kc == 7)) for kc in range(8)],
                             reads=[wb, s2B], writes=[pb[7]])
            pmv = pm[:, 0:32].rearrange("p (f w) -> p f w", w=2)
            tmp8 = small[:, 16:24]
            for which in range(2):
                tr.op(dve, TT(modv[:, 2 * which + 1, :], pmv[:, 0:8, which], vcol(l, V_BSH, 8), ALU.add),
                      reads=[pb[7], vecB], writes=[modB])
                tr.op(dve, TT(tmp8, pmv[:, 8:16, which], vcol(l, V_BSC, 8), ALU.add),
                      reads=[pb[7], vecB], writes=[smallB])
                tr.op(dve, STT(modv[:, 2 * which, :], tmp8, 1.0, vcol(l, V_G, 8), ALU.add, ALU.mult),
                      reads=[smallB, vecB], writes=[modB])
            lt = small[:, 24:28]
            prod = ar.f32(64)
            prodB = Buf("prod")
            for k in range(2):
                tr.op(dve, TT(prod, vcol(l, V_L + 128 * k, 64), vcol(l, V_L + 128 * k + 64, 64), ALU.mult),
                      reads=[vecB], writes=[prodB])
                tr.op(dve, MSET(lt[:, k:k + 1], 0.0), writes=[smallB])
                tr.op(act, ACTF(prod, prod, AF.Identity, accum_out=lt[:, k:k + 1]), reads=[prodB, smallB], writes=[prodB, smallB])
                tr.op(act, ACTF(lt[:, k:k + 1], lt[:, k:k + 1], AF.Exp), reads=[smallB], writes=[smallB])
            neglam = small[:, 28:29]
            gsub = small[:, 29:30]
            epsc = small[:, 30:31]
            tr.op(dve, MSET(epsc, EPS), writes=[smallB])
            tr.op(dve, TT(lt[:, 2:3], lt[:, 0:1], lt[:, 1:2], ALU.subtract), reads=[smallB], writes=[smallB])
            tr.op(dve, TS(neglam, lt[:, 2:3], lam_init, -1.0, ALU.add, ALU.mult), reads=[smallB], writes=[smallB])
            tr.op(dve, TS(gsub, vcol(l, V_SUB), 1.0 - lam_init), reads=[vecB], writes=[smallB])

            ar.reset()
            xt_r = Ring([ar.f32(1024) for _ in range(3)], "xt")
            xn_r = Ring([ar.f32(1024) for _ in range(3)], "xn")
            junk = ar.bf(1024)
            junkB = Buf("junk")
            st_r = Ring([small[:, 32 + 4 * i: 36 + 4 * i] for i in range(3)], "st")
            def p1_a(i):
                xt, xtB = xt_r.next()
                xn, xnB = xn_r.next()
                stt_, stB = st_r.next()
                if l == 0:
                    src = dx[i * 128:(i + 1) * 128, :] if i < 16 else dctx[(i - 16) * 128:(i - 15) * 128, :]
                    rd = []
                else:
                    src = dx1[i * 128:(i + 1) * 128, :]
                    rd = [x1B[i]]
                tr.dma(sp, xld[i % 3], [DMA(xt, src)], reads=rd, writes=[xtB])
                tr.op(dve, MSET(stt_[:, 0:1], 0.0), writes=[stB])
                tr.op(act, ACTF(junk, xt, AF.Square, accum_out=stt_[:, 0:1]), reads=[xtB, stB], writes=[junkB, stB])
                tr.op(act, ACTF(stt_[:, 1:2], stt_[:, 0:1], AF.Sqrt, scale=1.0 / DM, bias=EPS), reads=[stB], writes=[stB])
                tr.op(dve, RCP(stt_[:, 2:3], stt_[:, 1:2]), reads=[stB], writes=[stB])
                tr.op(dve, TS(xn, xt, stt_[:, 2:3]), reads=[xtB, stB], writes=[xnB])
                return xn, xnB

            def p1_b(i, xn, xnB):
                which = 0 if i < 16 else 1
                for hb in range(2):
                    bk = (2 * i + hb) % 8
                    tr.group(pe, [TRN(psum[bk][:, k4 * 128:(k4 + 1) * 128], xn[:, (hb * 4 + k4) * 128:(hb * 4 + k4 + 1) * 128], identf)
                                  for k4 in range(4)], reads=[xnB, constB], writes=[pb[bk]])
                    for k4 in range(4):
                        kc = hb * 4 + k4
                        dst = hT[:, kc, i * 128:(i + 1) * 128]
                        srcp = psum[bk][:, k4 * 128:(k4 + 1) * 128]
                        A = modv[:, 2 * which, kc:kc + 1]
                        Bc = modv[:, 2 * which + 1, kc:kc + 1]
                        if hb == 0:
                            tr.op(dve, TS(dst, srcp, A, Bc, ALU.mult, ALU.add), reads=[pb[bk], modB], writes=[hTB[i // 4]])
                        else:
                            tr.op(act, ACTF(dst, srcp, AF.Identity, scale=A, bias=Bc), reads=[pb[bk], modB], writes=[hTB2[i // 4]])

            prev = None
            for i in range(18):
                cur = p1_a(i)
                if prev is not None:
                    p1_b(i - 1, *prev)
                prev = cur
            p1_b(17, *prev)

            def proj_fm(ps_i, w, col0, t0, n, wb, t5):
                tr.group(pe, [MM(psum[ps_i][:, 0:n], w[:, kc, col0:col0 + 128], hT[:, kc, t0:t0 + n],
                                 start=(kc == 0), stop=(kc == 7)) for kc in range(8)],
                         reads=[wb, hTB[t5], hTB2[t5]], writes=[pb[ps_i]])

            def merge_branch(i):
                ar.reset()
                g_r = Ring([ar.f32(512) for _ in range(4)], "G")
                t_r = Ring([ar.f32(512) for _ in range(4)], "mt")
                cnt = 0
                for hf in range(2):
                    wbr_ap, wbrB = wget(f"br{l}_{i}_{hf}")
                    wbr = w3(wbr_ap, 4, 1024)
                    wl_ap, wlB = wget(f"lg{l}_{i}_{hf}")
                    wl = w3(wl_ap, 8, 512)
                    for fcl in range(4):
                        fc = hf * 4 + fcl
                        for t5, (t0, n) in enumerate(qtiles):
                            pz = (2 * cnt) % 8
                            pl = (2 * cnt + 1) % 8
                            cnt += 1
                            fz = [MM(psum[pz][:, 0:n], wbr[:, kc, fc * 128:(fc + 1) * 128], yT[:, kc, t0:t0 + n],
                                     start=(kc == 0), stop=(kc == 3)) for kc in range(4)]
                            fl = [MM(psum[pl][:, 0:n], wl[:, kc, fcl * 128:(fcl + 1) * 128], hT[:, kc, t0:t0 + n],
                                     start=(kc == 0), stop=(kc == 7)) for kc in range(8)]
                            tr.group(pe, fz + fl, reads=[wbrB, wlB, hTB[t5], hTB2[t5]] + [yTB[kc][t5] for kc in range(4)],
                                     writes=[pb[pz], pb[pl]])
                            G, GB = g_r.next()
                            tr.op(act, ACTF(G[:, 0:n], psum[pl][:, 0:n], AF.Sigmoid, bias=vcol(l, V_BM + i * 8 + fc)),
                                  reads=[pb[pl], vecB], writes=[GB])
                            if i == 0:
                                tr.op(dve, TT(mT[:, fc, t0:t0 + n], psum[pz][:, 0:n], G[:, 0:n], ALU.mult),
                                      reads=[pb[pz], GB], writes=[mTB[fc][t5]])
                            else:
                                tm, tmB = t_r.next()
                                tr.op(dve, TT(tm[:, 0:n], psum[pz][:, 0:n], G[:, 0:n], ALU.mult), reads=[pb[pz], GB], writes=[tmB])
                                tr.op(pool, TT(mT[:, fc, t0:t0 + n], mT[:, fc, t0:t0 + n], tm[:, 0:n], ALU.add),
                                      reads=[tmB], writes=[mTB[fc][t5]])

            def dump(i):
                if debug:
                    dd = tr.dsem(f"dbg{l}_{i}")
                    tr.dma(pool, dd, [DMA(ddbg[l * 4 + i].rearrange("c p t -> p c t"), yT[:, :, :])],
                           reads=[yTB[c][t] for c in range(4) for t in range(5)], writes=[outB])

            ar.reset()
            HG = 2364
            hglu_r = Ring([ar.bf(HG) for _ in range(2)], "hglu")
            diag_ap = ar.bf(31 * 128).rearrange("p (k n) -> p k n", k=31)
            diagB = Buf("diag")
            sg_r = Ring([ar.f32(512) for _ in range(2)], "sg")
            cbuf = mT[:, :, :].rearrange("p c t -> p (c t)").bitcast(F32).rearrange("p (c t) -> p c t", c=4)

            def cB(j):
                return [mTB[2 * j][t] for t in range(5)] + [mTB[2 * j + 1][t] for t in range(5)]
            segs = [(0, 0, 512), (512, 512, 512), (1024, 1024, 512), (1536, 1536, 512), (2078, 2048, 256)]
            segs = segs if need_ctx else segs[:4]
            for j in range(4):
                w_ap, wb = wget(f"A{l}_{j}")
                w = w3(w_ap, 8, 512)
                for k in range(31):
                    tr.op(dve, TS(diag_ap[:, k, :], ident, vcol(l, V_CW + j * 31 + k)), reads=[constB, vecB], writes=[diagB])
                hg, hgB = hglu_r.next()
                for (a, b) in ((0, 15), (2063, 2093), (2349, 2364)):
                    tr.op(pool, MSET(hg[:, a:b], 0.0), writes=[hgB])
                for t5, (bb, t0, n) in enumerate(segs):
                    proj_fm(0 + (t5 % 2) * 2, w, 0, t0, n, wb, t5)
                    proj_fm(1 + (t5 % 2) * 2, w, 128, t0, n, wb, t5)
                    pa, pg = (t5 % 2) * 2, 1 + (t5 % 2) * 2
                    sg, sgB = sg_r.next()
                    tr.op(act, ACTF(sg[:, 0:n], psum[pg][:, 0:n], AF.Sigmoid), reads=[pb[pg]], writes=[sgB])
                    tr.op(dve, TT(hg[:, bb + 15:bb + 15 + n], psum[pa][:, 0:n], sg[:, 0:n], ALU.mult),
                          reads=[pb[pa], sgB], writes=[hgB])
                for t5, (bb, t0, n) in enumerate(segs):
                    pc = 4 + (t5 % 2)
                    tr.group(pe, [MM(psum[pc][:, 0:n], diag_ap[:, k, :], hg[:, bb + k:bb + k + n], start=(k == 0), stop=(k == 30))
                                  for k in range(31)], reads=[diagB, hgB], writes=[pb[pc]])
                    tr.op(act, ACTF(cbuf[:, j, t0:t0 + n], psum[pc][:, 0:n], AF.Identity, bias=vcol(l, V_CB + j)),
                          reads=[pb[pc], vecB], writes=cB(j))
                    pgt = 6 + (t5 % 2)
                    proj_fm(pgt, w, 256, t0, n, wb, t5)
                    tr.op(act, ACTF(yT[:, j, t0:t0 + n], psum[pgt][:, 0:n], AF.Silu), reads=[pb[pgt]], writes=[yTB[j][t5]])
            ar.reset()
            onesf = ar.f32(128)
            onesfB = Buf("onesf")
            tr.op(dve, MSET(onesf, 1.0), writes=[onesfB])
            sq_r = Ring([ar.f32(512) for _ in range(2)], "csq")
            st_ring = Ring([(ar.f32(512), ar.f32(512), ar.f32(512)) for _ in range(2)], "lnstat")
            d_r = Ring([ar.f32(512) for _ in range(4)], "lnd")
            allc = [b for j in range(4) for b in cB(j)]

            def ln_a(t5, t0, n):
                (mean, rstd, msq), stB2 = st_ring.next()
                tr.group(pe, [MM(psum[0][:, 0:n], onesf, cbuf[:, j, t0:t0 + n], start=(j == 0), stop=(j == 3)) for j in range(4)],
                         reads=allc + [onesfB], writes=[pb[0]])
                for j in range(4):
                    sq, sqB = sq_r.next()
                    tr.op(act, ACTF(sq[:, 0:n], cbuf[:, j, t0:t0 + n], AF.Square), reads=allc, writes=[sqB])
                    tr.group(pe, [MM(psum[1][:, 0:n], onesf, sq[:, 0:n], start=(j == 0), stop=(j == 3))],
                             reads=[sqB, onesfB], writes=[pb[1]])
                tr.op(dve, TS(mean[:, 0:n], psum[0][:, 0:n], 1.0 / 512), reads=[pb[0]], writes=[stB2])
                tr.op(dve, TT(msq[:, 0:n], mean[:, 0:n], mean[:, 0:n], ALU.mult), reads=[stB2], writes=[stB2])
                tr.op(dve, STT(msq[:, 0:n], psum[1][:, 0:n], 1.0 / 512, msq[:, 0:n], ALU.mult, ALU.subtract),
                      reads=[pb[1], stB2], writes=[stB2])
                tr.op(act, ACTF(msq[:, 0:n], msq[:, 0:n], AF.Sqrt, bias=EPS, scale=1.0), reads=[stB2], writes=[stB2])
                tr.op(dve, RCP(rstd[:, 0:n], msq[:, 0:n]), reads=[stB2], writes=[stB2])
                return mean, rstd, stB2

            def ln_b(t5, t0, n, mean, rstd, stB2):
                ds = []
                for j in range(4):
                    d, dB = d_r.next()
                    ds.append((d, dB))
                    tr.op(dve, TT(d[:, 0:n], cbuf[:, j, t0:t0 + n], mean[:, 0:n], ALU.subtract), reads=allc + [stB2], writes=[dB])
                    tr.op(dve, TT(d[:, 0:n], d[:, 0:n], rstd[:, 0:n], ALU.mult), reads=[stB2], writes=[dB])
                for j, (d, dB) in enumerate(ds):
                    tr.op(act, ACTF(d[:, 0:n], d[:, 0:n], AF.Silu, scale=vcol(l, V_LG + j), bias=vcol(l, V_LB + j)),
                          reads=[vecB], writes=[dB])
                for j, (d, dB) in enumerate(ds):
                    tr.op(dve, TT(yT[:, j, t0:t0 + n], yT[:, j, t0:t0 + n], d[:, 0:n], ALU.mult), reads=[dB], writes=[yTB[j][t5]])

            prev_ln = None
            for t5, (t0, n) in enumerate(qtiles):
                cur_ln = (t5, t0, n) + ln_a(t5, t0, n)
                if prev_ln is not None:
                    ln_b(*prev_ln)
                prev_ln = cur_ln
            ln_b(*prev_ln)
            dump(0)
            merge_branch(0)

            def attn_branch(kind):
                ar.reset()
                QT = ar.bf(2 * T).rearrange("p (h t) -> p h t", h=2)
                KT = ar.bf(T)
                GT = ar.bf(T)
                Vp = ar.bf(18 * 256).rearrange("p (k n) -> p k n", k=18)
                QTB = [Buf(f"QT{t}") for t in range(5)]
                KTB = [Buf(f"KT{t}") for t in range(5)]
                GTB = [Buf(f"GT{t}") for t in range(5)]
                VB = [Buf(f"V{g}") for g in range(5)]
                sq_r = Ring([ar.bf(512) for _ in range(2)], "sq")
                rs_r = Ring([ar.f32(512) for _ in range(2)], "rs")
                xn_r2 = Ring([ar.bf(512) for _ in range(2)], "xnb")
                t1_r = Ring([ar.f32(512) for _ in range(1)], "t1")
                t2_r = Ring([ar.f32(512) for _ in range(1)], "t2")
                P_r = Ring([ar.bf(512) for _ in range(4 if kind == "B" else 6)], "P")
                ya_r = Ring([ar.f32(512) for _ in range(1)], "ya")
                yb_r = Ring([ar.f32(512) for _ in range(1)], "yb")
                if kind == "B":
                    tab = ar.bf(2 * NJ * 64).rearrange("p (h n) -> p h n", h=2)
                    tab64 = ar.bf(2 * NJ * 64).rearrange("p (t j q) -> p t j q", t=2, j=NJ)
                    rt2 = ar.bf(64)
                    rp = ar.f32(32)
                    tabB = [Buf("tab0"), Buf("tab1")]
                    tab64B, rt2B, rpB = Buf("tab64"), Buf("rt2"), Buf("rp")
                    rpd = tr.dsem(f"rp{l}")
                tr.op(pool, MSET(QT[:, :, :], 0.0), writes=QTB)
                if kind != "D":
                    tr.op(pool, MSET(Vp[:, :, 64:128], 1.0), writes=VB)
                    tr.op(pool, MSET(Vp[:, :, 192:256], 1.0), writes=VB)
                rope = kind != "B"
                qg = {"B": V_NAQ, "C": V_GQ, "D": V_DQ}[kind]
                kg = {"B": V_NAK, "C": V_GK, "D": V_DK}[kind]

                def normed(psi, dst, dstB, gcol, t0, n, do_rope):
                    sq, sqB = sq_r.next()
                    tr.op(act, ACTF(sq[:, 0:n], psum[psi][:, 0:n], AF.Square), reads=[pb[psi]], writes=[sqB])
                    tr.group(pe, [MM(psum[5][:, 0:n], blk, sq[:, 0:n])], reads=[sqB, constB], writes=[pb[5]])
                    rs, rsB = rs_r.next()
                    tr.op(act, ACTF(rs[:, 0:n], psum[5][:, 0:n], AF.Sqrt, scale=1.0 / 64, bias=EPS), reads=[pb[5]], writes=[rsB])
                    tr.op(dve, RCP(rs[:, 0:n], rs[:, 0:n]), reads=[rsB], writes=[rsB])
                    if not do_rope:
                        for (r0, r1, d_ap) in dst:
                            tr.op(dve, STT(d_ap, psum[psi][r0:r1, 0:n], vecs[r0:r1, l * NVL + gcol:l * NVL + gcol + 1], rs[r0:r1, 0:n],
                                           ALU.mult, ALU.mult), reads=[pb[psi], rsB, vecB], writes=[dstB])
                        return
                    xb, xbB = xn_r2.next()
                    tr.op(dve, STT(xb[:, 0:n], psum[psi][:, 0:n], vcol(l, gcol), rs[:, 0:n], ALU.mult, ALU.mult),
                          reads=[pb[psi], rsB, vecB], writes=[xbB])
                    tr.group(pe, [MM(psum[6][:, 0:n], perm, xb[:, 0:n])], reads=[xbB, constB], writes=[pb[6]])
                    t1, t1B = t1_r.next()
                    t2, t2B = t2_r.next()
                    tr.op(dve, TT(t1[:, 0:n], xb[:, 0:n], COS[:, t0:t0 + n], ALU.mult), reads=[xbB, constB], writes=[t1B])
                    tr.op(dve, TT(t2[:, 0:n], psum[6][:, 0:n], SIN[:, t0:t0 + n], ALU.mult), reads=[pb[6], constB], writes=[t2B])
                    for (r0, r1, d_ap) in dst:
                        tr.op(dve, TT(d_ap, t1[r0:r1, 0:n], t2[r0:r1, 0:n], ALU.add), reads=[t1B, t2B], writes=[dstB])

                for pr in range(4):
                    w_ap, wb = wget(f"{kind}{l}_{pr}")
                    w = w3(w_ap, 8, 512)
                    if kind == "B":
                        for h2 in range(2):
                            h = 2 * pr + h2
                            tr.dma(sp, rpd, [DMA(rp[0:15, 0:31], drpb[l, h])], writes=[rpB])
                            tr.group(pe, [MM(psum[7][0:31, 0:2 * NJ], rp[0:15, 0:31], selR[0:15, 0:2 * NJ])],
                                     reads=[rpB, constB], writes=[pb[7]])
                            tr.op(dve, CP(rt2[0:31, 0:2 * NJ], psum[7][0:31, 0:2 * NJ]), reads=[pb[7]], writes=[rt2B])
                            for g in range(8):
                                bk = g % 4
                                tr.group(pe, [MM(psum[bk][0:64, q8 * 2 * NJ:(q8 + 1) * 2 * NJ],
                                                 band[0:31, 63 - (8 * g + q8):127 - (8 * g + q8)], rt2[0:31, 0:2 * NJ])
                                              for q8 in range(8)], reads=[rt2B, constB], writes=[pb[bk]])
                                pv = psum[bk][0:64, 0:8 * 2 * NJ].rearrange("p (q t j) -> p t j q", q=8, t=2)
                                for hf in range(2):
                                    tr.op(dve, TT(tab64[0:64, hf, :, 8 * g:8 * g + 8], pv[:, hf], maskT[0:64, hf, :, 8 * g:8 * g + 8], ALU.add),
                                          reads=[pb[bk], constB], writes=[tab64B])
                            tr.op(dve, CP(tab[0:64, h2, :], tab64[0:64, 0].rearrange("p j q -> p (j q)")), reads=[tab64B], writes=[tabB[h2]])
                            tr.op(dve, CP(tab[64:128, h2, :], tab64[0:64, 1].rearrange("p j q -> p (j q)")), reads=[tab64B], writes=[tabB[h2]])
                    items = []
                    cnt = [0]

                    def qk_item(col0, dst, dstB, gcol, t0, n, t5, do_rope):
                        psi = cnt[0] % 4
                        cnt[0] += 1
                        st = {}

                        def s0():
                            proj_fm(psi, w, col0, t0, n, wb, t5)
                            sq, sqB = sq_r.next()
                            st["sq"] = (sq, sqB)
                            tr.op(act, ACTF(sq[:, 0:n], psum[psi][:, 0:n], AF.Square), reads=[pb[psi]], writes=[sqB])

                        def s1():
                            sq, sqB = st["sq"]
                            tr.group(pe, [MM(psum[5][:, 0:n], blk, sq[:, 0:n])], reads=[sqB, constB], writes=[pb[5]])
                            rs, rsB = rs_r.next()
                            tr.op(act, ACTF(rs[:, 0:n], psum[5][:, 0:n], AF.Sqrt, scale=1.0 / 64, bias=EPS), reads=[pb[5]], writes=[rsB])
                            tr.op(dve, RCP(rs[:, 0:n], rs[:, 0:n]), reads=[rsB], writes=[rsB])
                            if not do_rope:
                                for (r0, r1, d_ap) in dst:
                                    tr.op(dve, STT(d_ap, psum[psi][r0:r1, 0:n], vecs[r0:r1, l * NVL + gcol:l * NVL + gcol + 1],
                                                   rs[r0:r1, 0:n], ALU.mult, ALU.mult), reads=[pb[psi], rsB, vecB], writes=[dstB])
                                return
                            xb, xbB = xn_r2.next()
                            st["xb"] = (xb, xbB)
                            tr.op(dve, STT(xb[:, 0:n], psum[psi][:, 0:n], vcol(l, gcol), rs[:, 0:n], ALU.mult, ALU.mult),
                                  reads=[pb[psi], rsB, vecB], writes=[xbB])

                        def s2():
                            if not do_rope:
                                return
                            xb, xbB = st["xb"]
                            tr.group(pe, [MM(psum[6][:, 0:n], perm, xb[:, 0:n])], reads=[xbB, constB], writes=[pb[6]])
                            t1, t1B = t1_r.next()
                            t2, t2B = t2_r.next()
                            tr.op(dve, TT(t1[:, 0:n], xb[:, 0:n], COS[:, t0:t0 + n], ALU.mult), reads=[xbB, constB], writes=[t1B])
                            tr.op(dve, TT(t2[:, 0:n], psum[6][:, 0:n], SIN[:, t0:t0 + n], ALU.mult), reads=[pb[6], constB], writes=[t2B])
                            for (r0, r1, d_ap) in dst:
                                tr.op(dve, TT(d_ap, t1[r0:r1, 0:n], t2[r0:r1, 0:n], ALU.add), reads=[t1B, t2B], writes=[dstB])
                        return [s0, s1, s2]

                    def gate_item(t0, n, t5):
                        psi = cnt[0] % 4
                        cnt[0] += 1

                        def s0():
                            proj_fm(psi, w, 384, t0, n, wb, t5)
                            tr.op(act, ACTF(GT[:, t0:t0 + n], psum[psi][:, 0:n], AF.Silu), reads=[pb[psi]], writes=[GTB[t5]])
                        return [s0]

                    nv = 64 if kind == "C" else 128

                    def v_item(g5):
                        psi = cnt[0] % 4
                        cnt[0] += 1

                        def s0():
                            kts = list(range(4 * g5, min(4 * g5 + 4, 18)))
                            fns = []
                            for ii, kt in enumerate(kts):
                                for kc in range(8):
                                    fns.append(MM(psum[psi][:, ii * 128:ii * 128 + nv], hT[:, kc, kt * 128:(kt + 1) * 128],
                                                  w[:, kc, 256:256 + nv], start=(kc == 0), stop=(kc == 7)))
                            tr.group(pe, fns, reads=[wb, hTB[g5], hTB2[g5]], writes=[pb[psi]])
                            nk = len(kts)
                            pvv = psum[psi][:, 0:nk * 128].rearrange("p (k n) -> p k n", k=nk)
                            k0 = kts[0]
                            if kind == "B":
                                tr.op(act, ACTF(Vp[:, k0:k0 + nk, 0:64], pvv[:, :, 0:64], AF.Copy), reads=[pb[psi]], writes=[VB[g5]])
                                tr.op(dve, CP(Vp[:, k0:k0 + nk, 128:192], pvv[:, :, 64:128]), reads=[pb[psi]], writes=[VB[g5]])
                            elif kind == "C":
                                tr.op(act, ACTF(Vp[:, k0:k0 + nk, 0:64], pvv[:, :, 0:64], AF.Copy), reads=[pb[psi]], writes=[VB[g5]])
                            else:
                                tr.op(act, ACTF(Vp[:, k0:k0 + nk, 0:128], pvv[:, :, 0:128], AF.Copy), reads=[pb[psi]], writes=[VB[g5]])
                        return [s0]

                    qk_items = []
                    for t5, (t0, n) in enumerate(qtiles):
                        qk_items.append(qk_item(0, [(0, 64, QT[0:64, 0, t0:t0 + n]), (64, 128, QT[64:128, 1, t0:t0 + n])], QTB[t5], qg,
                                                t0, n, t5, rope and t5 < 4))
                    for t5, (t0, n) in enumerate(TILES):
                        qk_items.append(qk_item(128, [(0, 128, KT[:, t0:t0 + n])], KTB[t5], kg, t0, n, t5, rope and t5 < 4))
                    items = [gate_item(t0, n, t5) for t5, (t0, n) in enumerate(qtiles)] + qk_items + [v_item(g5) for g5 in range(5)]
                    KST = 3
                    for step in range(len(items) + KST - 1):
                        for k in range(KST):
                            i = step - k
                            if 0 <= i < len(items) and k < len(items[i]):
                                items[i][k]()

                    QW = 256

                    def both(ap512, qa, qb):
                        if qa == 0 and qb == QW:
                            return ap512
                        return ap512.rearrange("p (h q) -> p h q", h=2)[:, :, qa:qb]

                    qts = []
                    if kind == "B":
                        for r_lo in range(0, 32, 4):
                            ch = [(16, 0, QW, None), (17, 0, QW, None)]
                            if r_lo in (0, 28):
                                a0 = 0 if r_lo == 0 else 12
                                for a in range(a0, a0 + 4):
                                    ch.append((a, 0, QW, (10 + 7 - 2 * a + r_lo) * 64))
                            else:
                                for a in range(16):
                                    rs_ = max(r_lo, 2 * a - 3)
                                    re_ = min(r_lo + 3, 2 * a + 5)
                                    if rs_ <= re_:
                                        ch.append((a, (rs_ - r_lo) * 64, (re_ - r_lo + 1) * 64, (4 - 2 * a + rs_) * 64))
                            qts.append((r_lo * 64, ch))
                    else:
                        for q0 in range(0, S, QW):
                            qts.append((q0, [(kt, 0, QW, None) for kt in range(18)]))
                    if need_ctx:
                        qts.append((2048, [(16, 0, QW, None), (17, 0, QW, None)]))

                    NCH = 2
                    nacc = {"B": 2, "C": 1, "D": 2}[kind]
                    steps = []
                    pending_b = [None]
                    for ti, (q0, ch) in enumerate(qts):
                        t5 = min(q0 // 512, 4)
                        if nacc == 2:
                            accs = [(ti % 2) * 2, (ti % 2) * 2 + 1]
                        else:
                            accs = [ti % 2]
                        nch = len(ch)
                        for si_, c0 in enumerate(range(0, nch, NCH)):
                            subs = []
                            for ci in range(c0, min(c0 + NCH, nch)):
                                kt, qa, qb, bcol = ch[ci]
                                subs.append(dict(kt=kt, qa=qa, qb=qb, bcol=bcol, first=(ci == 0), last=(ci == nch - 1)))
                            steps.append(dict(q0=q0, t5=t5, accs=accs, subs=subs))
                            if si_ == 2 and pending_b[0] is not None:
                                steps.append(pending_b[0])
                                pending_b[0] = None
                        if pending_b[0] is not None:
                            steps.append(pending_b[0])
                            pending_b[0] = None
                        ea = dict(epi=True, part="a", q0=q0, t5=t5, accs=accs)
                        steps.append(ea)
                        if kind == "D":
                            pending_b[0] = dict(epi=True, part="b", ref=ea, q0=q0, t5=t5, accs=accs)
                    if pending_b[0] is not None:
                        steps.append(pending_b[0])
                        pending_b[0] = None

                    sring = [2, 3, 4, 5, 6, 7] if kind == "C" else [4, 5, 6, 7]
                    scount = [0]

                    def do_qk(st):
                        q0, t5 = st["q0"], st["t5"]
                        fns = []
                        rd = [QTB[t5], constB]
                        wr = []
                        for sub in st["subs"]:
                            si = sring[scount[0] % len(sring)]
                            scount[0] += 1
                            sub["si"] = si
                            rd.append(KTB[sub["kt"] // 4])
                            wr.append(pb[si])

                        for sub in st["subs"]:
                            si, kt, qa, qb, bcol = sub["si"], sub["kt"], sub["qa"], sub["qb"], sub["bcol"]
                            o3 = psum[si][:, 0:512].rearrange("p (h q) -> p h q", h=2)[:, :, qa:qb]
                            fns.append(MM(o3, KT[:, kt * 128:(kt + 1) * 128], QT[:, :, q0 + qa:q0 + qb], start=True, stop=(bcol is None)))
                            if bcol is not None:
                                fns.append(MM(o3, ident, tab[:, :, bcol:bcol + (qb - qa)], start=False, stop=True))
                                rd += tabB
                        tr.group(pe, fns, reads=rd, writes=wr)
                        for sub in st["subs"]:
                            P, PB = P_r.next()
                            sub["P"], sub["PB"] = P, PB
                            si, qa, qb = sub["si"], sub["qa"], sub["qb"]
                            tr.op(act, ACTF(both(P[:, 0:512], qa, qb), both(psum[si][:, 0:512], qa, qb), AF.Exp, scale=0.125),
                                  reads=[pb[si]], writes=[PB])

                    def do_pv(st):
                        accs = st["accs"]
                        fns = []
                        rd = [constB]
                        for sub in st["subs"]:
                            kt, qa, qb, P = sub["kt"], sub["qa"], sub["qb"], sub["P"]
                            rd += [sub["PB"], VB[kt // 4]]
                            f, la = sub["first"], sub["last"]
                            if kind == "B":
                                for hf in range(2):
                                    fns.append(MM(psum[accs[hf]][:, qa:qb], Vp[:, kt, hf * 128:(hf + 1) * 128],
                                                  P[:, hf * QW + qa:hf * QW + qb], start=f, stop=la))
                            elif kind == "C":
                                fns.append(MM(psum[accs[0]][:, 0:512], Vp[:, kt, 0:128], P[:, 0:512], start=f, stop=la))
                            else:
                                fns.append(MM(psum[accs[0]][:, 0:512], Vp[:, kt, 0:128], P[:, 0:512], start=f, stop=la))
                                fns.append(MM(psum[accs[1]][:, 0:512], ones, P[:, 0:512], start=f, stop=la))
                                tr.group(pe, fns, reads=[constB, sub["PB"], VB[kt // 4]], writes=[pb[a] for a in accs])
                                fns = []
                        if fns:
                            tr.group(pe, fns, reads=rd, writes=[pb[a] for a in accs])

                    def do_epi(st):
                        q0, accs, t5 = st["q0"], st["accs"], st["t5"]
                        qn = QW
                        ywr = [yTB[pr][t5]]
                        grd = [GTB[t5]]
                        ya, yaB = ya_r.next()
                        yb, ybB = yb_r.next()
                        if kind == "B":
                            for hf in range(2):
                                o = accs[hf]
                                tr.op(dve, RCP(ya[0:64, hf * QW:hf * QW + qn], psum[o][64:128, 0:qn]), reads=[pb[o]], writes=[yaB])
                                tr.op(dve, TT(yb[hf * 64:(hf + 1) * 64, 0:qn], psum[o][0:64, 0:qn], ya[0:64, hf * QW:hf * QW + qn], ALU.mult),
                                      reads=[pb[o], yaB], writes=[ybB])
                            tr.op(dve, TT(yT[:, pr, q0:q0 + qn], yb[:, 0:qn], GT[:, q0:q0 + qn], ALU.mult),
                                  reads=[ybB] + grd, writes=ywr)
                        elif kind == "C":
                            o = accs[0]
                            tr.op(dve, RCP(ya[0:64, 0:512], psum[o][64:128, 0:512]), reads=[pb[o]], writes=[yaB])
                            for hf in range(2):
                                tr.op(dve, TT(yb[hf * 64:(hf + 1) * 64, 0:qn], psum[o][0:64, hf * QW:hf * QW + qn],
                                              ya[0:64, hf * QW:hf * QW + qn], ALU.mult), reads=[pb[o], yaB], writes=[ybB])
                            tr.op(dve, TT(yT[:, pr, q0:q0 + qn], yb[:, 0:qn], GT[:, q0:q0 + qn], ALU.mult),
                                  reads=[ybB] + grd, writes=ywr)
                        else:
                            o, dn = accs
                            tr.op(dve, RCP(ya[:, 0:512], psum[dn][:, 0:512]), reads=[pb[dn]], writes=[yaB])
                            tr.op(dve, TT(ya[:, 0:512], psum[o][:, 0:512], ya[:, 0:512], ALU.mult), reads=[pb[o]], writes=[yaB])
                            tr.op(dve, STT(yb[:, 0:qn], ya[:, QW:QW + qn], neglam, ya[:, 0:qn], ALU.mult, ALU.add),
                                  reads=[yaB, smallB], writes=[ybB])
                            sq, sqB = sq_r.next()
                            tr.op(act, ACTF(sq[:, 0:qn], yb[:, 0:qn], AF.Square), reads=[ybB], writes=[sqB])
                            st["bufs"] = (ya, yaB, yb, ybB, sq, sqB)

                    def do_epi_b(st):
                        q0, accs, t5 = st["q0"], st["accs"], st["t5"]
                        qn = QW
                        o, dn = accs
                        ya, yaB, yb, ybB, sq, sqB = st["ref"]["bufs"]
                        tr.group(pe, [MM(psum[dn][:, 0:qn], ones, sq[:, 0:qn])], reads=[sqB, constB], writes=[pb[dn]])
                        tr.op(act, ACTF(ya[:, 0:qn], psum[dn][:, 0:qn], AF.Ln, scale=1.0 / 128, bias=epsc), reads=[pb[dn], smallB], writes=[yaB])
                        tr.op(act, ACTF(ya[:, 0:qn], ya[:, 0:qn], AF.Exp, scale=-0.5), reads=[], writes=[yaB])
                        tr.op(dve, TT(yb[:, 0:qn], yb[:, 0:qn], ya[:, 0:qn], ALU.mult), reads=[yaB], writes=[ybB])
                        tr.op(dve, STT(yT[:, pr, q0:q0 + qn], yb[:, 0:qn], gsub, GT[:, q0:q0 + qn], ALU.mult, ALU.mult),
                              reads=[ybB, smallB, GTB[t5]], writes=[yTB[pr][t5]])

                    LOOK = 2 if kind == "C" else 1
                    qk_list = [s_ for s_ in steps if "epi" not in s_]
                    qi = 0
                    done = 0
                    for s_ in steps:
                        if "epi" in s_:
                            if s_["part"] == "a":
                                do_epi(s_)
                            else:
                                do_epi_b(s_)
                            continue
                        while qi < len(qk_list) and qi <= done + LOOK:
                            do_qk(qk_list[qi])
                            qi += 1
                        do_pv(s_)
                        done += 1

            attn_branch("B")
            dump(1)
            merge_branch(1)
            attn_branch("C")
            dump(2)
            merge_branch(2)
            attn_branch("D")
            dump(3)
            merge_branch(3)

            ar.reset()
            if debug and l == 0:
                ddm = tr.dsem("dbgm")
                tr.dma(pool, ddm, [DMA(ddbgm.rearrange("c p t -> p c t"), mT[:, :, :])],
                       reads=[mTB[c][t] for c in range(8) for t in range(5)], writes=[outB])
            screp = ar.bf(8 * 128).rearrange("p (k n) -> p k n", k=8)
            sccrep = ar.bf(8 * 128).rearrange("p (k n) -> p k n", k=8)
            repB = Buf("rep")
            for kc in range(8):
                tr.op(dve, CP(screp[:, kc, :], s2[:, kc, 0:1].to_broadcast([128, 128])), reads=[s2B], writes=[repB])
                tr.op(dve, CP(sccrep[:, kc, :], s2[:, kc, 1:2].to_broadcast([128, 128])), reads=[s2B], writes=[repB])
            tr.dma(pool, bgd, [DMA(bgrow[:, :], dbg_rows[:, l * DM:(l + 1) * DM])], writes=[bgB])
            gx = ar.f32(1024)
            gc = ar.f32(1024)
            gB = Buf("gates")
            for pi in range(2):
                w_ap, wb = wget(f"adag{l}_{pi}")
                w = w3(w_ap, 8, 512)
                for which, (rep, dst) in enumerate(((screp, gx), (sccrep, gc))):
                    if which == 1 and not need_ctx:
                        continue
                    psi = pi * 2 + which
                    fns = [MM(psum[psi][:, :], rep[:, kc, :], w[:, kc, :], start=(kc == 0), stop=False) for kc in range(8)]
                    fns.append(MM(psum[psi][:, :], ones[0:1, :], bgrow[0:1, pi * 512:(pi + 1) * 512], start=False, stop=True))
                    tr.group(pe, fns, reads=[wb, repB, constB, bgB], writes=[pb[psi]])
                    tr.op(act, ACTF(dst[:, pi * 512:(pi + 1) * 512], psum[psi][:, :], AF.Copy), reads=[pb[psi]], writes=[gB])
            if debug and l == 0:
                ddg = tr.dsem("dbgg")
                tr.dma(sp, ddg, [DMA(ddbgg[:, 0:1024], gx), DMA(ddbgg[:, 1024:2048], gc)], reads=[gB], writes=[outB])
            xo_r = Ring([ar.f32(512) for _ in range(3)], "xo")
            res_r = Ring([ar.f32(512) for _ in range(2)], "res")
            tm_r = Ring([ar.f32(512) for _ in range(2)], "otm")
            ntile = 18 if need_ctx else 16
            its = [(ph, i) for ph in range(2) for i in range(ntile)]
            loaded = {}

            def issue_load(k):
                if k >= len(its):
                    return
                ph, i = its[k]
                xo, xoB = xo_r.next()
                if l == 0:
                    src = dx[i * 128:(i + 1) * 128, ph * 512:(ph + 1) * 512] if i < 16 else \
                        dctx[(i - 16) * 128:(i - 15) * 128, ph * 512:(ph + 1) * 512]
                    rd = []
                else:
                    src = dx1[i * 128:(i + 1) * 128, ph * 512:(ph + 1) * 512]
                    rd = [x1B[i]]
                tr.dma(sp, xld[k % 3], [DMA(xo, src)], reads=rd, writes=[xoB])
                loaded[k] = (xo, xoB)

            issue_load(0)
            issue_load(1)
            w = wb = None
            for cnt, (ph, i) in enumerate(its):
                if i == 0:
                    w_ap, wb = wget(f"wo{l}_{ph}")
                    w = w3(w_ap, 8, 512)
                psi = 4 + cnt % 4
                t5 = min(i // 4, 4)
                tr.group(pe, [MM(psum[psi][:, :], mT[:, kc, i * 128:(i + 1) * 128], w[:, kc, :], start=(kc == 0), stop=(kc == 7))
                              for kc in range(8)], reads=[wb] + [mTB[kc][t5] for kc in range(8)], writes=[pb[psi]])
                xo, xoB = loaded.pop(cnt)
                gate = gx if i < 16 else gc
                tm, tmB = tm_r.next()
                rs_, rsB_ = res_r.next()
                tr.op(dve, TT(tm, psum[psi][:, :], gate[:, ph * 512:(ph + 1) * 512], ALU.mult), reads=[pb[psi], gB], writes=[tmB])
                tr.op(dve, TT(rs_, tm, xo, ALU.add), reads=[tmB, xoB], writes=[rsB_])
                if l == NL - 1:
                    tr.dma(sp, std[cnt % 2], [DMA(dout[i * 128:(i + 1) * 128, ph * 512:(ph + 1) * 512], rs_)],
                           reads=[rsB_], writes=[Buf("o")])
                else:
                    tr.dma(sp, std[cnt % 2], [DMA(dx1[i * 128:(i + 1) * 128, ph * 512:(ph + 1) * 512], rs_)],
                           reads=[rsB_], writes=[x1B[i]])
                issue_load(cnt + 2)

        for l in range(NL):
            run_layer(l, l < NL - 1)
        tr.barrier()

        block = es.enter_context(nc.Block())

        @block.tensor
        def _(h):
            Tracer.replay(pe, h)

        @block.scalar
        def _(h):
            Tracer.replay(act, h)

        @block.vector
        def _(h):
            Tracer.replay(dve, h)

        @block.gpsimd
        def _(h):
            Tracer.replay(pool, h)

        @block.sync
        def _(h):
            Tracer.replay(sp, h)
    return nc


_CACHE = {}


def _prep_inputs(inputs):
    f = lambda a: np.ascontiguousarray(np.asarray(a, dtype=np.float32))
    x, c, ctx, c_ctx = f(inputs["x"]), f(inputs["c"]), f(inputs["ctx"]), f(inputs["c_ctx"])
    cf, cb = _host_consts()
    w_br = np.ascontiguousarray(np.stack([f(inputs["w_br_a"]), f(inputs["w_br_b"]), f(inputs["w_br_c"]), f(inputs["w_br_d"])], axis=1))
    vecs = np.zeros((128, NL * NVL), np.float32)

    def fm(v, n):
        return np.ascontiguousarray(v.reshape(n, 128).T)

    for l in range(NL):
        o = l * NVL
        vecs[:, o + V_G:o + V_G + 8] = fm(f(inputs["norm_g"])[l], 8)
        b_ada = f(inputs["b_ada"])[l]
        vecs[:, o + V_BSH:o + V_BSH + 8] = fm(b_ada[0:1024], 8)
        vecs[:, o + V_BSC:o + V_BSC + 8] = fm(b_ada[1024:2048], 8)
        vecs[:, o + V_BM:o + V_BM + 32] = fm(f(inputs["b_merge"])[l], 32)
        vecs[:, o + V_CB:o + V_CB + 4] = fm(f(inputs["conv_b"])[l], 4)
        vecs[:, o + V_LG:o + V_LG + 4] = fm(f(inputs["conv_ln_g"])[l], 4)
        vecs[:, o + V_LB:o + V_LB + 4] = fm(f(inputs["conv_ln_b"])[l], 4)
        cw = f(inputs["conv_w"])[l]
        vecs[:, o + V_CW:o + V_CW + 124] = cw.T.reshape(4, 128, 31).transpose(1, 0, 2).reshape(128, 124)
        for nm, off in (("na_qn_g", V_NAQ), ("na_kn_g", V_NAK), ("gqa_qn_g", V_GQ), ("gqa_kn_g", V_GK),
                        ("diff_qn_g", V_DQ), ("diff_kn_g", V_DK)):
            vecs[:, o + off] = np.tile(f(inputs[nm])[l], 2)
        vecs[:, o + V_SUB] = f(inputs["diff_subln_g"])[l]
        for k, nm in enumerate(("lam_q1", "lam_k1", "lam_q2", "lam_k2")):
            vecs[:, o + V_L + 64 * k:o + V_L + 64 * (k + 1)] = f(inputs[nm])[l][None, :]
    bgrow = np.ascontiguousarray(f(inputs["b_ada"])[:, 2048:3072].reshape(1, NL * DM))
    shared = {"w_ada": f(inputs["w_ada"]), "w_in": f(inputs["w_in"]), "w_br": w_br, "w_out": f(inputs["w_out"]),
              "vecs": vecs, "bgrow": bgrow, "rpb": f(inputs["na_rpb"]), "cf": cf, "cb": cb}
    maps = []
    for b in range(8):
        cT = np.concatenate([fm(c[b], 8), fm(c_ctx, 8)], axis=1)
        m = dict(shared)
        m.update({"x": x[b], "ctx": ctx[b], "cT": np.ascontiguousarray(cT)})
        maps.append(m)
    return maps


def kernel(**inputs):
    if "nc" not in _CACHE:
        _CACHE["nc"] = build_program(False)
    maps = _prep_inputs(inputs)
    res = run_bass_kernel_spmd(_CACHE["nc"], maps, core_ids=list(range(8)))
    return np.stack([np.asarray(r["out"], dtype=np.float32) for r in res.results], axis=0)
```

```python
import math
import numpy as np
from contextlib import ExitStack
import concourse.bass as bass
import concourse.mybir as mybir
from concourse.bass_utils import run_bass_kernel_spmd

F32 = mybir.dt.float32
BF16 = mybir.dt.bfloat16
AF = mybir.ActivationFunctionType
ALU = mybir.AluOpType

DM = 1024
S = 2048
LC = 256
T = S + LC
NL = 2
INW = 11008
EPS = 1e-6
NEGM = -30000.0
NVL = 455
NJ = 26
TILES = [(0, 512), (512, 512), (1024, 512), (1536, 512), (2048, 256)]

V_G, V_BSH, V_BSC, V_BM, V_CB, V_LG, V_LB, V_CW = 0, 8, 16, 24, 56, 60, 64, 68
V_NAQ, V_NAK, V_GQ, V_GK, V_DQ, V_DK, V_SUB = 192, 193, 194, 195, 196, 197, 198
V_L = 199


class Buf:
    __slots__ = ("w", "r", "name")

    def __init__(self, name=""):
        self.w = None
        self.r = {}
        self.name = name


class DSem:
    def __init__(self, h):
        self.h = h
        self.count = 0


class Eng:
    def __init__(self, tr, name):
        self.tr = tr
        self.name = name
        self.items = []
        self.seen = {}
        self.sems = []
        self.count = 0
        self.newsem()

    def newsem(self):
        h = self.tr.es.enter_context(self.tr.nc.semaphore(f"s_{self.name}{len(self.sems)}"))
        self.sems.append(h)
        self.count = 0


class Tracer:
    def __init__(self, nc, es):
        self.nc = nc
        self.es = es
        self.pe = Eng(self, "pe")
        self.act = Eng(self, "act")
        self.dve = Eng(self, "dve")
        self.pool = Eng(self, "pool")
        self.sp = Eng(self, "sp")
        self.engs = [self.pe, self.act, self.dve, self.pool, self.sp]
        self.dsems = []

    def dsem(self, name):
        d = DSem(self.es.enter_context(self.nc.semaphore("d_" + name)))
        self.dsems.append(d)
        return d

    def _deps(self, eng, reads, writes):
        need = {}

        def add(tok):
            if tok is None:
                return
            sem, val, src = tok
            if src is eng and eng.name == "pe":
                return
            k = id(sem)
            if k not in need or need[k][1] < val:
                need[k] = (sem, val)

        for b in reads:
            add(b.w)
        for b in writes:
            add(b.w)
            for t in b.r.values():
                add(t)
        for k, (sem, val) in need.items():
            if eng.seen.get(k, 0) < val:
                eng.items.append(("wait", sem, val))
                eng.seen[k] = val

    @staticmethod
    def _commit(tok, reads, writes):
        for b in writes:
            b.w = tok
            b.r = {}
        k = id(tok[0])
        for b in reads:
            b.r[k] = tok

    def op(self, eng, fn, reads=(), writes=()):
        self._deps(eng, reads, writes)
        if eng.count >= 16000:
            eng.newsem()
        eng.count += 1
        tok = (eng.sems[-1], eng.count, eng)
        eng.items.append(("ins", fn, eng.sems[-1], 1))
        self._commit(tok, reads, writes)
        return tok

    def group(self, eng, fns, reads=(), writes=()):
        self._deps(eng, reads, writes)
        if eng.count >= 16000:
            eng.newsem()
        for f in fns[:-1]:
            eng.items.append(("ins", f, None, 0))
        eng.count += 1
        tok = (eng.sems[-1], eng.count, eng)
        eng.items.append(("ins", fns[-1], eng.sems[-1], 1))
        self._commit(tok, reads, writes)
        return tok

    def dma(self, eng, dsem, fns, reads=(), writes=()):
        self._deps(eng, reads, writes)
        for f in fns:
            eng.items.append(("ins", f, dsem.h, 16))
            dsem.count += 16
        tok = (dsem.h, dsem.count, None)
        self._commit(tok, reads, writes)
        return tok

    def barrier(self):
        toks = [(e.sems[-1], e.count, e) for e in self.engs if e.count > 0]
        toks += [(d.h, d.count, None) for d in self.dsems if d.count > 0]
        for e in self.engs:
            for sem, val, src in toks:
                if src is e:
                    continue
                k = id(sem)
                if e.seen.get(k, 0) < val:
                    e.items.append(("wait", sem, val))
                    e.seen[k] = val

    @staticmethod
    def replay(eng, h):
        for it in eng.items:
            if it[0] == "wait":
                h.wait_ge(it[1], it[2])
            else:
                ins = it[1](h)
                if it[2] is not None:
                    ins.then_inc(it[2], it[3])


def MM(out, lhsT, rhs, start=True, stop=True):
    return lambda h: h.matmul(out, lhsT=lhsT, rhs=rhs, start=start, stop=stop)


def TRN(out, in_, ident):
    return lambda h: h.transpose(out, in_, ident)


def ACTF(out, in_, func, **kw):
    return lambda h: h.activation(out=out, in_=in_, func=func, **kw)


def TT(out, in0, in1, op):
    return lambda h: h.tensor_tensor(out=out, in0=in0, in1=in1, op=op)


def TS(out, in0, s1, s2=None, op0=ALU.mult, op1=None):
    if op1 is None:
        return lambda h: h.tensor_scalar(out=out, in0=in0, scalar1=s1, scalar2=None, op0=op0)
    return lambda h: h.tensor_scalar(out=out, in0=in0, scalar1=s1, scalar2=s2, op0=op0, op1=op1)


def STT(out, in0, scalar, in1, op0, op1):
    return lambda h: h.scalar_tensor_tensor(out=out, in0=in0, scalar=scalar, in1=in1, op0=op0, op1=op1)


def CP(out, in_):
    return lambda h: h.tensor_copy(out=out, in_=in_)


def RCP(out, in_):
    return lambda h: h.reciprocal(out=out, in_=in_)


def MSET(ap, v):
    return lambda h: h.memset(ap, v)


def DMA(out, in_):
    return lambda h: h.dma_start(out=out, in_=in_)


class Ring:
    def __init__(self, aps, name):
        self.aps = aps
        self.bufs = [Buf(f"{name}{i}") for i in range(len(aps))]
        self.i = 0

    def next(self):
        k = self.i % len(self.aps)
        self.i += 1
        return self.aps[k], self.bufs[k]


def _host_consts():
    identf = np.eye(128, dtype=np.float32)
    p = np.arange(128)
    hd = p % 64
    half = hd // 32
    fi = (hd % 32) % 16
    freq = (10000.0 ** (-(2.0 * fi) / 32.0)).astype(np.float32)
    t = np.arange(S)
    rows = (t // 64).astype(np.float32)
    cols = (t % 64).astype(np.float32)
    pos = np.where(half[:, None] == 0, rows[None, :], cols[None, :]).astype(np.float32)
    ang = (pos * freq[:, None]).astype(np.float32)
    cos = np.cos(ang).astype(np.float32)
    sgn = np.where((hd % 32) < 16, -1.0, 1.0).astype(np.float32)
    sin = (np.sin(ang).astype(np.float32) * sgn[:, None]).astype(np.float32)
    selR = np.zeros((128, 64), np.float32)

    def delta(hf, jj):
        return (4 - jj) + hf if jj < 10 else (7 - (jj - 10)) + hf

    for hf in range(2):
        for jj in range(NJ):
            dr = delta(hf, jj) + 7
            if 0 <= dr <= 14:
                selR[dr, hf * NJ + jj] = 8.0
    cf = np.concatenate([identf, cos, sin, selR], axis=1)

    ident = identf
    blk = ((p[:, None] // 64) == (p[None, :] // 64)).astype(np.float32)
    partner = np.where((p % 32) < 16, p + 16, p - 16)
    perm = np.zeros((128, 128), np.float32)
    perm[partner, p] = 1.0
    band = np.zeros((128, 128), np.float32)
    for c in range(31):
        band[c, c + 48] = 1.0
    mask = np.zeros((128, 2, NJ, 64), np.float32)
    qc = np.arange(64)
    c0 = np.clip(qc - 8, 0, 48)
    for kc in range(64):
        colok = (kc >= c0) & (kc < c0 + 16)
        for hf in range(2):
            for jj in range(NJ):
                d = delta(hf, jj)
                ok = (-4 <= d <= 3) if jj < 10 else (-7 <= d <= 7)
                mask[kc, hf, jj, :] = np.where(colok & ok, 0.0, NEGM)
    cb = np.concatenate([ident, blk, perm, band, np.ones((128, 128), np.float32),
                         mask.reshape(128, -1)], axis=1)
    return np.ascontiguousarray(cf), np.ascontiguousarray(cb)


CF_ID, CF_COS, CF_SIN, CF_SEL, CF_N = 0, 128, 128 + 2048, 128 + 4096, 128 + 4096 + 64
CB_ID, CB_BLK, CB_PERM, CB_BAND, CB_ONES, CB_MASK, CB_N = 0, 128, 256, 384, 512, 640, 640 + 2 * NJ * 64


def build_program(debug=False):
    nc = bass.Bass("TRN2", target_bir_lowering=False)
    dx = nc.dram_tensor("x", [S, DM], F32, kind="ExternalInput").ap()
    dctx = nc.dram_tensor("ctx", [LC, DM], F32, kind="ExternalInput").ap()
    dcT = nc.dram_tensor("cT", [128, 16], F32, kind="ExternalInput").ap()
    dwada = nc.dram_tensor("w_ada", [NL, DM, 3 * DM], F32, kind="ExternalInput").ap()
    dwin = nc.dram_tensor("w_in", [NL, DM, INW], F32, kind="ExternalInput").ap()
    dwbr = nc.dram_tensor("w_br", [NL, 4, 512, DM], F32, kind="ExternalInput").ap()
    dwout = nc.dram_tensor("w_out", [NL, DM, DM], F32, kind="ExternalInput").ap()
    dvecs = nc.dram_tensor("vecs", [128, NL * NVL], F32, kind="ExternalInput").ap()
    dbg_rows = nc.dram_tensor("bgrow", [1, NL * DM], F32, kind="ExternalInput").ap()
    drpb = nc.dram_tensor("rpb", [NL, 8, 15, 31], F32, kind="ExternalInput").ap()
    dcf = nc.dram_tensor("cf", [128, CF_N], F32, kind="ExternalInput").ap()
    dcb = nc.dram_tensor("cb", [128, CB_N], F32, kind="ExternalInput").ap()
    dout = nc.dram_tensor("out", [S, DM], F32, kind="ExternalOutput").ap()
    dx1 = nc.dram_tensor("x1s", [T, DM], F32, kind="ExternalOutput" if debug else "Internal").ap()
    ddbg = None
    if debug:
        ddbg = nc.dram_tensor("dbg", [8, 4, 128, T], F32, kind="ExternalOutput").ap()
        ddbgm = nc.dram_tensor("dbgm", [8, 128, T], F32, kind="ExternalOutput").ap()
        ddbgg = nc.dram_tensor("dbgg", [128, 2048], F32, kind="ExternalOutput").ap()

    es = ExitStack()
    with es:
        tr = Tracer(nc, es)
        pe, act, dve, pool, sp = tr.pe, tr.act, tr.dve, tr.pool, tr.sp

        def sb(name, shape, dt):
            return es.enter_context(nc.sbuf_tensor("sb_" + name, shape, dt))

        hT = sb("hT", [128, 8, T], BF16)
        mT = sb("mT", [128, 8, T], BF16)
        yT = sb("yT", [128, 4, T], BF16)
        cfs = sb("cfs", [128, CF_N], F32)
        cbs = sb("cbs", [128, CB_N], BF16)
        vecs = sb("vecs", [128, NL * NVL], F32)
        bgrow = sb("bgrow", [1, DM], BF16)
        bgB = Buf("bgrow")
        bgd = tr.dsem("bgrow")
        modv = sb("modv", [128, 4, 8], F32)
        s2 = sb("s2", [128, 8, 2], BF16)
        small = sb("small", [128, 64], F32)
        wring_t = [sb(f"wr{i}", [128, 4096], BF16) for i in range(3)]
        ARENA_N = 31780
        arena = sb("arena", [128, ARENA_N], BF16)
        psum = [es.enter_context(nc.psum_tensor(f"ps{i}", [128, 512], F32)) for i in range(8)]
        pb = [Buf(f"ps{i}") for i in range(8)]

        hTB = [Buf(f"hT{t}") for t in range(5)]
        hTB2 = [Buf(f"hTa{t}") for t in range(5)]
        mTB = [[Buf(f"mT{c}_{t}") for t in range(5)] for c in range(8)]
        yTB = [[Buf(f"yT{c}_{t}") for t in range(5)] for c in range(4)]
        constB = Buf("const")
        vecB = Buf("vecs")
        modB = Buf("modv")
        s2B = Buf("s2")
        smallB = Buf("small")
        x1B = [Buf(f"x1_{i}") for i in range(18)]
        outB = Buf("out")

        identf = cfs[:, CF_ID:CF_ID + 128]
        COS = cfs[:, CF_COS:CF_COS + S]
        SIN = cfs[:, CF_SIN:CF_SIN + S]
        selR = cfs[:, CF_SEL:CF_SEL + 64]
        ident = cbs[:, CB_ID:CB_ID + 128]
        blk = cbs[:, CB_BLK:CB_BLK + 128]
        perm = cbs[:, CB_PERM:CB_PERM + 128]
        band = cbs[:, CB_BAND:CB_BAND + 128]
        ones = cbs[:, CB_ONES:CB_ONES + 128]
        maskT = cbs[:, CB_MASK:CB_MASK + 2 * NJ * 64].rearrange("p (t j q) -> p t j q", t=2, j=NJ)

        class Arena:
            def __init__(self):
                self.off = 0

            def reset(self):
                tr.barrier()
                self.off = 0

            def bf(self, n):
                ap = arena[:, self.off:self.off + n]
                self.off += n
                assert self.off <= ARENA_N, self.off
                return ap

            def f32(self, n):
                ap = arena[:, self.off:self.off + 2 * n].bitcast(F32)
                self.off += 2 * n
                assert self.off <= ARENA_N, self.off
                return ap

        ar = Arena()

        wsl = [Buf(f"w{i}") for i in range(3)]
        wds = [tr.dsem(f"w{i}") for i in range(3)]
        pieces = []

        def w3(slot, kc, n):
            return slot[:, 0:kc * n].rearrange("p (k n) -> p k n", k=kc)

        def piece_cols(tag, src2d, cols):
            specs = []
            for (do, c0, n) in cols:
                specs.append((lambda sl, do=do, n=n: w3(sl, 8, 512)[:, :, do:do + n],
                              src2d[:, c0:c0 + n].rearrange("(k p) n -> p k n", p=128)))
            pieces.append((tag, specs))

        def layer_pieces(l):
            win = dwin[l]
            for pi in range(4):
                piece_cols(f"ada{l}_{pi}", dwada[l], [(0, pi * 512, 512)])
            for j in range(4):
                piece_cols(f"A{l}_{j}", win, [(0, j * 128, 128), (128, 512 + j * 128, 128), (256, 1024 + j * 128, 128)])
            merge_pieces(l, 0)
            for hp in range(4):
                piece_cols(f"B{l}_{hp}", win, [(0, 1536 + hp * 128, 128), (128, 2048 + hp * 128, 128),
                                               (256, 2560 + hp * 128, 128), (384, 3072 + hp * 128, 128)])
            merge_pieces(l, 1)
            for cp in range(4):
                n = cp // 2
                piece_cols(f"C{l}_{cp}", win, [(0, 3584 + cp * 128, 128), (128, 4096 + n * 64, 64), (192, 4096 + n * 64, 64),
                                               (256, 4224 + n * 64, 64), (384, 4352 + cp * 128, 128)])
            merge_pieces(l, 2)
            for hd in range(4):
                piece_cols(f"D{l}_{hd}", win, [(0, 4864 + hd * 128, 128), (128, 5376 + hd * 128, 128),
                                               (256, 5888 + hd * 128, 128), (384, 6400 + hd * 128, 128)])
            merge_pieces(l, 3)
            for pi in range(2):
                piece_cols(f"adag{l}_{pi}", dwada[l], [(0, 2048 + pi * 512, 512)])
            for ph in range(2):
                piece_cols(f"wo{l}_{ph}", dwout[l], [(0, ph * 512, 512)])

        def merge_pieces(l, i):
            for hf in range(2):
                pieces.append((f"br{l}_{i}_{hf}", [(lambda sl: w3(sl, 4, 1024),
                                                    dwbr[l, i].rearrange("(k p) n -> p k n", p=128))]))
                piece_cols(f"lg{l}_{i}_{hf}", dwin[l], [(0, 6912 + i * 1024 + hf * 512, 512)])

        for l in range(NL):
            layer_pieces(l)
        wstate = {"issued": 0, "next": 0}

        def w_issue(upto):
            while wstate["issued"] < min(upto, len(pieces)):
                i = wstate["issued"]
                tag, specs = pieces[i]
                sl = wring_t[i % 3]
                tr.dma(pool, wds[i % 3], [DMA(f(sl), src) for (f, src) in specs], writes=[wsl[i % 3]])
                wstate["issued"] += 1

        def wget(tag):
            i = wstate["next"]
            assert pieces[i][0] == tag, (pieces[i][0], tag)
            w_issue(i + 2)
            wstate["next"] += 1
            return wring_t[i % 3], wsl[i % 3]

        d_init = tr.dsem("init")
        tr.dma(sp, d_init, [DMA(cfs[:, :], dcf), DMA(vecs[:, :], dvecs), DMA(small[:, 0:16], dcT)],
               writes=[constB, vecB, smallB])
        d_init2 = tr.dsem("init2")
        tr.dma(pool, d_init2, [DMA(cbs[:, :], dcb)], writes=[constB])
        w_issue(2)

        def vcol(l, off, n=1):
            return vecs[:, l * NVL + off: l * NVL + off + n]

        xld = [tr.dsem(f"xld{i}") for i in range(3)]
        std = [tr.dsem(f"st{i}") for i in range(2)]

        def run_layer(l, need_ctx):
            lam_init = 0.8 - 0.6 * math.exp(-0.3 * l)
            qtiles = TILES if need_ctx else TILES[:4]

            ar.reset()
            tr.op(act, ACTF(s2[:, :, 0], small[:, 0:8], AF.Silu), reads=[smallB], writes=[s2B])
            tr.op(act, ACTF(s2[:, :, 1], small[:, 8:16], AF.Silu), reads=[smallB], writes=[s2B])
            pm = psum[7]
            for pi in range(4):
                wsl_ap, wb = wget(f"ada{l}_{pi}")
                w = w3(wsl_ap, 8, 512)
                for fc in range(4):
                    g = pi * 4 + fc
                    tr.group(pe, [MM(pm[:, g * 2:g * 2 + 2], w[:, kc, fc * 128:(fc + 1) * 128], s2[:, kc, :],
                                     start=(kc == 0), stop=(kc == 7)) for kc in range(8)],
                             reads=[wb, s2B], writes=[pb[7]])
            pmv = pm[:, 0:32].rearrange("p (f w) -> p f w", w=2)
            tmp8 = small[:, 16:24]
            for which in range(2):
                tr.op(dve, TT(modv[:, 2 * which + 1, :], pmv[:, 0:8, which], vcol(l, V_BSH, 8), ALU.add),
                      reads=[pb[7], vecB], writes=[modB])
                tr.op(dve, TT(tmp8, pmv[:, 8:16, which], vcol(l, V_BSC, 8), ALU.add),
                      reads=[pb[7], vecB], writes=[smallB])
                tr.op(dve, STT(modv[:, 2 * which, :], tmp8, 1.0, vcol(l, V_G, 8), ALU.add, ALU.mult),
                      reads=[smallB, vecB], writes=[modB])
            lt = small[:, 24:28]
            prod = ar.f32(64)
            prodB = Buf("prod")
            for k in range(2):
                tr.op(dve, TT(prod, vcol(l, V_L + 128 * k, 64), vcol(l, V_L + 128 * k + 64, 64), ALU.mult),
                      reads=[vecB], writes=[prodB])
                tr.op(dve, MSET(lt[:, k:k + 1], 0.0), writes=[smallB])
                tr.op(act, ACTF(prod, prod, AF.Identity, accum_out=lt[:, k:k + 1]), reads=[prodB, smallB], writes=[prodB, smallB])
                tr.op(act, ACTF(lt[:, k:k + 1], lt[:, k:k + 1], AF.Exp), reads=[smallB], writes=[smallB])
            neglam = small[:, 28:29]
            gsub = small[:, 29:30]
            epsc = small[:, 30:31]
            tr.op(dve, MSET(epsc, EPS), writes=[smallB])
            tr.op(dve, TT(lt[:, 2:3], lt[:, 0:1], lt[:, 1:2], ALU.subtract), reads=[smallB], writes=[smallB])
            tr.op(dve, TS(neglam, lt[:, 2:3], lam_init, -1.0, ALU.add, ALU.mult), reads=[smallB], writes=[smallB])
            tr.op(dve, TS(gsub, vcol(l, V_SUB), 1.0 - lam_init), reads=[vecB], writes=[smallB])

            ar.reset()
            xt_r = Ring([ar.f32(1024) for _ in range(3)], "xt")
            xn_r = Ring([ar.f32(1024) for _ in range(3)], "xn")
            junk = ar.bf(1024)
            junkB = Buf("junk")
            st_r = Ring([small[:, 32 + 4 * i: 36 + 4 * i] for i in range(3)], "st")
            def p1_a(i):
                xt, xtB = xt_r.next()
                xn, xnB = xn_r.next()
                stt_, stB = st_r.next()
                if l == 0:
                    src = dx[i * 128:(i + 1) * 128, :] if i < 16 else dctx[(i - 16) * 128:(i - 15) * 128, :]
                    rd = []
                else:
                    src = dx1[i * 128:(i + 1) * 128, :]
                    rd = [x1B[i]]
                tr.dma(sp, xld[i % 3], [DMA(xt, src)], reads=rd, writes=[xtB])
                tr.op(dve, MSET(stt_[:, 0:1], 0.0), writes=[stB])
                tr.op(act, ACTF(junk, xt, AF.Square, accum_out=stt_[:, 0:1]), reads=[xtB, stB], writes=[junkB, stB])
                tr.op(act, ACTF(stt_[:, 1:2], stt_[:, 0:1], AF.Sqrt, scale=1.0 / DM, bias=EPS), reads=[stB], writes=[stB])
                tr.op(dve, RCP(stt_[:, 2:3], stt_[:, 1:2]), reads=[stB], writes=[stB])
                tr.op(dve, TS(xn, xt, stt_[:, 2:3]), reads=[xtB, stB], writes=[xnB])
                return xn, xnB

            def p1_b(i, xn, xnB):
                which = 0 if i < 16 else 1
                for hb in range(2):
                    bk = (2 * i + hb) % 8
                    tr.group(pe, [TRN(psum[bk][:, k4 * 128:(k4 + 1) * 128], xn[:, (hb * 4 + k4) * 128:(hb * 4 + k4 + 1) * 128], identf)
                                  for k4 in range(4)], reads=[xnB, constB], writes=[pb[bk]])
                    for k4 in range(4):
                        kc = hb * 4 + k4
                        dst = hT[:, kc, i * 128:(i + 1) * 128]
                        srcp = psum[bk][:, k4 * 128:(k4 + 1) * 128]
                        A = modv[:, 2 * which, kc:kc + 1]
                        Bc = modv[:, 2 * which + 1, kc:kc + 1]
                        if hb == 0:
                            tr.op(dve, TS(dst, srcp, A, Bc, ALU.mult, ALU.add), reads=[pb[bk], modB], writes=[hTB[i // 4]])
                        else:
                            tr.op(act, ACTF(dst, srcp, AF.Identity, scale=A, bias=Bc), reads=[pb[bk], modB], writes=[hTB2[i // 4]])

            prev = None
            for i in range(18):
                cur = p1_a(i)
                if prev is not None:
                    p1_b(i - 1, *prev)
                prev = cur
            p1_b(17, *prev)

            def proj_fm(ps_i, w, col0, t0, n, wb, t5):
                tr.group(pe, [MM(psum[ps_i][:, 0:n], w[:, kc, col0:col0 + 128], hT[:, kc, t0:t0 + n],
                                 start=(kc == 0), stop=(kc == 7)) for kc in range(8)],
                         reads=[wb, hTB[t5], hTB2[t5]], writes=[pb[ps_i]])

            def merge_branch(i):
                ar.reset()
                g_r = Ring([ar.f32(512) for _ in range(4)], "G")
                t_r = Ring([ar.f32(512) for _ in range(4)], "mt")
                cnt = 0
                for hf in range(2):
                    wbr_ap, wbrB = wget(f"br{l}_{i}_{hf}")
                    wbr = w3(wbr_ap, 4, 1024)
                    wl_ap, wlB = wget(f"lg{l}_{i}_{hf}")
                    wl = w3(wl_ap, 8, 512)
                    for fcl in range(4):
                        fc = hf * 4 + fcl
                        for t5, (t0, n) in enumerate(qtiles):
                            pz = (2 * cnt) % 8
                            pl = (2 * cnt + 1) % 8
                            cnt += 1
                            fz = [MM(psum[pz][:, 0:n], wbr[:, kc, fc * 128:(fc + 1) * 128], yT[:, kc, t0:t0 + n],
                                     start=(kc == 0), stop=(kc == 3)) for kc in range(4)]
                            fl = [MM(psum[pl][:, 0:n], wl[:, kc, fcl * 128:(fcl + 1) * 128], hT[:, kc, t0:t0 + n],
                                     start=(kc == 0), stop=(kc == 7)) for kc in range(8)]
                            tr.group(pe, fz + fl, reads=[wbrB, wlB, hTB[t5], hTB2[t5]] + [yTB[kc][t5] for kc in range(4)],
                                     writes=[pb[pz], pb[pl]])
                            G, GB = g_r.next()
                            tr.op(act, ACTF(G[:, 0:n], psum[pl][:, 0:n], AF.Sigmoid, bias=vcol(l, V_BM + i * 8 + fc)),
                                  reads=[pb[pl], vecB], writes=[GB])
                            if i == 0:
                                tr.op(dve, TT(mT[:, fc, t0:t0 + n], psum[pz][:, 0:n], G[:, 0:n], ALU.mult),
                                      reads=[pb[pz], GB], writes=[mTB[fc][t5]])
                            else:
                                tm, tmB = t_r.next()
                                tr.op(dve, TT(tm[:, 0:n], psum[pz][:, 0:n], G[:, 0:n], ALU.mult), reads=[pb[pz], GB], writes=[tmB])
                                tr.op(pool, TT(mT[:, fc, t0:t0 + n], mT[:, fc, t0:t0 + n], tm[:, 0:n], ALU.add),
                                      reads=[tmB], writes=[mTB[fc][t5]])

            def dump(i):
                if debug:
                    dd = tr.dsem(f"dbg{l}_{i}")
                    tr.dma(pool, dd, [DMA(ddbg[l * 4 + i].rearrange("c p t -> p c t"), yT[:, :, :])],
                           reads=[yTB[c][t] for c in range(4) for t in range(5)], writes=[outB])

            ar.reset()
            HG = 2364
            hglu_r = Ring([ar.bf(HG) for _ in range(2)], "hglu")
            diag_ap = ar.bf(31 * 128).rearrange("p (k n) -> p k n", k=31)
            diagB = Buf("diag")
            sg_r = Ring([ar.f32(512) for _ in range(2)], "sg")
            cbuf = mT[:, :, :].rearrange("p c t -> p (c t)").bitcast(F32).rearrange("p (c t) -> p c t", c=4)

            def cB(j):
                return [mTB[2 * j][t] for t in range(5)] + [mTB[2 * j + 1][t] for t in range(5)]
            segs = [(0, 0, 512), (512, 512, 512), (1024, 1024, 512), (1536, 1536, 512), (2078, 2048, 256)]
            segs = segs if need_ctx else segs[:4]
            for j in range(4):
                w_ap, wb = wget(f"A{l}_{j}")
                w = w3(w_ap, 8, 512)
                for k in range(31):
                    tr.op(dve, TS(diag_ap[:, k, :], ident, vcol(l, V_CW + j * 31 + k)), reads=[constB, vecB], writes=[diagB])
                hg, hgB = hglu_r.next()
                for (a, b) in ((0, 15), (2063, 2093), (2349, 2364)):
                    tr.op(pool, MSET(hg[:, a:b], 0.0), writes=[hgB])
                for t5, (bb, t0, n) in enumerate(segs):
                    proj_fm(0 + (t5 % 2) * 2, w, 0, t0, n, wb, t5)
                    proj_fm(1 + (t5 % 2) * 2, w, 128, t0, n, wb, t5)
                    pa, pg = (t5 % 2) * 2, 1 + (t5 % 2) * 2
                    sg, sgB = sg_r.next()
                    tr.op(act, ACTF(sg[:, 0:n], psum[pg][:, 0:n], AF.Sigmoid), reads=[pb[pg]], writes=[sgB])
                    tr.op(dve, TT(hg[:, bb + 15:bb + 15 + n], psum[pa][:, 0:n], sg[:, 0:n], ALU.mult),
                          reads=[pb[pa], sgB], writes=[hgB])
                for t5, (bb, t0, n) in enumerate(segs):
                    pc = 4 + (t5 % 2)
                    tr.group(pe, [MM(psum[pc][:, 0:n], diag_ap[:, k, :], hg[:, bb + k:bb + k + n], start=(k == 0), stop=(k == 30))
                                  for k in range(31)], reads=[diagB, hgB], writes=[pb[pc]])
                    tr.op(act, ACTF(cbuf[:, j, t0:t0 + n], psum[pc][:, 0:n], AF.Identity, bias=vcol(l, V_CB + j)),
                          reads=[pb[pc], vecB], writes=cB(j))
                    pgt = 6 + (t5 % 2)
                    proj_fm(pgt, w, 256, t0, n, wb, t5)
                    tr.op(act, ACTF(yT[:, j, t0:t0 + n], psum[pgt][:, 0:n], AF.Silu), reads=[pb[pgt]], writes=[yTB[j][t5]])
            ar.reset()
            onesf = ar.f32(128)
            onesfB = Buf("onesf")
            tr.op(dve, MSET(onesf, 1.0), writes=[onesfB])
            sq_r = Ring([ar.f32(512) for _ in range(2)], "csq")
            st_ring = Ring([(ar.f32(512), ar.f32(512), ar.f32(512)) for _ in range(2)], "lnstat")
            d_r = Ring([ar.f32(512) for _ in range(4)], "lnd")
            allc = [b for j in range(4) for b in cB(j)]

            def ln_a(t5, t0, n):
                (mean, rstd, msq), stB2 = st_ring.next()
                tr.group(pe, [MM(psum[0][:, 0:n], onesf, cbuf[:, j, t0:t0 + n], start=(j == 0), stop=(j == 3)) for j in range(4)],
                         reads=allc + [onesfB], writes=[pb[0]])
                for j in range(4):
                    sq, sqB = sq_r.next()
                    tr.op(act, ACTF(sq[:, 0:n], cbuf[:, j, t0:t0 + n], AF.Square), reads=allc, writes=[sqB])
                    tr.group(pe, [MM(psum[1][:, 0:n], onesf, sq[:, 0:n], start=(j == 0), stop=(j == 3))],
                             reads=[sqB, onesfB], writes=[pb[1]])
                tr.op(dve, TS(mean[:, 0:n], psum[0][:, 0:n], 1.0 / 512), reads=[pb[0]], writes=[stB2])
                tr.op(dve, TT(msq[:, 0:n], mean[:, 0:n], mean[:, 0:n], ALU.mult), reads=[stB2], writes=[stB2])
                tr.op(dve, STT(msq[:, 0:n], psum[1][:, 0:n], 1.0 / 512, msq[:, 0:n], ALU.mult, ALU.subtract),
                      reads=[pb[1], stB2], writes=[stB2])
                tr.op(act, ACTF(msq[:, 0:n], msq[:, 0:n], AF.Sqrt, bias=EPS, scale=1.0), reads=[stB2], writes=[stB2])
                tr.op(dve, RCP(rstd[:, 0:n], msq[:, 0:n]), reads=[stB2], writes=[stB2])
                return mean, rstd, stB2

            def ln_b(t5, t0, n, mean, rstd, stB2):
                ds = []
                for j in range(4):
                    d, dB = d_r.next()
                    ds.append((d, dB))
                    tr.op(dve, TT(d[:, 0:n], cbuf[:, j, t0:t0 + n], mean[:, 0:n], ALU.subtract), reads=allc + [stB2], writes=[dB])
                    tr.op(dve, TT(d[:, 0:n], d[:, 0:n], rstd[:, 0:n], ALU.mult), reads=[stB2], writes=[dB])
                for j, (d, dB) in enumerate(ds):
                    tr.op(act, ACTF(d[:, 0:n], d[:, 0:n], AF.Silu, scale=vcol(l, V_LG + j), bias=vcol(l, V_LB + j)),
                          reads=[vecB], writes=[dB])
                for j, (d, dB) in enumerate(ds):
                    tr.op(dve, TT(yT[:, j, t0:t0 + n], yT[:, j, t0:t0 + n], d[:, 0:n], ALU.mult), reads=[dB], writes=[yTB[j][t5]])

            prev_ln = None
            for t5, (t0, n) in enumerate(qtiles):
                cur_ln = (t5, t0, n) + ln_a(t5, t0, n)
                if prev_ln is not None:
                    ln_b(*prev_ln)
                prev_ln = cur_ln
            ln_b(*prev_ln)
            dump(0)
            merge_branch(0)

            def attn_branch(kind):
                ar.reset()
                QT = ar.bf(2 * T).rearrange("p (h t) -> p h t", h=2)
                KT = ar.bf(T)
                GT = ar.bf(T)
                Vp = ar.bf(18 * 256).rearrange("p (k n) -> p k n", k=18)
                QTB = [Buf(f"QT{t}") for t in range(5)]
                KTB = [Buf(f"KT{t}") for t in range(5)]
                GTB = [Buf(f"GT{t}") for t in range(5)]
                VB = [Buf(f"V{g}") for g in range(5)]
                sq_r = Ring([ar.bf(512) for _ in range(2)], "sq")
                rs_r = Ring([ar.f32(512) for _ in range(2)], "rs")
                xn_r2 = Ring([ar.bf(512) for _ in range(2)], "xnb")
                t1_r = Ring([ar.f32(512) for _ in range(1)], "t1")
                t2_r = Ring([ar.f32(512) for _ in range(1)], "t2")
                P_r = Ring([ar.bf(512) for _ in range(4 if kind == "B" else 6)], "P")
                ya_r = Ring([ar.f32(512) for _ in range(1)], "ya")
                yb_r = Ring([ar.f32(512) for _ in range(1)], "yb")
                if kind == "B":
                    tab = ar.bf(2 * NJ * 64).rearrange("p (h n) -> p h n", h=2)
                    tab64 = ar.bf(2 * NJ * 64).rearrange("p (t j q) -> p t j q", t=2, j=NJ)
                    rt2 = ar.bf(64)
                    rp = ar.f32(32)
                    tabB = [Buf("tab0"), Buf("tab1")]
                    tab64B, rt2B, rpB = Buf("tab64"), Buf("rt2"), Buf("rp")
                    rpd = tr.dsem(f"rp{l}")
                tr.op(pool, MSET(QT[:, :, :], 0.0), writes=QTB)
                if kind != "D":
                    tr.op(pool, MSET(Vp[:, :, 64:128], 1.0), writes=VB)
                    tr.op(pool, MSET(Vp[:, :, 192:256], 1.0), writes=VB)
                rope = kind != "B"
                qg = {"B": V_NAQ, "C": V_GQ, "D": V_DQ}[kind]
                kg = {"B": V_NAK, "C": V_GK, "D": V_DK}[kind]

                def normed(psi, dst, dstB, gcol, t0, n, do_rope):
                    sq, sqB = sq_r.next()
                    tr.op(act, ACTF(sq[:, 0:n], psum[psi][:, 0:n], AF.Square), reads=[pb[psi]], writes=[sqB])
                    tr.group(pe, [MM(psum[5][:, 0:n], blk, sq[:, 0:n])], reads=[sqB, constB], writes=[pb[5]])
                    rs, rsB = rs_r.next()
                    tr.op(act, ACTF(rs[:, 0:n], psum[5][:, 0:n], AF.Sqrt, scale=1.0 / 64, bias=EPS), reads=[pb[5]], writes=[rsB])
                    tr.op(dve, RCP(rs[:, 0:n], rs[:, 0:n]), reads=[rsB], writes=[rsB])
                    if not do_rope:
                        for (r0, r1, d_ap) in dst:
                            tr.op(dve, STT(d_ap, psum[psi][r0:r1, 0:n], vecs[r0:r1, l * NVL + gcol:l * NVL + gcol + 1], rs[r0:r1, 0:n],
                                           ALU.mult, ALU.mult), reads=[pb[psi], rsB, vecB], writes=[dstB])
                        return
                    xb, xbB = xn_r2.next()
                    tr.op(dve, STT(xb[:, 0:n], psum[psi][:, 0:n], vcol(l, gcol), rs[:, 0:n], ALU.mult, ALU.mult),
                          reads=[pb[psi], rsB, vecB], writes=[xbB])
                    tr.group(pe, [MM(psum[6][:, 0:n], perm, xb[:, 0:n])], reads=[xbB, constB], writes=[pb[6]])
                    t1, t1B = t1_r.next()
                    t2, t2B = t2_r.next()
                    tr.op(dve, TT(t1[:, 0:n], xb[:, 0:n], COS[:, t0:t0 + n], ALU.mult), reads=[xbB, constB], writes=[t1B])
                    tr.op(dve, TT(t2[:, 0:n], psum[6][:, 0:n], SIN[:, t0:t0 + n], ALU.mult), reads=[pb[6], constB], writes=[t2B])
                    for (r0, r1, d_ap) in dst:
                        tr.op(dve, TT(d_ap, t1[r0:r1, 0:n], t2[r0:r1, 0:n], ALU.add), reads=[t1B, t2B], writes=[dstB])

                for pr in range(4):
                    w_ap, wb = wget(f"{kind}{l}_{pr}")
                    w = w3(w_ap, 8, 512)
                    if kind == "B":
                        for h2 in range(2):
                            h = 2 * pr + h2
                            tr.dma(sp, rpd, [DMA(rp[0:15, 0:31], drpb[l, h])], writes=[rpB])
                            tr.group(pe, [MM(psum[7][0:31, 0:2 * NJ], rp[0:15, 0:31], selR[0:15, 0:2 * NJ])],
                                     reads=[rpB, constB], writes=[pb[7]])
                            tr.op(dve, CP(rt2[0:31, 0:2 * NJ], psum[7][0:31, 0:2 * NJ]), reads=[pb[7]], writes=[rt2B])
                            for g in range(8):
                                bk = g % 4
                                tr.group(pe, [MM(psum[bk][0:64, q8 * 2 * NJ:(q8 + 1) * 2 * NJ],
                                                 band[0:31, 63 - (8 * g + q8):127 - (8 * g + q8)], rt2[0:31, 0:2 * NJ])
                                              for q8 in range(8)], reads=[rt2B, constB], writes=[pb[bk]])
                                pv = psum[bk][0:64, 0:8 * 2 * NJ].rearrange("p (q t j) -> p t j q", q=8, t=2)
                                for hf in range(2):
                                    tr.op(dve, TT(tab64[0:64, hf, :, 8 * g:8 * g + 8], pv[:, hf], maskT[0:64, hf, :, 8 * g:8 * g + 8], ALU.add),
                                          reads=[pb[bk], constB], writes=[tab64B])
                            tr.op(dve, CP(tab[0:64, h2, :], tab64[0:64, 0].rearrange("p j q -> p (j q)")), reads=[tab64B], writes=[tabB[h2]])
                            tr.op(dve, CP(tab[64:128, h2, :], tab64[0:64, 1].rearrange("p j q -> p (j q)")), reads=[tab64B], writes=[tabB[h2]])
                    items = []
                    cnt = [0]

                    def qk_item(col0, dst, dstB, gcol, t0, n, t5, do_rope):
                        psi = cnt[0] % 4
                        cnt[0] += 1
                        st = {}

                        def s0():
                            proj_fm(psi, w, col0, t0, n, wb, t5)
                            sq, sqB = sq_r.next()
                            st["sq"] = (sq, sqB)
                            tr.op(act, ACTF(sq[:, 0:n], psum[psi][:, 0:n], AF.Square), reads=[pb[psi]], writes=[sqB])

                        def s1():
                            sq, sqB = st["sq"]
                            tr.group(pe, [MM(psum[5][:, 0:n], blk, sq[:, 0:n])], reads=[sqB, constB], writes=[pb[5]])
                            rs, rsB = rs_r.next()
                            tr.op(act, ACTF(rs[:, 0:n], psum[5][:, 0:n], AF.Sqrt, scale=1.0 / 64, bias=EPS), reads=[pb[5]], writes=[rsB])
                            tr.op(dve, RCP(rs[:, 0:n], rs[:, 0:n]), reads=[rsB], writes=[rsB])
                            if not do_rope:
                                for (r0, r1, d_ap) in dst:
                                    tr.op(dve, STT(d_ap, psum[psi][r0:r1, 0:n], vecs[r0:r1, l * NVL + gcol:l * NVL + gcol + 1],
                                                   rs[r0:r1, 0:n], ALU.mult, ALU.mult), reads=[pb[psi], rsB, vecB], writes=[dstB])
                                return
                            xb, xbB = xn_r2.next()
                            st["xb"] = (xb, xbB)
                            tr.op(dve, STT(xb[:, 0:n], psum[psi][:, 0:n], vcol(l, gcol), rs[:, 0:n], ALU.mult, ALU.mult),
                                  reads=[pb[psi], rsB, vecB], writes=[xbB])

                        def s2():
                            if not do_rope:
                                return
                            xb, xbB = st["xb"]
                            tr.group(pe, [MM(psum[6][:, 0:n], perm, xb[:, 0:n])], reads=[xbB, constB], writes=[pb[6]])
                            t1, t1B = t1_r.next()
                            t2, t2B = t2_r.next()
                            tr.op(dve, TT(t1[:, 0:n], xb[:, 0:n], COS[:, t0:t0 + n], ALU.mult), reads=[xbB, constB], writes=[t1B])
                            tr.op(dve, TT(t2[:, 0:n], psum[6][:, 0:n], SIN[:, t0:t0 + n], ALU.mult), reads=[pb[6], constB], writes=[t2B])
                            for (r0, r1, d_ap) in dst:
                                tr.op(dve, TT(d_ap, t1[r0:r1, 0:n], t2[r0:r1, 0:n], ALU.add), reads=[t1B, t2B], writes=[dstB])
                        return [s0, s1, s2]

                    def gate_item(t0, n, t5):
                        psi = cnt[0] % 4
                        cnt[0] += 1

                        def s0():
                            proj_fm(psi, w, 384, t0, n, wb, t5)
                            tr.op(act, ACTF(GT[:, t0:t0 + n], psum[psi][:, 0:n], AF.Silu), reads=[pb[psi]], writes=[GTB[t5]])
                        return [s0]

                    nv = 64 if kind == "C" else 128

                    def v_item(g5):
                        psi = cnt[0] % 4
                        cnt[0] += 1

                        def s0():
                            kts = list(range(4 * g5, min(4 * g5 + 4, 18)))
                            fns = []
                            for ii, kt in enumerate(kts):
                                for kc in range(8):
                                    fns.append(MM(psum[psi][:, ii * 128:ii * 128 + nv], hT[:, kc, kt * 128:(kt + 1) * 128],
                                                  w[:, kc, 256:256 + nv], start=(kc == 0), stop=(kc == 7)))
                            tr.group(pe, fns, reads=[wb, hTB[g5], hTB2[g5]], writes=[pb[psi]])
                            nk = len(kts)
                            pvv = psum[psi][:, 0:nk * 128].rearrange("p (k n) -> p k n", k=nk)
                            k0 = kts[0]
                            if kind == "B":
                                tr.op(act, ACTF(Vp[:, k0:k0 + nk, 0:64], pvv[:, :, 0:64], AF.Copy), reads=[pb[psi]], writes=[VB[g5]])
                                tr.op(dve, CP(Vp[:, k0:k0 + nk, 128:192], pvv[:, :, 64:128]), reads=[pb[psi]], writes=[VB[g5]])
                            elif kind == "C":
                                tr.op(act, ACTF(Vp[:, k0:k0 + nk, 0:64], pvv[:, :, 0:64], AF.Copy), reads=[pb[psi]], writes=[VB[g5]])
                            else:
                                tr.op(act, ACTF(Vp[:, k0:k0 + nk, 0:128], pvv[:, :, 0:128], AF.Copy), reads=[pb[psi]], writes=[VB[g5]])
                        return [s0]

                    qk_items = []
                    for t5, (t0, n) in enumerate(qtiles):
                        qk_items.append(qk_item(0, [(0, 64, QT[0:64, 0, t0:t0 + n]), (64, 128, QT[64:128, 1, t0:t0 + n])], QTB[t5], qg,
                                                t0, n, t5, rope and t5 < 4))
                    for t5, (t0, n) in enumerate(TILES):
                        qk_items.append(qk_item(128, [(0, 128, KT[:, t0:t0 + n])], KTB[t5], kg, t0, n, t5, rope and t5 < 4))
                    items = [gate_item(t0, n, t5) for t5, (t0, n) in enumerate(qtiles)] + qk_items + [v_item(g5) for g5 in range(5)]
                    KST = 3
                    for step in range(len(items) + KST - 1):
                        for k in range(KST):
                            i = step - k
                            if 0 <= i < len(items) and k < len(items[i]):
                                items[i][k]()

                    QW = 256

                    def both(ap512, qa, qb):
                        if qa == 0 and qb == QW:
                            return ap512
                        return ap512.rearrange("p (h q) -> p h q", h=2)[:, :, qa:qb]

                    qts = []
                    if kind == "B":
                        for r_lo in range(0, 32, 4):
                            ch = [(16, 0, QW, None), (17, 0, QW, None)]
                            if r_lo in (0, 28):
                                a0 = 0 if r_lo == 0 else 12
                                for a in range(a0, a0 + 4):
                                    ch.append((a, 0, QW, (10 + 7 - 2 * a + r_lo) * 64))
                            else:
                                for a in range(16):
                                    rs_ = max(r_lo, 2 * a - 3)
                                    re_ = min(r_lo + 3, 2 * a + 5)
                                    if rs_ <= re_:
                                        ch.append((a, (rs_ - r_lo) * 64, (re_ - r_lo + 1) * 64, (4 - 2 * a + rs_) * 64))
                            qts.append((r_lo * 64, ch))
                    else:
                        for q0 in range(0, S, QW):
                            qts.append((q0, [(kt, 0, QW, None) for kt in range(18)]))
                    if need_ctx:
                        qts.append((2048, [(16, 0, QW, None), (17, 0, QW, None)]))

                    NCH = 2
                    nacc = {"B": 2, "C": 1, "D": 2}[kind]
                    steps = []
                    pending_b = [None]
                    for ti, (q0, ch) in enumerate(qts):
                        t5 = min(q0 // 512, 4)
                        if nacc == 2:
                            accs = [(ti % 2) * 2, (ti % 2) * 2 + 1]
                        else:
                            accs = [ti % 2]
                        nch = len(ch)
                        for si_, c0 in enumerate(range(0, nch, NCH)):
                            subs = []
                            for ci in range(c0, min(c0 + NCH, nch)):
                                kt, qa, qb, bcol = ch[ci]
                                subs.append(dict(kt=kt, qa=qa, qb=qb, bcol=bcol, first=(ci == 0), last=(ci == nch - 1)))
                            steps.append(dict(q0=q0, t5=t5, accs=accs, subs=subs))
                            if si_ == 2 and pending_b[0] is not None:
                                steps.append(pending_b[0])
                                pending_b[0] = None
                        if pending_b[0] is not None:
                            steps.append(pending_b[0])
                            pending_b[0] = None
                        ea = dict(epi=True, part="a", q0=q0, t5=t5, accs=accs)
                        steps.append(ea)
                        if kind == "D":
                            pending_b[0] = dict(epi=True, part="b", ref=ea, q0=q0, t5=t5, accs=accs)
                    if pending_b[0] is not None:
                        steps.append(pending_b[0])
                        pending_b[0] = None

                    sring = [2, 3, 4, 5, 6, 7] if kind == "C" else [4, 5, 6, 7]
                    scount = [0]

                    def do_qk(st):
                        q0, t5 = st["q0"], st["t5"]
                        fns = []
                        rd = [QTB[t5], constB]
                        wr = []
                        for sub in st["subs"]:
                            si = sring[scount[0] % len(sring)]
                            scount[0] += 1
                            sub["si"] = si
                            rd.append(KTB[sub["kt"] // 4])
                            wr.append(pb[si])

                        for sub in st["subs"]:
                            si, kt, qa, qb, bcol = sub["si"], sub["kt"], sub["qa"], sub["qb"], sub["bcol"]
                            o3 = psum[si][:, 0:512].rearrange("p (h q) -> p h q", h=2)[:, :, qa:qb]
                            fns.append(MM(o3, KT[:, kt * 128:(kt + 1) * 128], QT[:, :, q0 + qa:q0 + qb], start=True, stop=(bcol is None)))
                            if bcol is not None:
                                fns.append(MM(o3, ident, tab[:, :, bcol:bcol + (qb - qa)], start=False, stop=True))
                                rd += tabB
                        tr.group(pe, fns, reads=rd, writes=wr)
                        for sub in st["subs"]:
                            P, PB = P_r.next()
                            sub["P"], sub["PB"] = P, PB
                            si, qa, qb = sub["si"], sub["qa"], sub["qb"]
                            tr.op(act, ACTF(both(P[:, 0:512], qa, qb), both(psum[si][:, 0:512], qa, qb), AF.Exp, scale=0.125),
                                  reads=[pb[si]], writes=[PB])

                    def do_pv(st):
                        accs = st["accs"]
                        fns = []
                        rd = [constB]
                        for sub in st["subs"]:
                            kt, qa, qb, P = sub["kt"], sub["qa"], sub["qb"], sub["P"]
                            rd += [sub["PB"], VB[kt // 4]]
                            f, la = sub["first"], sub["last"]
                            if kind == "B":
                                for hf in range(2):
                                    fns.append(MM(psum[accs[hf]][:, qa:qb], Vp[:, kt, hf * 128:(hf + 1) * 128],
                                                  P[:, hf * QW + qa:hf * QW + qb], start=f, stop=la))
                                tr.group(pe, fns, reads=[constB, sub["PB"], VB[kt // 4]], writes=[pb[a] for a in accs])
                                fns = []
                            elif kind == "C":
                                fns.append(MM(psum[accs[0]][:, 0:512], Vp[:, kt, 0:128], P[:, 0:512], start=f, stop=la))
                            else:
                                fns.append(MM(psum[accs[0]][:, 0:512], Vp[:, kt, 0:128], P[:, 0:512], start=f, stop=la))
                                fns.append(MM(psum[accs[1]][:, 0:512], ones, P[:, 0:512], start=f, stop=la))
                                tr.group(pe, fns, reads=[constB, sub["PB"], VB[kt // 4]], writes=[pb[a] for a in accs])
                                fns = []
                        if fns:
                            tr.group(pe, fns, reads=rd, writes=[pb[a] for a in accs])

                    def do_epi(st):
                        q0, accs, t5 = st["q0"], st["accs"], st["t5"]
                        qn = QW
                        ywr = [yTB[pr][t5]]
                        grd = [GTB[t5]]
                        ya, yaB = ya_r.next()
                        yb, ybB = yb_r.next()
                        if kind == "B":
                            for hf in range(2):
                                o = accs[hf]
                                tr.op(dve, RCP(ya[0:64, hf * QW:hf * QW + qn], psum[o][64:128, 0:qn]), reads=[pb[o]], writes=[yaB])
                                tr.op(dve, TT(yb[hf * 64:(hf + 1) * 64, 0:qn], psum[o][0:64, 0:qn], ya[0:64, hf * QW:hf * QW + qn], ALU.mult),
                                      reads=[pb[o], yaB], writes=[ybB])
                            tr.op(dve, TT(yT[:, pr, q0:q0 + qn], yb[:, 0:qn], GT[:, q0:q0 + qn], ALU.mult),
                                  reads=[ybB] + grd, writes=ywr)
                        elif kind == "C":
                            o = accs[0]
                            tr.op(dve, RCP(ya[0:64, 0:512], psum[o][64:128, 0:512]), reads=[pb[o]], writes=[yaB])
                            for hf in range(2):
                                tr.op(dve, TT(yb[hf * 64:(hf + 1) * 64, 0:qn], psum[o][0:64, hf * QW:hf * QW + qn],
                                              ya[0:64, hf * QW:hf * QW + qn], ALU.mult), reads=[pb[o], yaB], writes=[ybB])
                            tr.op(dve, TT(yT[:, pr, q0:q0 + qn], yb[:, 0:qn], GT[:, q0:q0 + qn], ALU.mult),
                                  reads=[ybB] + grd, writes=ywr)
                        else:
                            o, dn = accs
                            tr.op(dve, RCP(ya[:, 0:512], psum[dn][:, 0:512]), reads=[pb[dn]], writes=[yaB])
                            tr.op(dve, TT(ya[:, 0:512], psum[o][:, 0:512], ya[:, 0:512], ALU.mult), reads=[pb[o]], writes=[yaB])
                            tr.op(dve, STT(yb[:, 0:qn], ya[:, QW:QW + qn], neglam, ya[:, 0:qn], ALU.mult, ALU.add),
                                  reads=[yaB, smallB], writes=[ybB])
                            sq, sqB = sq_r.next()
                            tr.op(act, ACTF(sq[:, 0:qn], yb[:, 0:qn], AF.Square), reads=[ybB], writes=[sqB])
                            st["bufs"] = (ya, yaB, yb, ybB, sq, sqB)

                    def do_epi_b(st):
                        q0, accs, t5 = st["q0"], st["accs"], st["t5"]
                        qn = QW
                        o, dn = accs
                        ya, yaB, yb, ybB, sq, sqB = st["ref"]["bufs"]
                        tr.group(pe, [MM(psum[dn][:, 0:qn], ones, sq[:, 0:qn])], reads=[sqB, constB], writes=[pb[dn]])
                        tr.op(act, ACTF(ya[:, 0:qn], psum[dn][:, 0:qn], AF.Ln, scale=1.0 / 128, bias=epsc), reads=[pb[dn], smallB], writes=[yaB])
                        tr.op(act, ACTF(ya[:, 0:qn], ya[:, 0:qn], AF.Exp, scale=-0.5), reads=[], writes=[yaB])
                        tr.op(dve, TT(yb[:, 0:qn], yb[:, 0:qn], ya[:, 0:qn], ALU.mult), reads=[yaB], writes=[ybB])
                        tr.op(dve, STT(yT[:, pr, q0:q0 + qn], yb[:, 0:qn], gsub, GT[:, q0:q0 + qn], ALU.mult, ALU.mult),
                              reads=[ybB, smallB, GTB[t5]], writes=[yTB[pr][t5]])

                    LOOK = 2 if kind == "C" else 1
                    qk_list = [s_ for s_ in steps if "epi" not in s_]
                    qi = 0
                    done = 0
                    for s_ in steps:
                        if "epi" in s_:
                            if s_["part"] == "a":
                                do_epi(s_)
                            else:
                                do_epi_b(s_)
                            continue
                        while qi < len(qk_list) and qi <= done + LOOK:
                            do_qk(qk_list[qi])
                            qi += 1
                        do_pv(s_)
                        done += 1

            attn_branch("B")
            dump(1)
            merge_branch(1)
            attn_branch("C")
            dump(2)
            merge_branch(2)
            attn_branch("D")
            dump(3)
            merge_branch(3)

            ar.reset()
            if debug and l == 0:
                ddm = tr.dsem("dbgm")
                tr.dma(pool, ddm, [DMA(ddbgm.rearrange("c p t -> p c t"), mT[:, :, :])],
                       reads=[mTB[c][t] for c in range(8) for t in range(5)], writes=[outB])
            screp = ar.bf(8 * 128).rearrange("p (k n) -> p k n", k=8)
            sccrep = ar.bf(8 * 128).rearrange("p (k n) -> p k n", k=8)
            repB = Buf("rep")
            for kc in range(8):
                tr.op(dve, CP(screp[:, kc, :], s2[:, kc, 0:1].to_broadcast([128, 128])), reads=[s2B], writes=[repB])
                tr.op(dve, CP(sccrep[:, kc, :], s2[:, kc, 1:2].to_broadcast([128, 128])), reads=[s2B], writes=[repB])
            tr.dma(pool, bgd, [DMA(bgrow[:, :], dbg_rows[:, l * DM:(l + 1) * DM])], writes=[bgB])
            gx = ar.f32(1024)
            gc = ar.f32(1024)
            gB = Buf("gates")
            for pi in range(2):
                w_ap, wb = wget(f"adag{l}_{pi}")
                w = w3(w_ap, 8, 512)
                for which, (rep, dst) in enumerate(((screp, gx), (sccrep, gc))):
                    if which == 1 and not need_ctx:
                        continue
                    psi = pi * 2 + which
                    fns = [MM(psum[psi][:, :], rep[:, kc, :], w[:, kc, :], start=(kc == 0), stop=False) for kc in range(8)]
                    fns.append(MM(psum[psi][:, :], ones[0:1, :], bgrow[0:1, pi * 512:(pi + 1) * 512], start=False, stop=True))
                    tr.group(pe, fns, reads=[wb, repB, constB, bgB], writes=[pb[psi]])
                    tr.op(act, ACTF(dst[:, pi * 512:(pi + 1) * 512], psum[psi][:, :], AF.Copy), reads=[pb[psi]], writes=[gB])
            if debug and l == 0:
                ddg = tr.dsem("dbgg")
                tr.dma(sp, ddg, [DMA(ddbgg[:, 0:1024], gx), DMA(ddbgg[:, 1024:2048], gc)], reads=[gB], writes=[outB])
            xo_r = Ring([ar.f32(512) for _ in range(3)], "xo")
            res_r = Ring([ar.f32(512) for _ in range(2)], "res")
            tm_r = Ring([ar.f32(512) for _ in range(2)], "otm")
            ntile = 18 if need_ctx else 16
            its = [(ph, i) for ph in range(2) for i in range(ntile)]
            loaded = {}

            def issue_load(k):
                if k >= len(its):
                    return
                ph, i = its[k]
                xo, xoB = xo_r.next()
                if l == 0:
                    src = dx[i * 128:(i + 1) * 128, ph * 512:(ph + 1) * 512] if i < 16 else \
                        dctx[(i - 16) * 128:(i - 15) * 128, ph * 512:(ph + 1) * 512]
                    rd = []
                else:
                    src = dx1[i * 128:(i + 1) * 128, ph * 512:(ph + 1) * 512]
                    rd = [x1B[i]]
                tr.dma(sp, xld[k % 3], [DMA(xo, src)], reads=rd, writes=[xoB])
                loaded[k] = (xo, xoB)

            issue_load(0)
            issue_load(1)
            w = wb = None
            for cnt, (ph, i) in enumerate(its):
                if i == 0:
                    w_ap, wb = wget(f"wo{l}_{ph}")
                    w = w3(w_ap, 8, 512)
                psi = 4 + cnt % 4
                t5 = min(i // 4, 4)
                tr.group(pe, [MM(psum[psi][:, :], mT[:, kc, i * 128:(i + 1) * 128], w[:, kc, :], start=(kc == 0), stop=(kc == 7))
                              for kc in range(8)], reads=[wb] + [mTB[kc][t5] for kc in range(8)], writes=[pb[psi]])
                xo, xoB = loaded.pop(cnt)
                gate = gx if i < 16 else gc
                tm, tmB = tm_r.next()
                rs_, rsB_ = res_r.next()
                tr.op(dve, TT(tm, psum[psi][:, :], gate[:, ph * 512:(ph + 1) * 512], ALU.mult), reads=[pb[psi], gB], writes=[tmB])
                tr.op(dve, TT(rs_, tm, xo, ALU.add), reads=[tmB, xoB], writes=[rsB_])
                if l == NL - 1:
                    tr.dma(sp, std[cnt % 2], [DMA(dout[i * 128:(i + 1) * 128, ph * 512:(ph + 1) * 512], rs_)],
                           reads=[rsB_], writes=[Buf("o")])
                else:
                    tr.dma(sp, std[cnt % 2], [DMA(dx1[i * 128:(i + 1) * 128, ph * 512:(ph + 1) * 512], rs_)],
                           reads=[rsB_], writes=[x1B[i]])
                issue_load(cnt + 2)

        for l in range(NL):
            run_layer(l, l < NL - 1)
        tr.barrier()

        block = es.enter_context(nc.Block())

        @block.tensor
        def _(h):
            Tracer.replay(pe, h)

        @block.scalar
        def _(h):
            Tracer.replay(act, h)

        @block.vector
        def _(h):
            Tracer.replay(dve, h)

        @block.gpsimd
        def _(h):
            Tracer.replay(pool, h)

        @block.sync
        def _(h):
            Tracer.replay(sp, h)
    return nc


_CACHE = {}


def _prep_inputs(inputs):
    f = lambda a: np.ascontiguousarray(np.asarray(a, dtype=np.float32))
    x, c, ctx, c_ctx = f(inputs["x"]), f(inputs["c"]), f(inputs["ctx"]), f(inputs["c_ctx"])
    cf, cb = _host_consts()
    w_br = np.ascontiguousarray(np.stack([f(inputs["w_br_a"]), f(inputs["w_br_b"]), f(inputs["w_br_c"]), f(inputs["w_br_d"])], axis=1))
    vecs = np.zeros((128, NL * NVL), np.float32)

    def fm(v, n):
        return np.ascontiguousarray(v.reshape(n, 128).T)

    for l in range(NL):
        o = l * NVL
        vecs[:, o + V_G:o + V_G + 8] = fm(f(inputs["norm_g"])[l], 8)
        b_ada = f(inputs["b_ada"])[l]
        vecs[:, o + V_BSH:o + V_BSH + 8] = fm(b_ada[0:1024], 8)
        vecs[:, o + V_BSC:o + V_BSC + 8] = fm(b_ada[1024:2048], 8)
        vecs[:, o + V_BM:o + V_BM + 32] = fm(f(inputs["b_merge"])[l], 32)
        vecs[:, o + V_CB:o + V_CB + 4] = fm(f(inputs["conv_b"])[l], 4)
        vecs[:, o + V_LG:o + V_LG + 4] = fm(f(inputs["conv_ln_g"])[l], 4)
        vecs[:, o + V_LB:o + V_LB + 4] = fm(f(inputs["conv_ln_b"])[l], 4)
        cw = f(inputs["conv_w"])[l]
        vecs[:, o + V_CW:o + V_CW + 124] = cw.T.reshape(4, 128, 31).transpose(1, 0, 2).reshape(128, 124)
        for nm, off in (("na_qn_g", V_NAQ), ("na_kn_g", V_NAK), ("gqa_qn_g", V_GQ), ("gqa_kn_g", V_GK),
                        ("diff_qn_g", V_DQ), ("diff_kn_g", V_DK)):
            vecs[:, o + off] = np.tile(f(inputs[nm])[l], 2)
        vecs[:, o + V_SUB] = f(inputs["diff_subln_g"])[l]
        for k, nm in enumerate(("lam_q1", "lam_k1", "lam_q2", "lam_k2")):
            vecs[:, o + V_L + 64 * k:o + V_L + 64 * (k + 1)] = f(inputs[nm])[l][None, :]
    bgrow = np.ascontiguousarray(f(inputs["b_ada"])[:, 2048:3072].reshape(1, NL * DM))
    shared = {"w_ada": f(inputs["w_ada"]), "w_in": f(inputs["w_in"]), "w_br": w_br, "w_out": f(inputs["w_out"]),
              "vecs": vecs, "bgrow": bgrow, "rpb": f(inputs["na_rpb"]), "cf": cf, "cb": cb}
    maps = []
    for b in range(8):
        cT = np.concatenate([fm(c[b], 8), fm(c_ctx, 8)], axis=1)
        m = dict(shared)
        m.update({"x": x[b], "ctx": ctx[b], "cT": np.ascontiguousarray(cT)})
        maps.append(m)
    return maps


def kernel(**inputs):
    if "nc" not in _CACHE:
        _CACHE["nc"] = build_program(False)
    maps = _prep_inputs(inputs)
    res = run_bass_kernel_spmd(_CACHE["nc"], maps, core_ids=list(range(8)))
    return np.stack([np.asarray(r["out"], dtype=np.float32) for r in res.results], axis=0)
```
